# Optimizing a Trainium2 kernel written in Bass

```python
import numpy as np
import jax
import jax.numpy as jnp
from jax import lax

D_MODEL = 1024
BATCH = 16
SEQ = 2048
DEPTH = 4

N_MIXERS = 3
HEAD_DIM = 128
ROPE_DIM = HEAD_DIM // 4
ROPE_THETA = 500000.0
NORM_EPS = 1e-6
BLK = 128

A_GROUPS = ((128, 1), (512, 4), (2048, 16))
A_HEADS = D_MODEL // HEAD_DIM
A_WIDTH = A_HEADS * HEAD_DIM
A_IN = 3 * len(A_GROUPS) * A_WIDTH + A_WIDTH

B_HEADS = D_MODEL // HEAD_DIM
B_KV = 2
B_REP = B_HEADS // B_KV
B_WIDTH = B_HEADS * HEAD_DIM
B_KVW = B_KV * HEAD_DIM
CMP_LEN = 32
CMP_STRIDE = 16
CMP_HIDDEN = HEAD_DIM
SEL_LEN = 64
N_SELECT = 16
WIN = 512
SEL_CHUNK = 16
FORCE_SCORE = 1000.0
B_IN = B_WIDTH + 6 * B_KVW + B_WIDTH + 3 * B_HEADS

C_WIDTH = D_MODEL * 5 // 4
C_BLOCKS = 10
C_BLOCK_DIM = C_WIDTH // C_BLOCKS
CONV_W = 4
LRU_C = 8.0

N_A = (DEPTH + 2) // 3
N_B = (DEPTH + 1) // 3
N_C = DEPTH // 3

kernel_name = "hybrid_dilated_nsa_rglru"


def rmsnorm(x, g):
    xf = x.astype(jnp.float32)
    y = xf * lax.rsqrt(jnp.mean(xf * xf, axis=-1, keepdims=True) + NORM_EPS)
    return (y * g.astype(jnp.float32)).astype(x.dtype)


def rope(x):
    S = x.shape[1]
    half = ROPE_DIM // 2
    inv_freq = ROPE_THETA ** (-2.0 * jnp.arange(half, dtype=jnp.float32) / ROPE_DIM)
    ang = jnp.arange(S, dtype=jnp.float32)[:, None] * inv_freq[None, :]
    cos = jnp.cos(ang)[None, :, None, :]
    sin = jnp.sin(ang)[None, :, None, :]
    xf = x.astype(jnp.float32)
    x1 = xf[..., :half]
    x2 = xf[..., half:ROPE_DIM]
    out = jnp.concatenate([x1 * cos - x2 * sin, x2 * cos + x1 * sin, xf[..., ROPE_DIM:]], axis=-1)
    return out.astype(x.dtype)


def masked_softmax(s, mask):
    s = jnp.where(mask, s, -jnp.inf)
    m = jnp.max(s, axis=-1, keepdims=True)
    m = jnp.where(jnp.isfinite(m), m, 0.0)
    e = jnp.exp(s - m)
    den = jnp.sum(e, axis=-1, keepdims=True)
    p = e / jnp.maximum(den, 1e-30)
    lse = (m + jnp.log(den))[..., 0]
    return p, lse


def banded_attention(q, k, v, max_dist):
    B_, L, G, R, Dh = q.shape
    n_prev = -(-max_dist // BLK)
    nb = -(-L // BLK)
    pad = nb * BLK - L
    W = (n_prev + 1) * BLK
    qp = jnp.pad(q, ((0, 0), (0, pad), (0, 0), (0, 0), (0, 0)))
    kp = jnp.pad(k, ((0, 0), (n_prev * BLK, pad), (0, 0), (0, 0)))
    vp = jnp.pad(v, ((0, 0), (n_prev * BLK, pad), (0, 0), (0, 0)))
    qb = qp.reshape(B_, nb, BLK, G, R, Dh)
    kb = kp.reshape(B_, nb + n_prev, BLK, G, Dh)
    vb = vp.reshape(B_, nb + n_prev, BLK, G, Dh)
    kw = jnp.concatenate([kb[:, j:j + nb] for j in range(n_prev + 1)], axis=2)
    vw = jnp.concatenate([vb[:, j:j + nb] for j in range(n_prev + 1)], axis=2)
    s = jnp.einsum('bnqgrd,bnkgd->bngrqk', qb, kw).astype(jnp.float32) * (Dh ** -0.5)
    qpos = jnp.arange(nb)[:, None] * BLK + jnp.arange(BLK)[None, :]
    kpos = jnp.arange(nb)[:, None] * BLK + jnp.arange(W)[None, :] - n_prev * BLK
    dist = qpos[:, :, None] - kpos[:, None, :]
    mask = (dist >= 0) & (dist <= max_dist) & (kpos[:, None, :] >= 0)
    p, lse = masked_softmax(s, mask[None, :, None, None, :, :])
    o = jnp.einsum('bngrqk,bnkgd->bnqgrd', p.astype(vw.dtype), vw)
    o = o.reshape(B_, nb * BLK, G, R, Dh)[:, :L]
    lse = jnp.transpose(lse, (0, 1, 4, 2, 3)).reshape(B_, nb * BLK, G, R)[:, :L]
    return o, lse


def dilate(t, d):
    B_, S = t.shape[:2]
    t = t.reshape(B_, S // d, d, *t.shape[2:])
    t = jnp.moveaxis(t, 2, 1)
    return t.reshape(B_ * d, S // d, *t.shape[3:])


def undilate(t, d, B_):
    L = t.shape[1]
    t = t.reshape(B_, d, L, *t.shape[2:])
    t = jnp.moveaxis(t, 1, 2)
    return t.reshape(B_, L * d, *t.shape[3:])


def mixer_a(xn, w_in, w_out):
    B_, S, _ = xn.shape
    n_g = len(A_GROUPS)
    u = xn @ w_in
    qkv = u[..., :3 * n_g * A_WIDTH].reshape(B_, S, n_g, 3, A_HEADS, HEAD_DIM)
    z = u[..., 3 * n_g * A_WIDTH:]
    outs, lses = [], []
    for g, (win, dil) in enumerate(A_GROUPS):
        q = dilate(rope(qkv[:, :, g, 0]), dil)[:, :, :, None, :]
        k = dilate(rope(qkv[:, :, g, 1]), dil)
        v = dilate(qkv[:, :, g, 2], dil)
        o, lse = banded_attention(q, k, v, win // dil)
        outs.append(undilate(o[:, :, :, 0], dil, B_))
        lses.append(undilate(lse[:, :, :, 0], dil, B_))
    alpha = jax.nn.softmax(jnp.stack(lses), axis=0)
    o = jnp.sum(alpha[..., None] * jnp.stack(outs).astype(jnp.float32), axis=0)
    y = o.reshape(B_, S, A_WIDTH).astype(xn.dtype) * jax.nn.silu(z)
    return y @ w_out


def compress(k, pe, w1, w2):
    B_, S, G, Dh = k.shape
    ch = k.reshape(B_, S // CMP_STRIDE, CMP_STRIDE, G, Dh)
    blocks = jnp.concatenate([ch[:, :-1], ch[:, 1:]], axis=2)
    h = jax.nn.silu(jnp.einsum('bjpgd,pde->bjge', blocks + pe[:, None, :], w1))
    return jnp.einsum('bjge,ef->bjgf', h, w2)


def block_cover(S):
    n_cmp = S // CMP_STRIDE - 1
    n_slc = S // SEL_LEN
    j = np.arange(n_cmp)[:, None]
    s = np.arange(n_slc)[None, :]
    cover = (j * CMP_STRIDE < (s + 1) * SEL_LEN) & (j * CMP_STRIDE + CMP_LEN > s * SEL_LEN)
    return jnp.asarray(cover.astype(np.float32))


def selected_attention(q, k, v, idx, valid):
    B_, S, G, R, Dh = q.shape
    n_slc = S // SEL_LEN
    K = idx.shape[-1]
    nc = S // SEL_CHUNK
    kb = jnp.transpose(k.reshape(B_, n_slc, SEL_LEN, G, Dh), (0, 3, 1, 2, 4))
    vb = jnp.transpose(v.reshape(B_, n_slc, SEL_LEN, G, Dh), (0, 3, 1, 2, 4))
    qc = jnp.transpose(q.reshape(B_, nc, SEL_CHUNK, G, R, Dh), (1, 0, 3, 4, 2, 5))
    ic = jnp.transpose(idx.reshape(B_, G, nc, SEL_CHUNK, K), (2, 0, 1, 3, 4))
    vc = jnp.transpose(valid.reshape(B_, G, nc, SEL_CHUNK, K), (2, 0, 1, 3, 4))
    tc = jnp.arange(S).reshape(nc, SEL_CHUNK)
    bi = jnp.arange(B_)[:, None, None, None]
    gi = jnp.arange(G)[None, :, None, None]

    def one_chunk(args):
        qx, ix, vx, tx = args
        gk = kb[bi, gi, ix]
        gv = vb[bi, gi, ix]
        s = jnp.einsum('bgrcd,bgckld->bgrckl', qx, gk).astype(jnp.float32) * (Dh ** -0.5)
        tok = ix[..., None] * SEL_LEN + jnp.arange(SEL_LEN)
        mask = (tok <= tx[None, None, :, None, None]) & vx[..., None]
        s = s.reshape(B_, G, R, SEL_CHUNK, K * SEL_LEN)
        mask = mask[:, :, None].reshape(B_, G, 1, SEL_CHUNK, K * SEL_LEN)
        p, _ = masked_softmax(s, mask)
        return jnp.einsum('bgrcn,bgcnd->bgrcd', p.astype(gv.dtype),
                          gv.reshape(B_, G, SEL_CHUNK, K * SEL_LEN, Dh))

    o = lax.map(one_chunk, (qc, ic, vc, tc))
    return jnp.transpose(o, (1, 0, 4, 2, 3, 5)).reshape(B_, S, G, R, Dh)


def mixer_b(xn, w_in, gate_b, pe_k, w1_k, w2_k, pe_v, w1_v, w2_v, w_out):
    B_, S, _ = xn.shape
    u = xn @ w_in
    o0 = B_WIDTH
    o1 = o0 + 6 * B_KVW
    o2 = o1 + B_WIDTH
    q = rope(u[..., :o0].reshape(B_, S, B_HEADS, HEAD_DIM)).reshape(B_, S, B_KV, B_REP, HEAD_DIM)
    kv = u[..., o0:o1].reshape(B_, S, 6, B_KV, HEAD_DIM)
    k_cmp, v_cmp = rope(kv[:, :, 0]), kv[:, :, 1]
    k_sel, v_sel = rope(kv[:, :, 2]), kv[:, :, 3]
    k_win, v_win = rope(kv[:, :, 4]), kv[:, :, 5]
    z = u[..., o1:o2]
    gates = jax.nn.sigmoid((u[..., o2:] + gate_b).astype(jnp.float32)).reshape(B_, S, B_KV, B_REP, 3)

    kc = compress(k_cmp, pe_k, w1_k, w2_k)
    vc = compress(v_cmp, pe_v, w1_v, w2_v)
    n_cmp = kc.shape[1]
    t = jnp.arange(S)
    blk_end = jnp.arange(n_cmp) * CMP_STRIDE + CMP_LEN - 1
    s = jnp.einsum('bsgrd,bjgd->bgrsj', q, kc).astype(jnp.float32) * (HEAD_DIM ** -0.5)
    p_cmp, _ = masked_softmax(s, blk_end[None, :] <= t[:, None])
    o_cmp = jnp.einsum('bgrsj,bjgd->bsgrd', p_cmp.astype(vc.dtype), vc)

    n_slc = S // SEL_LEN
    p_slc = jnp.einsum('bgrsj,jn->bgsn', p_cmp, block_cover(S))
    blk = jnp.arange(n_slc)[None, :]
    cur = (t // SEL_LEN)[:, None]
    forced = (blk == 0) | (blk == cur) | (blk == cur - 1)
    score = jnp.where(blk <= cur, jnp.where(forced, FORCE_SCORE, p_slc), -jnp.inf)
    vals, idx = lax.top_k(score, min(N_SELECT, n_slc))
    o_sel = selected_attention(q, k_sel, v_sel, idx, jnp.isfinite(vals))

    o_win, _ = banded_attention(q, k_win, v_win, WIN - 1)

    o = gates[..., 0:1] * o_cmp + gates[..., 1:2] * o_sel + gates[..., 2:3] * o_win
    y = o.reshape(B_, S, B_WIDTH).astype(xn.dtype) * jax.nn.silu(z)
    return y @ w_out


def mixer_c(xn, w_in, conv_w, conv_b, wa, ba, wx, bx, lam, w_out):
    B_, S, _ = xn.shape
    u = xn @ w_in
    xb = u[..., :C_WIDTH]
    z = u[..., C_WIDTH:]
    xc = lax.conv_general_dilated(xb, conv_w[:, None, :], window_strides=(1,),
                                  padding=[(CONV_W - 1, 0)],
                                  dimension_numbers=('NWC', 'WIO', 'NWC'),
                                  feature_group_count=C_WIDTH) + conv_b
    xr = xc.reshape(B_, S, C_BLOCKS, C_BLOCK_DIM)
    r = jax.nn.sigmoid((jnp.einsum('bsnc,ncd->bsnd', xr, wa).reshape(B_, S, C_WIDTH) + ba).astype(jnp.float32))
    i = jax.nn.sigmoid((jnp.einsum('bsnc,ncd->bsnd', xr, wx).reshape(B_, S, C_WIDTH) + bx).astype(jnp.float32))
    log_a = -LRU_C * r * jax.nn.softplus(-lam.astype(jnp.float32))
    a = jnp.exp(log_a)
    b = jnp.sqrt(-jnp.expm1(2.0 * log_a)) * i * xc.astype(jnp.float32)

    def combine(left, right):
        a1, b1 = left
        a2, b2 = right
        return a1 * a2, a2 * b1 + b2

    _, h = lax.associative_scan(combine, (a, b), axis=1)
    y = h.astype(xn.dtype) * jax.nn.silu(z)
    return y @ w_out


def _normal(key, shape, scale):
    return scale * jax.random.normal(key, shape, dtype=jnp.float32)


def setup_inputs(seed: int = 0) -> dict:
    key = jax.random.key(seed)
    ks = jax.random.split(key, 24)
    hd = HEAD_DIM
    a0 = jax.random.uniform(ks[22], (N_C, C_WIDTH), dtype=jnp.float32, minval=0.9, maxval=0.999)
    p = a0 ** (1.0 / LRU_C)
    return {
        'x': _normal(ks[0], (BATCH, SEQ, D_MODEL), 1.0),
        'norm_g': 1.0 + _normal(ks[1], (DEPTH, D_MODEL), 0.1),
        'final_g': 1.0 + _normal(ks[2], (D_MODEL,), 0.1),
        'a_w_in': _normal(ks[3], (N_A, D_MODEL, A_IN), D_MODEL ** -0.5),
        'a_w_out': _normal(ks[4], (N_A, A_WIDTH, D_MODEL), A_WIDTH ** -0.5),
        'b_w_in': _normal(ks[5], (N_B, D_MODEL, B_IN), D_MODEL ** -0.5),
        'b_gate_b': _normal(ks[6], (N_B, 3 * B_HEADS), 0.1),
        'b_pe_k': _normal(ks[7], (N_B, CMP_LEN, hd), 0.5),
        'b_w1_k': _normal(ks[8], (N_B, CMP_LEN, hd, CMP_HIDDEN), (CMP_LEN * hd) ** -0.5),
        'b_w2_k': _normal(ks[9], (N_B, CMP_HIDDEN, hd), CMP_HIDDEN ** -0.5),
        'b_pe_v': _normal(ks[10], (N_B, CMP_LEN, hd), 0.5),
        'b_w1_v': _normal(ks[11], (N_B, CMP_LEN, hd, CMP_HIDDEN), (CMP_LEN * hd) ** -0.5),
        'b_w2_v': _normal(ks[12], (N_B, CMP_HIDDEN, hd), CMP_HIDDEN ** -0.5),
        'b_w_out': _normal(ks[13], (N_B, B_WIDTH, D_MODEL), B_WIDTH ** -0.5),
        'c_w_in': _normal(ks[14], (N_C, D_MODEL, 2 * C_WIDTH), D_MODEL ** -0.5),
        'c_conv_w': _normal(ks[15], (N_C, CONV_W, C_WIDTH), CONV_W ** -0.5),
        'c_conv_b': _normal(ks[16], (N_C, C_WIDTH), 0.02),
        'c_wa': _normal(ks[17], (N_C, C_BLOCKS, C_BLOCK_DIM, C_BLOCK_DIM), C_BLOCK_DIM ** -0.5),
        'c_ba': _normal(ks[18], (N_C, C_WIDTH), 0.1),
        'c_wx': _normal(ks[19], (N_C, C_BLOCKS, C_BLOCK_DIM, C_BLOCK_DIM), C_BLOCK_DIM ** -0.5),
        'c_bx': _normal(ks[20], (N_C, C_WIDTH), 0.1),
        'c_lambda': jnp.log(p) - jnp.log1p(-p),
        'c_w_out': _normal(ks[21], (N_C, C_WIDTH, D_MODEL), C_WIDTH ** -0.5),
    }


def reference(x, norm_g, final_g, a_w_in, a_w_out, b_w_in, b_gate_b, b_pe_k, b_w1_k, b_w2_k,
              b_pe_v, b_w1_v, b_w2_v, b_w_out, c_w_in, c_conv_w, c_conv_b, c_wa, c_ba,
              c_wx, c_bx, c_lambda, c_w_out):
    for i in range(DEPTH):
        kind = i % N_MIXERS
        j = i // N_MIXERS
        h = rmsnorm(x, norm_g[i])
        if kind == 0:
            y = mixer_a(h, a_w_in[j], a_w_out[j])
        elif kind == 1:
            y = mixer_b(h, b_w_in[j], b_gate_b[j], b_pe_k[j], b_w1_k[j], b_w2_k[j],
                        b_pe_v[j], b_w1_v[j], b_w2_v[j], b_w_out[j])
        else:
            y = mixer_c(h, c_w_in[j], c_conv_w[j], c_conv_b[j], c_wa[j], c_ba[j],
                        c_wx[j], c_bx[j], c_lambda[j], c_w_out[j])
        x = x + y
    return rmsnorm(x, final_g)
```

```python
import numpy as np
import concourse.bass as bass
import concourse.mybir as mybir
from concourse.bass_utils import run_bass_kernel_spmd

F32 = mybir.dt.float32
BF16 = mybir.dt.bfloat16
U32 = mybir.dt.uint32
AF = mybir.ActivationFunctionType
ALU = mybir.AluOpType
AX = mybir.AxisListType

D = 1024
S = 2048
NCH = 8
NT = 16
NQ = 4
EPS = 1e-6
N_CORES = 8
SEQ_PER_CORE = 2


class Sched:
    NDSEM = 24

    def __init__(self, nc, stack):
        self.nc = nc
        self.engs = {"pe": nc.tensor, "act": nc.scalar, "dve": nc.vector,
                     "pool": nc.gpsimd, "sp": nc.sync}
        self.sem = {}
        for e in self.engs:
            self.sem[e] = stack.enter_context(nc.semaphore("sem_" + e))
        self.dsem = [stack.enter_context(nc.semaphore("dsem%d" % i)) for i in range(self.NDSEM)]
        self.dcount = [0] * self.NDSEM
        self.dnext = 0
        self.count = {e: 0 for e in self.engs}
        self.waited = {}
        self.last_w = {}
        self.readers = {}
        self.ninst = 0

    def _wait(self, eng, tok):
        kind, a, v = tok
        if kind == "e" and a == eng and eng == "pe":
            return
        key = (eng, kind, a)
        if self.waited.get(key, 0) >= v:
            return
        self.waited[key] = v
        sem = self.sem[a] if kind == "e" else self.dsem[a]
        self.engs[eng].wait_ge(sem, v)

    def _deps(self, eng, r, w):
        toks = set()
        for k in r:
            t = self.last_w.get(k)
            if t is not None:
                toks.add(t)
        for k in w:
            t = self.last_w.get(k)
            if t is not None:
                toks.add(t)
            for t2 in self.readers.get(k, {}).values():
                toks.add(t2)
        for t in sorted(toks):
            self._wait(eng, t)

    def _record(self, tok, r, w):
        for k in w:
            self.last_w[k] = tok
            self.readers[k] = {}
        for k in r:
            d = self.readers.setdefault(k, {})
            kk = (tok[0], tok[1])
            if kk not in d or d[kk][2] < tok[2]:
                d[kk] = tok

    def op(self, eng, fn, r=(), w=()):
        pr = [k for k in r if isinstance(k, str) and k.startswith("ps")]
        if pr:
            r = [k for k in r if k not in pr]
            w = list(w) + pr
        self._deps(eng, r, w)
        inst = fn()
        self.count[eng] += 1
        inst.then_inc(self.sem[eng], 1)
        tok = ("e", eng, self.count[eng])
        self._record(tok, r, w)
        self.ninst += 1
        return tok

    def dma(self, q, out, in_, r=(), w=()):
        self._deps(q, r, w)
        k = self.dnext
        self.dnext = (self.dnext + 1) % self.NDSEM
        if self.dcount[k] > 0:
            self._wait(q, ("d", k, 16 * self.dcount[k]))
        self.dcount[k] += 1
        self.engs[q].dma_start(out=out, in_=in_).then_inc(self.dsem[k], 16)
        tok = ("d", k, 16 * self.dcount[k])
        self._record(tok, r, w)
        self.ninst += 1
        return tok

    def barrier(self):
        toks = [("e", e, c) for e, c in self.count.items() if c > 0]
        toks += [("d", k, 16 * c) for k, c in enumerate(self.dcount) if c > 0]
        for e in self.engs:
            for t in toks:
                if t[0] == "e" and t[1] == e:
                    continue
                self._wait(e, t)
        self.last_w = {}
        self.readers = {}

    def wait_all_on(self, eng):
        toks = [("e", e, c) for e, c in self.count.items() if c > 0 and e != eng]
        toks += [("d", k, 16 * c) for k, c in enumerate(self.dcount) if c > 0]
        for t in toks:
            self._wait(eng, t)


class Ctx:
    pass


def build(layers=(0, 1, 2, 3), nseq=SEQ_PER_CORE):
    from contextlib import ExitStack
    nc = bass.Bass("TRN2", target_bir_lowering=False)
    C = Ctx()
    C.nc = nc
    dt = lambda n, s, d=F32, kind="ExternalInput": nc.dram_tensor(n, list(s), d, kind=kind).ap()
    C.x = dt("x", [nseq, S, D])
    C.out = dt("out", [nseq, S, D], kind="ExternalOutput")
    C.norm_g = dt("norm_g", [4, D])
    C.final_g = dt("final_g", [D])
    C.ident_in = dt("ident", [128, 128])
    C.c_rope = dt("c_rope", [2, 32, S])
    C.c_mask = dt("c_mask", [4, 128, 128])
    C.c_prot = dt("c_prot", [32, 32])
    C.c_cmask = dt("c_cmask", [16, 128, 128])
    C.c_eexp = dt("c_eexp", [32, S])
    C.c_cover = dt("c_cover", [128, 32])
    C.c_selk = dt("c_selk", [16, 128, 32])
    C.c_sela = dt("c_sela", [16, 128, 32])
    C.xspill = nc.dram_tensor("xspill", [128, NCH * S], F32, kind="Internal").ap()
    C.W = {}
    for name, shp in WSHAPES.items():
        C.W[name] = dt(name, shp)

    with ExitStack() as st:
        sc = Sched(nc, st)
        C.sc = sc
        sb = lambda n, s, d=F32: st.enter_context(nc.sbuf_tensor(n, list(s), d))
        C.xT = sb("xT", [128, NCH, S])
        C.hT = sb("hT", [128, NCH, S], BF16)
        C.ident = sb("identf", [128, 128])
        C.identb = sb("identb", [128, 128], BF16)
        C.onesb = sb("onesb", [128, 128], BF16)
        C.gall = sb("gall", [128, 5, NCH])
        C.gtmp = sb("gtmp", [40, 128])
        C.wst = [sb("wst%d" % i, [128, WSLOT]) for i in range(3)]
        C.wbf = [sb("wbf%d" % i, [128, WSLOT], BF16) for i in range(6)]
        C.wst_i = 0
        C.wbf_i = 0
        C.ps = [st.enter_context(nc.psum_tensor("ps%d" % i, [128, 512], F32)) for i in range(8)]

        sc.dma("sp", C.ident[:], C.ident_in, w=["ident"])
        sc.op("dve", lambda: nc.vector.tensor_copy(out=C.identb[:], in_=C.ident[:]), r=["ident"], w=["identb"])
        sc.op("dve", lambda: nc.vector.memset(C.onesb[:], 1.0 / D), w=["onesb"])
        sc.dma("sp", C.gtmp[0:32, :], C.norm_g.rearrange("l (c p) -> (l c) p", p=128), w=["gtmp"])
        sc.dma("sp", C.gtmp[32:40, :], C.final_g.rearrange("(c p) -> c p", p=128), w=["gtmp2"])
        sc.op("pe", lambda: nc.tensor.transpose(out=C.ps[0][:, 0:40], in_=C.gtmp[0:40, :], identity=C.ident[0:40, 0:40]),
              r=["gtmp", "gtmp2", "ident"], w=["ps0"])
        sc.op("dve", lambda: nc.vector.tensor_copy(out=C.gall[:].rearrange("p l c -> p (l c)"), in_=C.ps[0][:, 0:40]),
              r=["ps0"], w=["gall"])

        C.ones1b = sb("ones1b", [128, 128], BF16)
        C.maskb = sb("maskb", [128, 4, 128], BF16)
        C.protb = sb("protb", [32, 32], BF16)
        C.ropeb = sb("ropeb", [32, 2, S], BF16)
        sc.op("dve", lambda: nc.vector.memset(C.ones1b[:], 1.0), w=["ones1b"])
        with ExitStack() as st2:
            tmpc = st2.enter_context(nc.sbuf_tensor("tmpc", [128, 2 * S], F32))
            sc.dma("sp", tmpc[:, 0:512].rearrange("p (a b) -> p a b", a=4), C.c_mask.rearrange("a p b -> p a b"), w=["tmpc"])
            sc.op("dve", lambda: nc.vector.tensor_copy(out=C.maskb[:].rearrange("p a b -> p (a b)"), in_=tmpc[:, 0:512]), r=["tmpc"], w=["maskb"])
            sc.dma("sp", tmpc[0:32, 0:32], C.c_prot, r=["maskb"], w=["tmpc"])
            sc.op("dve", lambda: nc.vector.tensor_copy(out=C.protb[:], in_=tmpc[0:32, 0:32]), r=["tmpc"], w=["protb"])
            sc.dma("sp", tmpc[0:32, :].rearrange("p (a b) -> p a b", a=2), C.c_rope.rearrange("a p b -> p a b"), r=["protb"], w=["tmpc"])
            sc.op("dve", lambda: nc.vector.tensor_copy(out=C.ropeb[:].rearrange("p a b -> p (a b)"), in_=tmpc[0:32, :]), r=["tmpc"], w=["ropeb"])
            sc.barrier()
        C.rope_i = 0
        C.sb_i = 0
        C.pt_i = 0

        for s in range(nseq):
            sc.barrier()
            with ExitStack() as st2:
                C.io = st2.enter_context(nc.sbuf_tensor("io_l%d" % s, [128, 2, D], F32))
                load_x(C, s)
                sc.barrier()
            for li in layers:
                with ExitStack() as st2:
                    C.st2 = st2
                    C.uid = "s%dl%d" % (s, li)
                    if li % 3 == 1:
                        for c in range(NCH):
                            sc.dma("sp", C.xspill[:, c * S:(c + 1) * S], C.xT[:, c, :], r=[("xT", c, tt) for tt in range(NT)])
                    make_hT(C, li)
                    if li % 3 == 2:
                        C.cast_eng = "dve"
                        layer_c(C, li // 3)
                        C.cast_eng = "pool"
                    elif li % 3 == 0:
                        layer_a(C, li // 3)
                    else:
                        layer_b(C, li // 3)
                    sc.barrier()
            with ExitStack() as st2:
                C.io = st2.enter_context(nc.sbuf_tensor("io_s%d" % s, [128, 2, D], F32))
                C.rstd = st2.enter_context(nc.sbuf_tensor("rstd_f%d" % s, [128, 512], F32))
                C.sq = st2.enter_context(nc.sbuf_tensor("sq_f%d" % s, [128, 2, 512], BF16))
                final_norm_store(C, s)
                sc.barrier()
        sc.wait_all_on("sp")
    return nc


WSLOT = 1024


WSHAPES = {
    "a_w_in": [2, 1024, 10240], "a_w_out": [2, 1024, 1024],
    "b_w_in": [1, 1024, 3608], "b_gate_b": [1, 24], "b_pe_k": [1, 32, 128], "b_w1_k": [1, 32, 128, 128],
    "b_w2_k": [1, 128, 128], "b_pe_v": [1, 32, 128], "b_w1_v": [1, 32, 128, 128], "b_w2_v": [1, 128, 128],
    "b_w_out": [1, 1024, 1024],
    "c_w_in": [1, 1024, 2560], "c_conv_w": [1, 4, 1280], "c_conv_b": [1, 1280], "c_wa": [1, 10, 128, 128],
    "c_ba": [1, 1280], "c_wx": [1, 10, 128, 128], "c_bx": [1, 1280], "c_lambda": [1, 1280], "c_w_out": [1, 1280, 1024],
}


def load_w(C, src, K, W):
    nc, sc = C.nc, C.sc
    assert K * W <= WSLOT
    si = C.wst_i % len(C.wst)
    C.wst_i += 1
    bi = C.wbf_i % len(C.wbf)
    C.wbf_i += 1
    stv = C.wst[si][:, 0:K * W].rearrange("p (k w) -> p k w", w=W)
    bfv = C.wbf[bi][:, 0:K * W].rearrange("p (k w) -> p k w", w=W)
    sc.dma("sp", stv, src.rearrange("(k p) w -> p k w", p=128), w=[("wst", si)])
    if getattr(C, "cast_eng", "pool") == "dve":
        sc.op("dve", lambda: nc.vector.tensor_copy(out=C.wbf[bi][:, 0:K * W], in_=C.wst[si][:, 0:K * W]),
              r=[("wst", si)], w=[("wbf", bi)])
    else:
        sc.op("pool", lambda: nc.gpsimd.tensor_copy(out=C.wbf[bi][:, 0:K * W], in_=C.wst[si][:, 0:K * W]),
              r=[("wst", si)], w=[("wbf", bi)])
    return bfv, ("wbf", bi)


def hk(c, q):
    return ("hT", c, q)


def make_hT(C, li):
    from contextlib import ExitStack
    nc, sc = C.nc, C.sc
    with ExitStack() as st3:
        C.rstd = st3.enter_context(nc.sbuf_tensor("rstd" + C.uid, [128, 512], F32))
        C.sq = st3.enter_context(nc.sbuf_tensor("sq" + C.uid, [128, 2, 512], BF16))
        _make_hT(C, li)
        sc.barrier()


def _make_hT(C, li):
    nc, sc = C.nc, C.sc
    for q in range(NQ):
        rms_rstd(C, q)
        for c in range(NCH):
            sc.op("dve", lambda: nc.vector.scalar_tensor_tensor(
                out=C.hT[:, c, q * 512:(q + 1) * 512], in0=C.xT[:, c, q * 512:(q + 1) * 512],
                scalar=C.gall[:, li, c:c + 1], in1=C.rstd[:], op0=ALU.mult, op1=ALU.mult),
                r=xk(c, q) + ["rstd", "gall"], w=[hk(c, q)])


def proj_T(C, wv, wkey, wcol, M, q, pb, K=NCH, rhs=None, rkeys=None):
    nc, sc = C.nc, C.sc
    for k in range(K):
        sc.op("pe", lambda: nc.tensor.matmul(C.ps[pb][0:M, :], lhsT=wv[:, k, wcol:wcol + M],
                                              rhs=C.hT[:, k, q * 512:(q + 1) * 512],
                                              start=(k == 0), stop=(k == K - 1)),
              r=[wkey, hk(k, q)], w=["ps%d" % pb])


def residual_add(C, oc, q, pb):
    nc, sc = C.nc, C.sc
    sc.op("dve", lambda: nc.vector.tensor_tensor(out=C.xT[:, oc, q * 512:(q + 1) * 512], in0=C.ps[pb][:],
                                                  in1=C.xT[:, oc, q * 512:(q + 1) * 512], op=ALU.add),
          r=["ps%d" % pb] + xk(oc, q), w=xk(oc, q))


def layer_c(C, j):
    nc, sc, st = C.nc, C.sc, C.st2
    u = C.uid
    sb = lambda n, s, d=F32: st.enter_context(nc.sbuf_tensor(n + u, list(s), d))
    NB = 10
    cpt = sb("cpt", [80, 128])
    cpar = sb("cpar", [128, 80])
    spt = sb("spt", [128, 4, NB])
    XB = [sb("cXB%d" % i, [128, 3 + S]) for i in range(2)]
    B = [None] + [sb("cB%d" % i, [128, 3 + S]) for i in range(1, 4)]
    xcb = sb("xcb", [128, S], BF16)
    SZ = [sb("sz%d" % i, [128, S], BF16) for i in range(2)]
    yT = sb("yT", [128, 5, S], BF16)
    W = C.W
    sc.dma("sp", cpt[0:40, :], W["c_conv_w"][j].rearrange("j (n p) -> (j n) p", p=128), w=["cpt0"])
    for i, nm in enumerate(["c_conv_b", "c_ba", "c_bx", "c_lambda"]):
        sc.dma("sp", cpt[40 + 10 * i:50 + 10 * i, :], W[nm][j].rearrange("(n p) -> n p", p=128), w=["cpt%d" % (i + 1)])
    sc.op("pe", lambda: nc.tensor.transpose(out=C.ps[0][:, 0:80], in_=cpt[:, :], identity=C.ident[0:80, 0:80]),
          r=["cpt%d" % i for i in range(5)] + ["ident"], w=["ps0"])
    sc.op("dve", lambda: nc.vector.tensor_copy(out=cpar[:], in_=C.ps[0][:, 0:80]), r=["ps0"], w=["cpar"])
    lam = cpar[:, 70:80]
    sc.op("dve", lambda: nc.vector.tensor_scalar(out=spt[:, 0, :], in0=lam, scalar1=-1.0, scalar2=None, op0=ALU.mult), r=["cpar"], w=["spt0"])
    sc.op("dve", lambda: nc.vector.tensor_tensor(out=spt[:, 0, :], in0=spt[:, 0, :], in1=lam, op=ALU.max), r=["cpar", "spt0"], w=["spt0"])
    sc.op("act", lambda: nc.scalar.activation(out=spt[:, 0, :], in_=spt[:, 0, :], func=AF.Exp, scale=-1.0), r=["spt0"], w=["spt0"])
    sc.op("dve", lambda: nc.vector.tensor_scalar(out=spt[:, 0, :], in0=spt[:, 0, :], scalar1=1.0, scalar2=None, op0=ALU.add), r=["spt0"], w=["spt0"])
    sc.op("act", lambda: nc.scalar.activation(out=spt[:, 0, :], in_=spt[:, 0, :], func=AF.Ln), r=["spt0"], w=["spt0"])
    sc.op("dve", lambda: nc.vector.tensor_scalar(out=spt[:, 1, :], in0=lam, scalar1=-1.0, scalar2=0.0, op0=ALU.mult, op1=ALU.max), r=["cpar"], w=["spt1"])
    sc.op("dve", lambda: nc.vector.tensor_tensor(out=spt[:, 1, :], in0=spt[:, 1, :], in1=spt[:, 0, :], op=ALU.add), r=["spt0", "spt1"], w=["spt1"])
    sc.op("dve", lambda: nc.vector.tensor_scalar(out=spt[:, 2, :], in0=spt[:, 1, :], scalar1=-8.0, scalar2=None, op0=ALU.mult), r=["spt1"], w=["spt2"])
    sc.op("dve", lambda: nc.vector.tensor_scalar(out=spt[:, 3, :], in0=spt[:, 1, :], scalar1=-16.0, scalar2=None, op0=ALU.mult), r=["spt1"], w=["spt3"])
    for i in range(2):
        sc.op("pool", lambda: nc.gpsimd.memset(XB[i][:, 0:3], 0.0), w=[("XBpad", i)])

    win = W["c_w_in"][j]
    wo = W["c_w_out"][j]

    def P(n):
        xb, sz = XB[n % 2], SZ[n % 2]
        wxb, kxb = load_w(C, win[:, n * 128:(n + 1) * 128], NCH, 128)
        wz, kz = load_w(C, win[:, 1280 + n * 128:1280 + (n + 1) * 128], NCH, 128)
        for q in range(NQ):
            pb = q % 2
            proj_T(C, wxb, kxb, 0, 128, q, pb)
            sc.op("act", lambda: nc.scalar.copy(out=xb[:, 3 + q * 512:3 + (q + 1) * 512], in_=C.ps[pb][:]),
                  r=["ps%d" % pb, ("XBpad", n % 2)], w=[("xb", n % 2, q)])
            yield
            pb2 = 2 + q % 2
            proj_T(C, wz, kz, 0, 128, q, pb2)
            sc.op("act", lambda: nc.scalar.activation(out=sz[:, q * 512:(q + 1) * 512], in_=C.ps[pb2][:], func=AF.Silu),
                  r=["ps%d" % pb2], w=[("sz", n % 2, q)])
            yield

    def E(n):
        n5 = n % 5
        xb, sz = XB[n % 2], SZ[n % 2]
        xc, rr, ii, aa = B[1], B[2], B[3], xb
        wa, ka = load_w(C, W["c_wa"][j, n], 1, 128)
        wx, kx = load_w(C, W["c_wx"][j, n], 1, 128)
        cw = lambda jj: cpar[:, jj * 10 + n:jj * 10 + n + 1]
        for q in range(NQ):
            sl = slice(3 + q * 512, 3 + (q + 1) * 512)
            xbk = [("xb", n % 2, q)] + ([("xb", n % 2, q - 1)] if q > 0 else [("XBpad", n % 2)])
            sc.op("dve", lambda: nc.vector.tensor_scalar(out=xc[:, sl], in0=xb[:, sl], scalar1=cw(3), scalar2=cpar[:, 40 + n:41 + n],
                                                          op0=ALU.mult, op1=ALU.add), r=xbk + ["cpar"], w=[("xc", q)])
            for jj in range(3):
                sc.op("dve", lambda: nc.vector.scalar_tensor_tensor(out=xc[:, sl], in0=xb[:, jj + q * 512:jj + (q + 1) * 512], scalar=cw(jj),
                                                                     in1=xc[:, sl], op0=ALU.mult, op1=ALU.add),
                      r=xbk + [("xc", q), "cpar"], w=[("xc", q)])
            sc.op("pool", lambda: nc.gpsimd.tensor_copy(out=xcb[:, q * 512:(q + 1) * 512], in_=xc[:, sl]), r=[("xc", q)], w=[("xcb", q)])
            pb = 4 + q % 2
            sc.op("pe", lambda: nc.tensor.matmul(C.ps[pb][:], lhsT=wa[:, 0, :], rhs=xcb[:, q * 512:(q + 1) * 512], start=True, stop=True),
                  r=[ka, ("xcb", q)], w=["ps%d" % pb])
            sc.op("act", lambda: nc.scalar.activation(out=rr[:, sl], in_=C.ps[pb][:], func=AF.Sigmoid,
                                                      bias=cpar[:, 50 + n:51 + n]), r=["ps%d" % pb, "cpar"], w=[("rr", q)])
            pb = 6 + q % 2
            sc.op("pe", lambda: nc.tensor.matmul(C.ps[pb][:], lhsT=wx[:, 0, :], rhs=xcb[:, q * 512:(q + 1) * 512], start=True, stop=True),
                  r=[kx, ("xcb", q)], w=["ps%d" % pb])
            sc.op("act", lambda: nc.scalar.activation(out=ii[:, sl], in_=C.ps[pb][:], func=AF.Sigmoid,
                                                      bias=cpar[:, 60 + n:61 + n]), r=["ps%d" % pb, "cpar"], w=[("ii", q)])
            yield
        for q in range(NQ):
            sl = slice(3 + q * 512, 3 + (q + 1) * 512)
            sc.op("act", lambda: nc.scalar.activation(out=aa[:, sl], in_=rr[:, sl], func=AF.Exp, scale=spt[:, 2, n:n + 1]),
                  r=[("rr", q), "spt2"], w=[("xb", n % 2, q)])
            sc.op("act", lambda: nc.scalar.activation(out=rr[:, sl], in_=rr[:, sl], func=AF.Exp, scale=spt[:, 3, n:n + 1]),
                  r=[("rr", q), "spt3"], w=[("rr", q)])
            sc.op("pool", lambda: nc.gpsimd.tensor_tensor(out=ii[:, sl], in0=ii[:, sl], in1=xc[:, sl], op=ALU.mult),
                  r=[("ii", q), ("xc", q)], w=[("ii", q)])
        for q in range(NQ):
            sl = slice(3 + q * 512, 3 + (q + 1) * 512)
            sc.op("dve", lambda: nc.vector.tensor_scalar(out=rr[:, sl], in0=rr[:, sl], scalar1=-1.0, scalar2=1.0, op0=ALU.mult, op1=ALU.add),
                  r=[("rr", q)], w=[("rr", q)])
        yield
        for q in range(NQ):
            sl = slice(3 + q * 512, 3 + (q + 1) * 512)
            sc.op("act", lambda: nc.scalar.activation(out=rr[:, sl], in_=rr[:, sl], func=AF.Sqrt), r=[("rr", q)], w=[("rr", q)])
            sc.op("pool", lambda: nc.gpsimd.tensor_tensor(out=ii[:, sl], in0=ii[:, sl], in1=rr[:, sl], op=ALU.mult),
                  r=[("ii", q), ("rr", q)], w=[("ii", q)])
        yield
        for q in range(NQ):
            sl = slice(3 + q * 512, 3 + (q + 1) * 512)
            init = 0.0 if q == 0 else xc[:, 3 + q * 512 - 1:3 + q * 512]
            sc.op("dve", lambda: nc.vector.tensor_tensor_scan(out=xc[:, sl], data0=aa[:, sl], data1=ii[:, sl], initial=init,
                                                               op0=ALU.mult, op1=ALU.add),
                  r=[("xb", n % 2, q), ("ii", q), ("xc", q)] + ([("xc", q - 1)] if q else []), w=[("xc", q)])
            sc.op("pool", lambda: nc.gpsimd.tensor_tensor(out=yT[:, n5, q * 512:(q + 1) * 512], in0=xc[:, sl], in1=sz[:, q * 512:(q + 1) * 512], op=ALU.mult),
                  r=[("xc", q), ("sz", n % 2, q)], w=[("yT", n5, q)])
            yield

    def O(half):
        for oc in range(8):
            wv, wk = load_w(C, wo[half * 640:(half + 1) * 640, oc * 128:(oc + 1) * 128], 5, 128)
            for q in range(NQ):
                pb = (oc * NQ + q) % 2
                for k in range(5):
                    sc.op("pe", lambda: nc.tensor.matmul(C.ps[pb][:], lhsT=wv[:, k, :],
                                                          rhs=yT[:, k, q * 512:(q + 1) * 512], start=(k == 0), stop=(k == 4)),
                          r=[wk, ("yT", k, q)], w=["ps%d" % pb])
                residual_add(C, oc, q, pb)

    run_interleaved([P(0)])
    for n in range(10):
        tasks = [E(n)]
        if n + 1 < 10:
            tasks.append(P(n + 1))
        run_interleaved(tasks)
        if n % 5 == 4:
            O(n // 5)


def xk(c, q):
    return [("xT", c, tt) for tt in range(4 * q, 4 * q + 4)]


def load_x(C, s):
    nc, sc = C.nc, C.sc
    for tt in range(NT):
        b = tt % 2
        sc.dma("sp", C.io[:, b, :], C.x[s, tt * 128:(tt + 1) * 128, :], w=[("io", b, cc) for cc in range(NCH)])
        for c in range(NCH):
            pb = (tt * NCH + c) % 4
            sc.op("pe", lambda: nc.tensor.transpose(out=C.ps[pb][:, 0:128], in_=C.io[:, b, c * 128:(c + 1) * 128],
                                                     identity=C.ident[:]),
                  r=[("io", b, c), "ident"], w=["ps%d" % pb])
            eng = "act" if c % 2 else "dve"
            if eng == "act":
                sc.op("act", lambda: nc.scalar.copy(out=C.xT[:, c, tt * 128:(tt + 1) * 128], in_=C.ps[pb][:, 0:128]),
                      r=["ps%d" % pb], w=[("xT", c, tt)])
            else:
                sc.op("dve", lambda: nc.vector.tensor_copy(out=C.xT[:, c, tt * 128:(tt + 1) * 128], in_=C.ps[pb][:, 0:128]),
                      r=["ps%d" % pb], w=[("xT", c, tt)])


def rms_rstd(C, q, dst_keys_done=None):
    nc, sc = C.nc, C.sc
    pb = 4 + (q % 2)
    for c in range(NCH):
        b = c % 2
        sc.op("act", lambda: nc.scalar.activation(out=C.sq[:, b, :], in_=C.xT[:, c, q * 512:(q + 1) * 512], func=AF.Square),
              r=xk(c, q), w=[("sq", b)])
        sc.op("pe", lambda: nc.tensor.matmul(C.ps[pb][:], lhsT=C.onesb[:], rhs=C.sq[:, b, :], start=(c == 0), stop=(c == NCH - 1)),
              r=[("sq", b), "onesb"], w=["ps%d" % pb])
    sc.op("dve", lambda: nc.vector.tensor_scalar(out=C.rstd[:], in0=C.ps[pb][:], scalar1=EPS, scalar2=None, op0=ALU.add),
          r=["ps%d" % pb], w=["rstd"])
    sc.op("act", lambda: nc.scalar.activation(out=C.rstd[:], in_=C.rstd[:], func=AF.Sqrt), r=["rstd"], w=["rstd"])
    sc.op("dve", lambda: nc.vector.reciprocal(out=C.rstd[:], in_=C.rstd[:]), r=["rstd"], w=["rstd"])


def final_norm_store(C, s):
    nc, sc = C.nc, C.sc
    for q in range(NQ):
        rms_rstd(C, q)
        for c in range(NCH):
            sc.op("dve", lambda: nc.vector.scalar_tensor_tensor(
                out=C.xT[:, c, q * 512:(q + 1) * 512], in0=C.xT[:, c, q * 512:(q + 1) * 512],
                scalar=C.gall[:, 4, c:c + 1], in1=C.rstd[:], op0=ALU.mult, op1=ALU.mult),
                r=xk(c, q) + ["rstd", "gall"], w=xk(c, q))
        for t4 in range(4):
            tt = q * 4 + t4
            b = tt % 2
            for c in range(NCH):
                pb = (tt * NCH + c) % 4
                sc.op("pe", lambda: nc.tensor.transpose(out=C.ps[pb][:, 0:128], in_=C.xT[:, c, tt * 128:(tt + 1) * 128],
                                                         identity=C.ident[:]),
                      r=[("xT", c, tt), "ident"], w=["ps%d" % pb])
                if c % 2:
                    sc.op("act", lambda: nc.scalar.copy(out=C.io[:, b, c * 128:(c + 1) * 128], in_=C.ps[pb][:, 0:128]),
                          r=["ps%d" % pb], w=[("io", b, c)])
                else:
                    sc.op("dve", lambda: nc.vector.tensor_copy(out=C.io[:, b, c * 128:(c + 1) * 128], in_=C.ps[pb][:, 0:128]),
                          r=["ps%d" % pb], w=[("io", b, c)])
            sc.dma("sp", C.out[s, tt * 128:(tt + 1) * 128, :], C.io[:, b, :], r=[("io", b, cc) for cc in range(NCH)])


NEG = -30000.0
SCALE = 128.0 ** -0.5
A_GROUPS = ((128, 1), (512, 4), (2048, 16))


def dil_views(dst, src, dil, q):
    if dil == 1:
        return dst[:, q * 512:(q + 1) * 512], src
    n = 512 // dil
    dv = dst.rearrange("p (r m) -> p m r", r=dil)[:, n * q:n * (q + 1), :]
    sv = src.rearrange("p (m r) -> p m r", r=dil)
    return dv, sv


def rope_store(C, pb, views, q, wkey, qraw, rt):
    rope_part_a(C, pb, views, q, wkey, qraw, rt)()


def rope_part_a(C, pb, views, q, wkey, qraw, rt, rbanks=(2, 3)):
    nc, sc = C.nc, C.sc
    slot = C.rope_i % 2
    C.rope_i += 1
    rb = rbanks[slot % len(rbanks)]
    psk = "ps%d" % pb
    dv, sv = views(slice(0, 128), C.ps[pb][:, :])
    lokey = (wkey[0] + "lo",) + tuple(wkey[1:])
    sc.op("act", lambda: nc.scalar.copy(out=qraw[:, slot, :], in_=C.ps[pb][0:32, :]), r=[psk], w=[("qraw", slot)])
    sc.op("act", lambda: nc.scalar.copy(out=dv, in_=sv), r=[psk], w=[wkey, lokey])

    def part_b():
        sc.op("pe", lambda: nc.tensor.matmul(C.ps[rb][0:32, :], lhsT=C.protb[:, :], rhs=qraw[:, slot, :], start=True, stop=True),
              r=[("qraw", slot), "protb"], w=["ps%d" % rb])
        sc.op("dve", lambda: nc.vector.tensor_tensor(out=rt[:, slot, 0, :], in0=C.ps[pb][0:32, :], in1=C.ropeb[:, 0, q * 512:(q + 1) * 512], op=ALU.mult),
              r=[psk, "ropeb"], w=[("rt", slot, 0)])
        sc.op("dve", lambda: nc.vector.tensor_tensor(out=rt[:, slot, 1, :], in0=C.ps[rb][0:32, :], in1=C.ropeb[:, 1, q * 512:(q + 1) * 512], op=ALU.mult),
              r=["ps%d" % rb, "ropeb"], w=[("rt", slot, 1)])
        dv0, sv0 = views(slice(0, 32), rt[:, slot, 0, :])
        _, sv1 = views(slice(0, 32), rt[:, slot, 1, :])
        sc.op("dve", lambda: nc.vector.tensor_tensor(out=dv0, in0=sv0, in1=sv1, op=ALU.add),
              r=[("rt", slot, 0), ("rt", slot, 1)], w=[lokey])
    return part_b


def run_interleaved(tasks):
    active = list(tasks)
    while active:
        for t in list(active):
            try:
                next(t)
            except StopIteration:
                active.remove(t)


def layer_a(C, j):
    nc, sc, st = C.nc, C.sc, C.st2
    u_ = C.uid
    sb = lambda n, s, d=F32: st.enter_context(nc.sbuf_tensor(n + u_, list(s), d))
    qT = [sb("qT%d" % i, [128, S], BF16) for i in range(2)]
    kT = [sb("kT%d" % i, [128, S], BF16) for i in range(2)]
    V = [sb("V%d" % i, [128, 16, 128], BF16) for i in range(2)]
    szT = [sb("szT%d" % i, [128, S], BF16) for i in range(2)]
    num = sb("num", [128, S])
    den = sb("den", [128, S])
    yT = sb("yT", [128, 4, S], BF16)
    qraw = sb("qraw", [32, 2, 512], BF16)
    rt = sb("rt", [32, 2, 2, 512], BF16)
    PT = sb("PT", [128, 4, 512], BF16)
    win = C.W["a_w_in"][j]
    wout = C.W["a_w_out"][j]
    DIAG, PREV = 0, 1

    PJ = (0, 1, 3)
    pj = [0]
    preW = {}

    def wcols(s_):
        hd, g = divmod(s_, 3)
        cols = [9216 + hd * 128] if g == 0 else []
        return cols + [((g * 3 + which) * 8 + hd) * 128 for which in range(3)]

    def getw(s_, i):
        if (s_, i) not in preW:
            col = wcols(s_)[i]
            preW[(s_, i)] = load_w(C, win[:, col:col + 128], NCH, 128)
        return preW.pop((s_, i)) if False else preW[(s_, i)]

    def nextbank():
        pj[0] += 1
        return PJ[pj[0] % 3]

    def P(s):
        hd, g = divmod(s, 3)
        b = s % 2
        wlen, dil = A_GROUPS[g]
        L = S // dil
        nbs = L // 128
        wi = 0
        if g == 0:
            wz, kz = getw(s, wi)
            wi += 1
            for q in range(NQ):
                pb = nextbank()
                proj_T(C, wz, kz, 0, 128, q, pb)
                sc.op("act", lambda: nc.scalar.activation(out=szT[hd % 2][:, q * 512:(q + 1) * 512], in_=C.ps[pb][:], func=AF.Silu),
                      r=["ps%d" % pb], w=[("szT", hd % 2, q)])
                yield
        pending = None
        for which, dst, nm in ((0, qT[b], "qT"), (1, kT[b], "kT")):
            wv, wk = getw(s, wi)
            wi += 1
            for q in range(NQ):
                pb = nextbank()
                proj_T(C, wv, wk, 0, 128, q, pb)
                nxt = rope_part_a(C, pb, (lambda rows, src, dst=dst, q=q: dil_views(dst[rows, :], src, 1, q)), q, (nm, b, q), qraw, rt,
                                  rbanks=(2,))
                if pending is not None:
                    pending()
                pending = nxt
                yield
        wv, wk = getw(s, wi)
        for b4 in range(4):
            pb = nextbank()
            for bi in range(4):
                bb = b4 * 4 + bi
                r_, jb = bb // nbs, bb % nbs
                t0 = r_ + dil * jb * 128
                t1 = t0 + dil * 127
                hks = [hk(k, qq) for k in range(NCH) for qq in range(t0 // 512, t1 // 512 + 1)]
                for k in range(NCH):
                    sc.op("pe", lambda: nc.tensor.matmul(C.ps[pb][:, bi * 128:(bi + 1) * 128],
                                                          lhsT=C.hT[:, k, t0:t1 + 1:dil], rhs=wv[:, k, :],
                                                          start=(k == 0), stop=(k == NCH - 1)),
                          r=[wk] + hks, w=["ps%d" % pb])
                if bi == 1 and pending is not None:
                    pending()
                    pending = None
                if bi % 2 == 1:
                    yield
            sc.op("act", lambda: nc.scalar.copy(out=V[b][:, b4 * 4:(b4 + 1) * 4, :].rearrange("p a d -> p (a d)"), in_=C.ps[pb][:]),
                  r=["ps%d" % pb], w=[("V", b, b4)])
            yield

    def T(s):
        hd, g = divmod(s, 3)
        b = s % 2
        wlen, dil = A_GROUPS[g]
        L = S // dil
        nbs = L // 128
        qk_keys = [(nm, b, q) for nm in ("qT", "qTlo", "kT", "kTlo") for q in range(NQ)]

        def cols(blk):
            r_, jb = blk // nbs, blk % nbs
            t0 = r_ + dil * 128 * jb
            return slice(t0, t0 + dil * 127 + 1, dil)

        if g == 0:
            units = [[4 * c + i for i in range(4)] for c in range(4)]
        elif g == 1:
            units = [[r_ * nbs + jb for r_ in range(4)] for jb in range(nbs)]
        else:
            units = [[r0 + i for i in range(4)] for r0 in range(0, 16, 4)]

        def s_phase(qbs):
            pairs = []
            for qb in qbs:
                if qb % nbs > 0:
                    pairs.append((qb - 1, qb, PREV))
                pairs.append((qb, qb, DIAG))
            chunks = []
            for c0 in range(0, len(pairs), 4):
                sub = pairs[c0:c0 + 4]
                sbank = 4 + C.sb_i % 2
                C.sb_i += 1
                sk = "ps%d" % sbank
                for p, (kb, qb, typ) in enumerate(sub):
                    sc.op("pe", lambda: nc.tensor.matmul(C.ps[sbank][:, p * 128:(p + 1) * 128], lhsT=kT[b][:, cols(kb)],
                                                          rhs=qT[b][:, cols(qb)], start=True, stop=False),
                          r=qk_keys, w=[sk])
                    sc.op("pe", lambda: nc.tensor.matmul(C.ps[sbank][:, p * 128:(p + 1) * 128], lhsT=C.identb[:],
                                                          rhs=C.maskb[:, typ, :], start=False, stop=True),
                          r=["identb", "maskb"], w=[sk])
                npair = len(sub)
                pti = C.pt_i % 4
                C.pt_i += 1
                sc.op("act", lambda: nc.scalar.activation(out=PT[:, pti, 0:npair * 128], in_=C.ps[sbank][:, 0:npair * 128],
                                                          func=AF.Exp, scale=SCALE), r=[sk], w=[("PT", pti)])
                chunks.append((sub, pti))
            return qbs, chunks

        def pv_phase(state):
            qbs, chunks = state
            for sub, pti in chunks:
                for p, (kb, qb, typ) in enumerate(sub):
                    ci = qbs.index(qb)
                    first = (typ == PREV) or (qb % nbs == 0)
                    sc.op("pe", lambda: nc.tensor.matmul(C.ps[6][:, ci * 128:(ci + 1) * 128], lhsT=V[b][:, kb, :],
                                                          rhs=PT[:, pti, p * 128:(p + 1) * 128], start=first, stop=(typ == DIAG)),
                          r=[("V", b, kb // 4), ("PT", pti)], w=["ps6"])
                    sc.op("pe", lambda: nc.tensor.matmul(C.ps[7][:, ci * 128:(ci + 1) * 128], lhsT=C.ones1b[:],
                                                          rhs=PT[:, pti, p * 128:(p + 1) * 128], start=first, stop=(typ == DIAG)),
                          r=["ones1b", ("PT", pti)], w=["ps7"])
            qb0 = qbs[0]
            if g == 0:
                c = qb0 // 4
                views = lambda t, pv: (t[:, c * 512:(c + 1) * 512], pv)
                chunks_k = [c]
            elif g == 1:
                jb = qb0 % nbs
                views = lambda t, pv: (t[:, jb * 512:(jb + 1) * 512].rearrange("p (m r) -> p m r", r=4),
                                       pv.rearrange("p (r m) -> p m r", r=4))
                chunks_k = [jb]
            else:
                views = lambda t, pv: (t.rearrange("p (m r) -> p m r", r=16)[:, :, qb0:qb0 + 4],
                                       pv.rearrange("p (r m) -> p m r", r=4))
                chunks_k = [0, 1, 2, 3]
            for acc, bank, nm in ((num, 6, "num"), (den, 7, "den")):
                av, pv = views(acc[:, :], C.ps[bank][:, :])
                keys = [(nm, c) for c in chunks_k]
                if g == 0:
                    if nm == "num":
                        sc.op("dve", lambda: nc.vector.tensor_copy(out=av, in_=pv), r=["ps%d" % bank], w=keys)
                    else:
                        sc.op("act", lambda: nc.scalar.copy(out=av, in_=pv), r=["ps%d" % bank], w=keys)
                else:
                    sc.op("dve", lambda: nc.vector.tensor_tensor(out=av, in0=pv, in1=av, op=ALU.add),
                          r=["ps%d" % bank] + keys, w=keys)

        pending = None
        for qbs in units + [None]:
            if qbs is not None:
                stt = s_phase(qbs)
                yield
            else:
                stt = None
            if pending is not None:
                pv_phase(pending)
                yield
            pending = stt

    def F(hd):
        for q in range(NQ):
            sl = slice(q * 512, (q + 1) * 512)
            sc.op("act", lambda: nc.scalar.activation(out=den[:, sl], in_=den[:, sl], func=AF.Ln), r=[("den", q)], w=[("den", q)])
            sc.op("act", lambda: nc.scalar.activation(out=den[:, sl], in_=den[:, sl], func=AF.Exp, scale=-1.0), r=[("den", q)], w=[("den", q)])
            sc.op("pool", lambda: nc.gpsimd.tensor_tensor(out=den[:, sl], in0=den[:, sl], in1=szT[hd % 2][:, sl], op=ALU.mult),
                  r=[("den", q), ("szT", hd % 2, q)], w=[("den", q)])
            sc.op("dve", lambda: nc.vector.tensor_tensor(out=yT[:, hd % 4, sl], in0=num[:, sl], in1=den[:, sl], op=ALU.mult),
                  r=[("num", q), ("den", q)], w=[("yT", hd % 4, q)])

    preO = {}

    def getO(half, og):
        if (half, og) not in preO:
            preO[(half, og)] = load_w(C, wout[half * 512:(half + 1) * 512, og * 256:(og + 1) * 256], 4, 256)
        return preO[(half, og)]

    def O(half):
        for og in range(4):
            wv, wk = getO(half, og)
            for o2 in range(2):
                oc = og * 2 + o2
                for q in range(NQ):
                    pb = (oc * NQ + q) % 2
                    for k in range(4):
                        sc.op("pe", lambda: nc.tensor.matmul(C.ps[pb][:], lhsT=wv[:, k, o2 * 128:(o2 + 1) * 128],
                                                              rhs=yT[:, k, q * 512:(q + 1) * 512], start=(k == 0), stop=(k == 3)),
                              r=[wk, ("yT", k, q)], w=["ps%d" % pb])
                    residual_add(C, oc, q, pb)

    NS = 24
    run_interleaved([P(0)])
    for s in range(NS):
        tasks = [T(s)]
        if s + 1 < NS:
            tasks.append(P(s + 1))
        run_interleaved(tasks)
        hd, g = divmod(s, 3)
        if s + 2 < NS:
            getw(s + 2, 0)
        if g == 2:
            if hd % 4 == 3:
                getO(hd // 4, 0)
            F(hd)
            if hd % 4 == 3:
                O(hd // 4)


def load_w_pre(C, src3, K, W):
    nc, sc = C.nc, C.sc
    assert K * W <= WSLOT
    si = C.wst_i % len(C.wst)
    C.wst_i += 1
    bi = C.wbf_i % len(C.wbf)
    C.wbf_i += 1
    stv = C.wst[si][:, 0:K * W].rearrange("p (k w) -> p k w", w=W)
    bfv = C.wbf[bi][:, 0:K * W].rearrange("p (k w) -> p k w", w=W)
    sc.dma("sp", stv, src3, w=[("wst", si)])
    sc.op("pool", lambda: nc.gpsimd.tensor_copy(out=C.wbf[bi][:, 0:K * W], in_=C.wst[si][:, 0:K * W]),
          r=[("wst", si)], w=[("wbf", bi)])
    return bfv, ("wbf", bi)


def layer_b(C, j):
    nc, sc, st = C.nc, C.sc, C.st2
    u = C.uid
    sb = lambda n, s, d=F32: st.enter_context(nc.sbuf_tensor(n + u, list(s), d))
    W = C.W
    win = W["b_w_in"][j]
    TINY = 1e-30
    DIAG, W4 = 0, 2
    allx = [("xT", c, tt) for c in range(NCH) for tt in range(NT)]
    sc.barrier()
    xb = C.xT.bitcast(BF16)[:].rearrange("p c t -> p (c t)")
    reg = lambda off, n: xb[:, off:off + n]
    qT4 = reg(0, 8192)
    big = reg(8192, 8192).rearrange("p (a t) -> p a t", a=4)
    ksT = reg(16384, 2048)
    kwT = reg(18432, 2048)
    Vs = reg(20480, 2080).rearrange("p (a d) -> p a d", d=130)
    Vw = reg(22560, 2080).rearrange("p (a d) -> p a d", d=130)
    PT = reg(24640, 2048).rearrange("p (a d) -> p a d", d=512)
    cmaskb = reg(26688, 2048).rearrange("p (a d) -> p a d", d=128)
    eexpb = reg(28736, 2048)
    yT8 = sb("yT8", [128, 8, S], BF16)
    g_tok = sb("gtok", [128, NT, 24])
    gb = sb("gb", [128, 24])
    kccT = sb("kccT", [128, 2, 128], BF16)
    vcc = sb("vcc", [128, 2, 162], BF16)
    selk = sb("selk", [128, NT, 32])
    sela = sb("sela", [128, NT, 32])
    o_toks = [sb("otok%d" % t, [128, 4, 128]) for t in range(4)]
    sms = [sb("sm%d" % t, [128, 160]) for t in range(4)]
    selbT4s = [sb("selbT4%d" % t, [32, 512], BF16) for t in range(4)]
    maskb4 = sb("maskb4", [128, 3, 512], BF16)
    qraw = sb("qraw", [32, 2, 512], BF16)
    rt = sb("rt", [32, 2, 2, 512])
    stg = sb("stg", [128, 2048])
    peT = sb("peT", [128, 2, 32], BF16)
    hcb = sb("hcb", [128, 128], BF16)
    biasc = sb("biasc", [128, 2])

    sc.dma("sp", stg[:, :].rearrange("p (a b) -> p a b", a=16), C.c_cmask.rearrange("a p b -> p a b"), w=["stg"])
    sc.op("dve", lambda: nc.vector.tensor_copy(out=cmaskb.rearrange("p a d -> p (a d)"), in_=stg[:, :]), r=["stg"], w=["cmaskb"])
    sc.dma("sp", stg[0:32, :], C.c_eexp, r=["cmaskb"], w=["stg"])
    sc.op("dve", lambda: nc.vector.tensor_copy(out=eexpb[0:32, :], in_=stg[0:32, :]), r=["stg"], w=["eexpb"])
    sc.dma("sp", stg[:, 0:32], C.c_cover, r=["eexpb"], w=["stg"])
    for g in range(2):
        sc.op("dve", lambda: nc.vector.tensor_copy(out=vcc[:, g, 129:161], in_=stg[:, 0:32]), r=["stg"], w=[("vcc", g, "cov")])
        sc.op("dve", lambda: nc.vector.memset(vcc[:, g, 128:129], 1.0), w=[("vcc", g, "one")])
    sc.dma("sp", selk[:], C.c_selk.rearrange("a p n -> p a n"), w=["selk"])
    sc.dma("sp", sela[:], C.c_sela.rearrange("a p n -> p a n"), w=["sela"])
    sc.dma("sp", gb[:], W["b_gate_b"][j].partition_broadcast(128), w=["gb"])
    for kv, nm in enumerate(["b_pe_k", "b_pe_v"]):
        sc.dma("sp", stg[0:32, 128 * (1 + kv):128 * (2 + kv)], W[nm][j], r=[("vcc", 0, "cov"), ("vcc", 1, "cov")], w=[("stgpe", kv)])
        sc.op("pe", lambda: nc.tensor.transpose(out=C.ps[0][:, 0:32], in_=stg[0:32, 128 * (1 + kv):128 * (2 + kv)], identity=C.ident[0:32, 0:32]),
              r=[("stgpe", kv), "ident"], w=["ps0"])
        sc.op("dve", lambda: nc.vector.tensor_copy(out=peT[:, kv, :], in_=C.ps[0][:, 0:32]), r=["ps0"], w=[("peT", kv)])
    for vt in (Vs, Vw):
        sc.op("pool", lambda: nc.gpsimd.memset(vt[:, :, 128:129], 1.0), w=["Vones"])

    nat = lambda dst: (lambda rows, src, dst=dst: None)

    def proj_rope(col, dst2d, nm):
        wv, wk = load_w(C, win[:, col:col + 128], NCH, 128)
        for q in range(NQ):
            pb = q % 2
            proj_T(C, wv, wk, 0, 128, q, pb)
            rope_store(C, pb, (lambda rows, src, q=q: (dst2d[rows, q * 512:(q + 1) * 512], src)), q, (nm, q), qraw, rt)

    def proj_plain(col, dst2d, nm, func=None):
        wv, wk = load_w(C, win[:, col:col + 128], NCH, 128)
        for q in range(NQ):
            pb = q % 2
            proj_T(C, wv, wk, 0, 128, q, pb)
            if func is None:
                sc.op("act", lambda: nc.scalar.copy(out=dst2d[:, q * 512:(q + 1) * 512], in_=C.ps[pb][:]), r=["ps%d" % pb], w=[(nm, q)])
            else:
                sc.op("act", lambda: nc.scalar.activation(out=dst2d[:, q * 512:(q + 1) * 512], in_=C.ps[pb][:], func=func),
                      r=["ps%d" % pb], w=[(nm, q)])

    kvcol = lambda i, g: 1024 + (i * 2 + g) * 128
    for g in range(2):
        proj_rope(kvcol(0, g), big[:, g * 2 + 0, :], "cmp%d" % (g * 2))
        proj_plain(kvcol(1, g), big[:, g * 2 + 1, :], "cmp%d" % (g * 2 + 1))
    def compress_task():
        cst = stg[:, :].rearrange("p (a b) -> p a b", a=2)
        cbf = reg(24640, 2048).rearrange("p (a b) -> p a b", a=2)
        w2bf = reg(30784, 128)
        li = [0]

        def cload(src3, K):
            sl = li[0] % 2
            li[0] += 1
            sc.dma("sp", cst[:, sl, 0:K * 128].rearrange("p (k w) -> p k w", w=128), src3,
                   r=[("peT", 0), ("peT", 1), ("vcc", 0, "cov"), ("vcc", 1, "cov"), "eexpb", "cmaskb"], w=[("cst", sl)])
            sc.op("pool", lambda: nc.gpsimd.tensor_copy(out=cbf[:, sl, 0:K * 128], in_=cst[:, sl, 0:K * 128]),
                  r=[("cst", sl)], w=[("cbf", sl)])
            return cbf[:, sl, :].rearrange("p (k w) -> p k w", w=128), ("cbf", sl)

        for kv in range(2):
            w1 = W["b_w1_k" if kv == 0 else "b_w1_v"][j]
            w2 = W["b_w2_k" if kv == 0 else "b_w2_v"][j]
            w2v_, w2k_ = cload(w2.rearrange("(k p) w -> p k w", p=128), 1)
            sc.op("pool", lambda: nc.gpsimd.tensor_copy(out=w2bf[:, :], in_=w2v_[:, 0, :]), r=[w2k_], w=["w2bf"])
            srcs = [big[:, g * 2 + kv, :] for g in range(2)]
            skeys = [[("cmp%d" % (g * 2 + kv), q) for q in range(NQ)] +
                     ([("cmp%dlo" % (g * 2 + kv), q) for q in range(NQ)] if kv == 0 else []) for g in range(2)]
            for h in range(4):
                wv, wk = cload(w1[h * 8:(h + 1) * 8].rearrange("p d e -> d p e"), 8)
                for p8 in range(8):
                    p = h * 8 + p8
                    sc.op("pe", lambda: nc.tensor.matmul(C.ps[4][:, 0:1], lhsT=wv[:, p8, :], rhs=peT[:, kv, p:p + 1], start=(p == 0), stop=(p == 31)),
                          r=[wk, ("peT", kv)], w=["ps4"])
                yield
                for g in range(2):
                    bank = 5 if g == 0 else 7
                    for p8 in range(8):
                        p = h * 8 + p8
                        sc.op("pe", lambda: nc.tensor.matmul(C.ps[bank][:, 0:127], lhsT=wv[:, p8, :], rhs=srcs[g][:, p:p + 16 * 126 + 1:16],
                                                              start=(p == 0), stop=(p == 31)), r=[wk] + skeys[g], w=["ps%d" % bank])
                    yield
            sc.op("dve", lambda: nc.vector.tensor_copy(out=biasc[:, kv:kv + 1], in_=C.ps[4][:, 0:1]), r=["ps4"], w=[("biasc", kv)])
            for g in range(2):
                bank = 5 if g == 0 else 7
                sc.op("act", lambda: nc.scalar.activation(out=hcb[:, 0:127], in_=C.ps[bank][:, 0:127], func=AF.Silu, bias=biasc[:, kv:kv + 1]),
                      r=["ps%d" % bank, ("biasc", kv)], w=["hcb"])
                if kv == 0:
                    sc.op("pe", lambda: nc.tensor.matmul(C.ps[6][:, 0:127], lhsT=w2bf[:, :], rhs=hcb[:, 0:127], start=True, stop=True),
                          r=["w2bf", "hcb"], w=["ps6"])
                    sc.op("dve", lambda: nc.vector.tensor_copy(out=kccT[:, g, 0:127], in_=C.ps[6][:, 0:127]), r=["ps6"], w=[("kccT", g)])
                else:
                    sc.op("pe", lambda: nc.tensor.matmul(C.ps[6][0:127, 0:128], lhsT=hcb[:, 0:127], rhs=w2bf[:, :], start=True, stop=True),
                          r=["w2bf", "hcb"], w=["ps6"])
                    sc.op("dve", lambda: nc.vector.tensor_copy(out=vcc[0:127, g, 0:128], in_=C.ps[6][0:127, 0:128]), r=["ps6"], w=[("vcc", g, "v")])
                yield

    def gates_task():
        wg, wgk = load_w(C, win[:, 3584:3608], NCH, 24)
        for tt in range(NT):
            pb = tt % 2
            for k in range(NCH):
                sc.op("pe", lambda: nc.tensor.matmul(C.ps[pb][:, 0:24], lhsT=C.hT[:, k, tt * 128:(tt + 1) * 128], rhs=wg[:, k, :],
                                                      start=(k == 0), stop=(k == NCH - 1)), r=[wgk, hk(k, tt // 4)], w=["ps%d" % pb])
            sc.op("dve", lambda: nc.vector.tensor_tensor(out=g_tok[:, tt, :], in0=C.ps[pb][:, 0:24], in1=gb[:], op=ALU.add),
                  r=["ps%d" % pb, "gb"], w=[("gtok", tt)])
            sc.op("act", lambda: nc.scalar.activation(out=g_tok[:, tt, :], in_=g_tok[:, tt, :], func=AF.Sigmoid), r=[("gtok", tt)], w=[("gtok", tt)])
            if tt % 2 == 1:
                yield

    sbi = [0]
    pti = [0]
    for typ in (DIAG, W4):
        for r in range(4):
            sc.op("dve", lambda: nc.vector.tensor_copy(out=maskb4[:, typ, r * 128:(r + 1) * 128], in_=C.maskb[:, typ, :]),
                  r=["maskb"], w=["maskb4"])

    def s_block(lhsT, lkeys, qblk, qk, np_, masks):
        sbank = sbi[0] % 2
        sbi[0] += 1
        sk = "ps%d" % sbank
        sc.op("pe", lambda: nc.tensor.matmul(C.ps[sbank][0:np_, :], lhsT=lhsT, rhs=qblk, start=True, stop=(len(masks) == 0)),
              r=lkeys + qk, w=[sk])
        for mi, (ml, mr, mk) in enumerate(masks):
            sc.op("pe", lambda: nc.tensor.matmul(C.ps[sbank][0:np_, :], lhsT=ml, rhs=mr, start=False, stop=(mi == len(masks) - 1)),
                  r=mk, w=[sk])
        pi = pti[0] % 4
        pti[0] += 1
        sc.op("act", lambda: nc.scalar.activation(out=PT[0:np_, pi, :], in_=C.ps[sbank][0:np_, :], func=AF.Exp, scale=SCALE),
              r=[sk], w=[("PT", pi)])
        return PT[:, pi, :], ("PT", pi)

    def qblock_task(g, i, tk, qkeys_x, kskeys, kwkeys):
        sm = sms[tk]
        o_tok = o_toks[tk]
        selbT4 = selbT4s[tk]
        K_ = lambda nm: (nm, tk)
        rdc, rdx, gco = sm[:, 0:4], sm[:, 4:8], sm[:, 8:12]
        imp, scb = sm[:, 16:48], sm[:, 48:80]
        mx8a, mx8b = sm[:, 80:88], sm[:, 88:96]
        selb = sm[:, 96:128]
        qblk = qT4[:, i * 512:(i + 1) * 512]
        qk = [("qT4", r, i // 4) for r in range(4)] + [("qT4lo", r, i // 4) for r in range(4)]
        gcol = lambda br: g_tok[:, i, g * 12 + br:g * 12 + br + 10:3]
        sbank = sbi[0] % 2
        sbi[0] += 1
        sk = "ps%d" % sbank
        for r in range(4):
            sc.op("pe", lambda: nc.tensor.matmul(C.ps[sbank][0:127, r * 128:(r + 1) * 128], lhsT=kccT[:, g, 0:127],
                                                  rhs=qblk[:, r * 128:(r + 1) * 128], start=True, stop=False),
                  r=[("kccT", g)] + qk, w=[sk])
            sc.op("pe", lambda: nc.tensor.matmul(C.ps[sbank][0:127, r * 128:(r + 1) * 128], lhsT=C.identb[0:127, 0:127],
                                                  rhs=cmaskb[0:127, i, :], start=False, stop=True),
                  r=["identb", "cmaskb"], w=[sk])
        pi = pti[0] % 4
        pti[0] += 1
        sc.op("act", lambda: nc.scalar.activation(out=PT[0:127, pi, :], in_=C.ps[sbank][0:127, :], func=AF.Exp, scale=SCALE),
              r=[sk], w=[("PT", pi)])
        pt, ptk = PT[:, pi, :], ("PT", pi)
        yield
        for r in range(4):
            bank = 2 + r // 2
            c0 = (r % 2) * 161
            sc.op("pe", lambda: nc.tensor.matmul(C.ps[bank][:, c0:c0 + 161], lhsT=pt[0:127, r * 128:(r + 1) * 128],
                                                  rhs=vcc[0:127, g, 0:161], start=True, stop=True),
                  r=[ptk, ("vcc", g, "v"), ("vcc", g, "cov"), ("vcc", g, "one")], w=["ps%d" % bank])
        for h2 in range(2):
            sc.op("dve", lambda: nc.vector.tensor_scalar(out=rdc[:, 2 * h2:2 * h2 + 2], in0=C.ps[2 + h2][:, 128:128 + 162:161],
                                                          scalar1=TINY, scalar2=None, op0=ALU.max), r=["ps%d" % (2 + h2)], w=[K_("rdc")])
        sc.op("dve", lambda: nc.vector.reciprocal(out=rdc, in_=rdc), r=[K_("rdc")], w=[K_("rdc")])
        for r in range(4):
            bank = 2 + r // 2
            c0 = (r % 2) * 161 + 129
            if r == 0:
                sc.op("dve", lambda: nc.vector.tensor_scalar(out=imp, in0=C.ps[bank][:, c0:c0 + 32], scalar1=rdc[:, 0:1], scalar2=None, op0=ALU.mult),
                      r=["ps%d" % bank, K_("rdc")], w=[K_("imp")])
            else:
                sc.op("dve", lambda: nc.vector.scalar_tensor_tensor(out=imp, in0=C.ps[bank][:, c0:c0 + 32], scalar=rdc[:, r:r + 1], in1=imp,
                                                                     op0=ALU.mult, op1=ALU.add), r=["ps%d" % bank, K_("rdc"), K_("imp")], w=[K_("imp")])
        sc.op("dve", lambda: nc.vector.tensor_tensor(out=scb, in0=imp, in1=selk[:, i, :], op=ALU.mult), r=[K_("imp"), "selk"], w=[K_("scb")])
        sc.op("dve", lambda: nc.vector.tensor_tensor(out=scb, in0=scb, in1=sela[:, i, :], op=ALU.add), r=[K_("scb"), "sela"], w=[K_("scb")])
        sc.op("dve", lambda: nc.vector.max(out=mx8a, in_=scb), r=[K_("scb")], w=[K_("mx8a")])
        sc.op("dve", lambda: nc.vector.match_replace(out=imp, in_to_replace=mx8a, in_values=scb, imm_value=-2.0), r=[K_("scb"), K_("mx8a")], w=[K_("imp")])
        sc.op("dve", lambda: nc.vector.max(out=mx8b, in_=imp), r=[K_("imp")], w=[K_("mx8b")])
        sc.op("dve", lambda: nc.vector.tensor_scalar(out=selb, in0=scb, scalar1=mx8b[:, 7:8], scalar2=NEG, op0=ALU.is_lt, op1=ALU.mult),
              r=[K_("scb"), K_("mx8b")], w=[K_("selb")])
        sc.op("dve", lambda: nc.vector.tensor_tensor(out=gco, in0=rdc, in1=gcol(0), op=ALU.mult), r=[K_("rdc"), ("gtok", i)], w=[K_("gco")])
        for r in range(4):
            bank = 2 + r // 2
            c0 = (r % 2) * 161
            sc.op("dve", lambda: nc.vector.tensor_scalar(out=o_tok[:, r, :], in0=C.ps[bank][:, c0:c0 + 128], scalar1=gco[:, r:r + 1], scalar2=None,
                                                          op0=ALU.mult), r=["ps%d" % bank, K_("gco")], w=[K_(("otok", r))])
        yield "PHASE"
        sc.op("pe", lambda: nc.tensor.transpose(out=C.ps[2][0:32, 0:128], in_=selb, identity=C.ident[:]), r=[K_("selb"), "ident"], w=["ps2"])
        src = C.ps[2][0:32, 0:128]
        src4 = bass.AP(src.tensor, src.offset, [list(src.ap[0]), [0, 4], list(src.ap[1])])
        sc.op("act", lambda: nc.scalar.copy(out=selbT4[:, :].rearrange("p (a b) -> p a b", a=4), in_=src4), r=["ps2"], w=[K_("selbT4")])
        yield

        def branch(kbs, kT_, kkeys, vt, vnm, bank0, masks_for, br):
            def pv(kb, pt, ptk):
                for r in range(4):
                    bank = bank0 + r // 2
                    c0 = (r % 2) * 129
                    sc.op("pe", lambda: nc.tensor.matmul(C.ps[bank][:, c0:c0 + 129], lhsT=pt[:, r * 128:(r + 1) * 128], rhs=vt[:, kb, 0:129],
                                                          start=(kb == kbs[0] and r % 2 == 0), stop=(kb == kbs[-1]), skip_group_check=True),
                          r=[ptk, (vnm, kb // 4), "Vones"], w=["ps%d" % bank])
            prev = None
            for kb in list(kbs) + [None]:
                cur = None
                if kb is not None:
                    pt, ptk = s_block(kT_[:, kb * 128:(kb + 1) * 128], kkeys, qblk, qk, 128, masks_for(kb))
                    cur = (kb, pt, ptk)
                if prev is not None:
                    pv(*prev)
                prev = cur
                yield
            for h2 in range(2):
                sc.op("dve", lambda: nc.vector.tensor_scalar(out=rdx[:, 2 * h2:2 * h2 + 2], in0=C.ps[bank0 + h2][:, 128:128 + 130:129],
                                                              scalar1=TINY, scalar2=None, op0=ALU.max), r=["ps%d" % (bank0 + h2)], w=[K_("rdx")])
            sc.op("dve", lambda: nc.vector.reciprocal(out=rdx, in_=rdx), r=[K_("rdx")], w=[K_("rdx")])
            sc.op("dve", lambda: nc.vector.tensor_tensor(out=gco, in0=rdx, in1=gcol(br), op=ALU.mult), r=[K_("rdx"), ("gtok", i)], w=[K_("gco")])
            for r in range(4):
                bank = bank0 + r // 2
                c0 = (r % 2) * 129
                sc.op("dve", lambda: nc.vector.scalar_tensor_tensor(out=o_tok[:, r, :], in0=C.ps[bank][:, c0:c0 + 128], scalar=gco[:, r:r + 1],
                                                                     in1=o_tok[:, r, :], op0=ALU.mult, op1=ALU.add),
                      r=["ps%d" % bank, K_("gco"), K_(("otok", r))], w=[K_(("otok", r))])
            yield

        dmask = (C.identb[:, :], maskb4[:, DIAG, :], ["identb", "maskb4"])
        w4mask = (C.identb[:, :], maskb4[:, W4, :], ["identb", "maskb4"])
        yield from branch(list(range(max(0, i - 4), i + 1)), kwT, kwkeys, Vw, "Vw", 6,
                          lambda kb: ([dmask] if kb == i else []) + ([w4mask] if kb == i - 4 else []), 2)
        yield "PHASE"
        yield from branch(list(range(0, i + 1)), ksT, kskeys, Vs, "Vs", 4,
                          lambda kb: [(eexpb[0:32, kb * 128:(kb + 1) * 128], selbT4[:, :], ["eexpb", K_("selbT4")])] + ([dmask] if kb == i else []), 1)
        yield "PHASE"
        for r in range(4):
            sc.op("pe", lambda: nc.tensor.transpose(out=C.ps[3][:, r * 128:(r + 1) * 128], in_=o_tok[:, r, :], identity=C.ident[:]),
                  r=[K_(("otok", r)), "ident"], w=["ps3"])
        sc.op("dve", lambda: nc.vector.tensor_tensor(out=yT8[:, g * 4:g * 4 + 4, i * 128:(i + 1) * 128],
                                                      in0=C.ps[3][:].rearrange("p (a d) -> p a d", a=4),
                                                      in1=big[:, 0:4, i * 128:(i + 1) * 128], op=ALU.mult),
              r=["ps3"] + [("sz%d" % r, i // 4) for r in range(4)], w=[("yT8", g, i)])
        yield

    def phase2_task(g):
        qT4v = qT4.rearrange("p (b r i) -> p b r i", r=4, i=128)
        pending = [None]

        def rope_chunk(wv, wk, q, views, key):
            pb = q % 2
            proj_T(C, wv, wk, 0, 128, q, pb)
            nxt = rope_part_a(C, pb, views, q, key, qraw, rt)
            if pending[0] is not None:
                pending[0]()
            pending[0] = nxt

        for r in range(4):
            wv, wk = load_w(C, win[:, (g * 4 + r) * 128:(g * 4 + r + 1) * 128], NCH, 128)
            for q in range(NQ):
                rope_chunk(wv, wk, q, (lambda rows, src, q=q, r=r: (qT4v[rows, 4 * q:4 * q + 4, r, :], src.rearrange("p (b i) -> p b i", b=4))),
                           ("qT4", r, q))
                yield
        for col, dst2d, nm in ((kvcol(2, g), ksT, "ksT"), (kvcol(4, g), kwT, "kwT")):
            wv, wk = load_w(C, win[:, col:col + 128], NCH, 128)
            for q in range(NQ):
                rope_chunk(wv, wk, q, (lambda rows, src, q=q, dst2d=dst2d: (dst2d[rows, q * 512:(q + 1) * 512], src)), (nm, q))
                yield
        pending[0]()
        pending[0] = None
        for slot_i, vt, nm in ((3, Vs, "Vs"), (5, Vw, "Vw")):
            wv, wk = load_w(C, win[:, kvcol(slot_i, g):kvcol(slot_i, g) + 128], NCH, 128)
            for b4 in range(4):
                pb = b4 % 2
                for bi in range(4):
                    b = b4 * 4 + bi
                    for k in range(NCH):
                        sc.op("pe", lambda: nc.tensor.matmul(C.ps[pb][:, bi * 128:(bi + 1) * 128], lhsT=C.hT[:, k, b * 128:(b + 1) * 128],
                                                              rhs=wv[:, k, :], start=(k == 0), stop=(k == NCH - 1)),
                              r=[wk, hk(k, b // 4)], w=["ps%d" % pb])
                sc.op("act", lambda: nc.scalar.copy(out=vt[:, b4 * 4:(b4 + 1) * 4, 0:128], in_=C.ps[pb][:].rearrange("p (a d) -> p a d", a=4)),
                      r=["ps%d" % pb, "Vones"], w=[(nm, b4)])
                yield

    def chain(*gens):
        for g_ in gens:
            yield from g_

    for g in range(2):
        if g == 0:
            run_interleaved([compress_task(), chain(gates_task(), phase2_task(0))])
            sc.barrier()
        else:
            run_interleaved([phase2_task(1)])
        for r in range(4):
            proj_plain(2560 + (g * 4 + r) * 128, big[:, r, :], "sz%d" % r, func=AF.Silu)
        kskeys = [("ksT", q) for q in range(NQ)] + [("ksTlo", q) for q in range(NQ)]
        kwkeys = [("kwT", q) for q in range(NQ)] + [("kwTlo", q) for q in range(NQ)]

        gens = [qblock_task(g, i, i % 4, None, kskeys, kwkeys) for i in range(NT)]
        for step in range(NT + 3):
            act = [gens[step - k] for k in range(4) if 0 <= step - k < NT]
            while act:
                for t in list(act):
                    try:
                        if next(t) == "PHASE":
                            act.remove(t)
                    except StopIteration:
                        act.remove(t)
        sc.barrier()

    for c in range(NCH):
        sc.dma("sp", C.xT[:, c, :], C.xspill[:, c * S:(c + 1) * S], w=[("xT", c, tt) for tt in range(NT)])
    wout = W["b_w_out"][j]
    for oc in range(8):
        wv, wk = load_w(C, wout[:, oc * 128:(oc + 1) * 128], 8, 128)
        for o2 in range(1):
            for q in range(NQ):
                pb = (oc * NQ + q) % 2
                for k in range(8):
                    sc.op("pe", lambda: nc.tensor.matmul(C.ps[pb][:], lhsT=wv[:, k, o2 * 128:(o2 + 1) * 128],
                                                          rhs=yT8[:, k, q * 512:(q + 1) * 512], start=(k == 0), stop=(k == 7)),
                          r=[wk], w=["ps%d" % pb])
                residual_add(C, oc, q, pb)


_NC_CACHE = {}


def make_consts():
    half = 16
    inv = 500000.0 ** (-2.0 * np.arange(half, dtype=np.float64) / 32.0)
    ang = np.arange(S, dtype=np.float64)[None, :] * inv[:, None]
    ang = (np.arange(S, dtype=np.float32)[None, :] * inv.astype(np.float32)[:, None]).astype(np.float64)
    cos = np.concatenate([np.cos(ang), np.cos(ang)], 0)
    sin = np.concatenate([-np.sin(ang), np.sin(ang)], 0)
    rope = np.stack([cos, sin]).astype(np.float32)
    k = np.arange(128)[:, None]
    q = np.arange(128)[None, :]
    mask = np.zeros((4, 128, 128), np.float32)
    mask[0] = np.where(q >= k, 0.0, NEG)
    mask[1] = np.where(q <= k, 0.0, NEG)
    mask[2] = np.where(k > q, 0.0, NEG)
    mask[3] = NEG
    prot = np.zeros((32, 32), np.float32)
    for m in range(32):
        prot[(m + 16) % 32, m] = 1.0
    jj = np.arange(128)[:, None]
    ql = np.arange(128)[None, :]
    cmask = np.zeros((16, 128, 128), np.float32)
    for i in range(16):
        cmask[i] = np.where((16 * jj + 31 <= 128 * i + ql) & (jj < 127), 0.0, NEG)
    eexp = (np.arange(S)[None, :] // 64 == np.arange(32)[:, None]).astype(np.float32)
    n = np.arange(32)[None, :]
    cover = ((jj * 16 < (n + 1) * 64) & (jj * 16 + 32 > n * 64) & (jj < 127)).astype(np.float32)
    selk = np.zeros((16, 128, 32), np.float32)
    sela = np.zeros((16, 128, 32), np.float32)
    for i in range(16):
        cur = (128 * i + np.arange(128)[:, None]) // 64
        forced = (n == 0) | (n == cur) | (n == cur - 1)
        valid = n <= cur
        selk[i] = np.where(valid & ~forced, 1.0, 0.0)
        sela[i] = np.where(valid, np.where(forced, 1000.0, 0.0), -1.0)
    return {"c_rope": rope, "c_mask": mask, "c_prot": prot, "c_cmask": cmask, "c_eexp": eexp,
            "c_cover": cover, "c_selk": selk, "c_sela": sela}


def kernel(**inputs):
    layers = inputs.pop("_layers", (0, 1, 2, 3))
    x = np.ascontiguousarray(inputs["x"], dtype=np.float32)
    key = tuple(layers)
    if key not in _NC_CACHE:
        _NC_CACHE[key] = build(layers)
    nc = _NC_CACHE[key]
    ident = np.eye(128, dtype=np.float32)
    consts = make_consts()
    in_maps = []
    for i in range(N_CORES):
        m = {"x": x[i * SEQ_PER_CORE:(i + 1) * SEQ_PER_CORE],
             "norm_g": np.ascontiguousarray(inputs["norm_g"], dtype=np.float32),
             "final_g": np.ascontiguousarray(inputs["final_g"], dtype=np.float32),
             "ident": ident}
        m.update(consts)
        for name in WSHAPES:
            m[name] = np.ascontiguousarray(inputs[name], dtype=np.float32)
        in_maps.append(m)
    res = run_bass_kernel_spmd(nc, in_maps, core_ids=list(range(N_CORES)))
    return np.concatenate([r["out"] for r in res.results], axis=0)
```

```python
import numpy as np
import concourse.bass as bass
import concourse.mybir as mybir
from concourse.bass_utils import run_bass_kernel_spmd

F32 = mybir.dt.float32
BF16 = mybir.dt.bfloat16
U32 = mybir.dt.uint32
AF = mybir.ActivationFunctionType
ALU = mybir.AluOpType
AX = mybir.AxisListType

D = 1024
S = 2048
NCH = 8
NT = 16
NQ = 4
EPS = 1e-6
N_CORES = 8
SEQ_PER_CORE = 2


class Sched:
    NDSEM = 24

    def __init__(self, nc, stack):
        self.nc = nc
        self.engs = {"pe": nc.tensor, "act": nc.scalar, "dve": nc.vector,
                     "pool": nc.gpsimd, "sp": nc.sync}
        self.sem = {}
        for e in self.engs:
            self.sem[e] = stack.enter_context(nc.semaphore("sem_" + e))
        self.dsem = [stack.enter_context(nc.semaphore("dsem%d" % i)) for i in range(self.NDSEM)]
        self.dcount = [0] * self.NDSEM
        self.dnext = 0
        self.count = {e: 0 for e in self.engs}
        self.waited = {}
        self.last_w = {}
        self.readers = {}
        self.ninst = 0

    def _wait(self, eng, tok):
        kind, a, v = tok
        if kind == "e" and a == eng and eng == "pe":
            return
        key = (eng, kind, a)
        if self.waited.get(key, 0) >= v:
            return
        self.waited[key] = v
        sem = self.sem[a] if kind == "e" else self.dsem[a]
        self.engs[eng].wait_ge(sem, v)

    def _deps(self, eng, r, w):
        toks = set()
        for k in r:
            t = self.last_w.get(k)
            if t is not None:
                toks.add(t)
        for k in w:
            t = self.last_w.get(k)
            if t is not None:
                toks.add(t)
            for t2 in self.readers.get(k, {}).values():
                toks.add(t2)
        for t in sorted(toks):
            self._wait(eng, t)

    def _record(self, tok, r, w):
        for k in w:
            self.last_w[k] = tok
            self.readers[k] = {}
        for k in r:
            d = self.readers.setdefault(k, {})
            kk = (tok[0], tok[1])
            if kk not in d or d[kk][2] < tok[2]:
                d[kk] = tok

    def op(self, eng, fn, r=(), w=()):
        pr = [k for k in r if isinstance(k, str) and k.startswith("ps")]
        if pr:
            r = [k for k in r if k not in pr]
            w = list(w) + pr
        self._deps(eng, r, w)
        inst = fn()
        self.count[eng] += 1
        inst.then_inc(self.sem[eng], 1)
        tok = ("e", eng, self.count[eng])
        self._record(tok, r, w)
        self.ninst += 1
        return tok

    def dma(self, q, out, in_, r=(), w=()):
        self._deps(q, r, w)
        k = self.dnext
        self.dnext = (self.dnext + 1) % self.NDSEM
        if self.dcount[k] > 0:
            self._wait(q, ("d", k, 16 * self.dcount[k]))
        self.dcount[k] += 1
        self.engs[q].dma_start(out=out, in_=in_).then_inc(self.dsem[k], 16)
        tok = ("d", k, 16 * self.dcount[k])
        self._record(tok, r, w)
        self.ninst += 1
        return tok

    def barrier(self):
        toks = [("e", e, c) for e, c in self.count.items() if c > 0]
        toks += [("d", k, 16 * c) for k, c in enumerate(self.dcount) if c > 0]
        for e in self.engs:
            for t in toks:
                if t[0] == "e" and t[1] == e:
                    continue
                self._wait(e, t)
        self.last_w = {}
        self.readers = {}

    def wait_all_on(self, eng):
        toks = [("e", e, c) for e, c in self.count.items() if c > 0 and e != eng]
        toks += [("d", k, 16 * c) for k, c in enumerate(self.dcount) if c > 0]
        for t in toks:
            self._wait(eng, t)


class Ctx:
    pass


def build(layers=(0, 1, 2, 3), nseq=SEQ_PER_CORE):
    from contextlib import ExitStack
    nc = bass.Bass("TRN2", target_bir_lowering=False)
    C = Ctx()
    C.nc = nc
    dt = lambda n, s, d=F32, kind="ExternalInput": nc.dram_tensor(n, list(s), d, kind=kind).ap()
    C.x = dt("x", [nseq, S, D])
    C.out = dt("out", [nseq, S, D], kind="ExternalOutput")
    C.norm_g = dt("norm_g", [4, D])
    C.final_g = dt("final_g", [D])
    C.ident_in = dt("ident", [128, 128])
    C.c_rope = dt("c_rope", [2, 32, S])
    C.c_mask = dt("c_mask", [4, 128, 128])
    C.c_prot = dt("c_prot", [32, 32])
    C.c_cmask = dt("c_cmask", [16, 128, 128])
    C.c_eexp = dt("c_eexp", [32, S])
    C.c_cover = dt("c_cover", [128, 32])
    C.c_selk = dt("c_selk", [16, 128, 32])
    C.c_sela = dt("c_sela", [16, 128, 32])
    C.xspill = nc.dram_tensor("xspill", [128, NCH * S], F32, kind="Internal").ap()
    C.W = {}
    for name, shp in WSHAPES.items():
        C.W[name] = dt(name, shp)

    with ExitStack() as st:
        sc = Sched(nc, st)
        C.sc = sc
        sb = lambda n, s, d=F32: st.enter_context(nc.sbuf_tensor(n, list(s), d))
        C.xT = sb("xT", [128, NCH, S])
        C.hT = sb("hT", [128, NCH, S], BF16)
        C.ident = sb("identf", [128, 128])
        C.identb = sb("identb", [128, 128], BF16)
        C.onesb = sb("onesb", [128, 128], BF16)
        C.gall = sb("gall", [128, 5, NCH])
        C.gtmp = sb("gtmp", [40, 128])
        C.wst = [sb("wst%d" % i, [128, WSLOT]) for i in range(3)]
        C.wbf = [sb("wbf%d" % i, [128, WSLOT], BF16) for i in range(6)]
        C.wst_i = 0
        C.wbf_i = 0
        C.ps = [st.enter_context(nc.psum_tensor("ps%d" % i, [128, 512], F32)) for i in range(8)]

        sc.dma("sp", C.ident[:], C.ident_in, w=["ident"])
        sc.op("dve", lambda: nc.vector.tensor_copy(out=C.identb[:], in_=C.ident[:]), r=["ident"], w=["identb"])
        sc.op("dve", lambda: nc.vector.memset(C.onesb[:], 1.0 / D), w=["onesb"])
        sc.dma("sp", C.gtmp[0:32, :], C.norm_g.rearrange("l (c p) -> (l c) p", p=128), w=["gtmp"])
        sc.dma("sp", C.gtmp[32:40, :], C.final_g.rearrange("(c p) -> c p", p=128), w=["gtmp2"])
        sc.op("pe", lambda: nc.tensor.transpose(out=C.ps[0][:, 0:40], in_=C.gtmp[0:40, :], identity=C.ident[0:40, 0:40]),
              r=["gtmp", "gtmp2", "ident"], w=["ps0"])
        sc.op("dve", lambda: nc.vector.tensor_copy(out=C.gall[:].rearrange("p l c -> p (l c)"), in_=C.ps[0][:, 0:40]),
              r=["ps0"], w=["gall"])

        C.ones1b = sb("ones1b", [128, 128], BF16)
        C.maskb = sb("maskb", [128, 4, 128], BF16)
        C.protb = sb("protb", [32, 32], BF16)
        C.ropeb = sb("ropeb", [32, 2, S], BF16)
        sc.op("dve", lambda: nc.vector.memset(C.ones1b[:], 1.0), w=["ones1b"])
        with ExitStack() as st2:
            tmpc = st2.enter_context(nc.sbuf_tensor("tmpc", [128, 2 * S], F32))
            sc.dma("sp", tmpc[:, 0:512].rearrange("p (a b) -> p a b", a=4), C.c_mask.rearrange("a p b -> p a b"), w=["tmpc"])
            sc.op("dve", lambda: nc.vector.tensor_copy(out=C.maskb[:].rearrange("p a b -> p (a b)"), in_=tmpc[:, 0:512]), r=["tmpc"], w=["maskb"])
            sc.dma("sp", tmpc[0:32, 0:32], C.c_prot, r=["maskb"], w=["tmpc"])
            sc.op("dve", lambda: nc.vector.tensor_copy(out=C.protb[:], in_=tmpc[0:32, 0:32]), r=["tmpc"], w=["protb"])
            sc.dma("sp", tmpc[0:32, :].rearrange("p (a b) -> p a b", a=2), C.c_rope.rearrange("a p b -> p a b"), r=["protb"], w=["tmpc"])
            sc.op("dve", lambda: nc.vector.tensor_copy(out=C.ropeb[:].rearrange("p a b -> p (a b)"), in_=tmpc[0:32, :]), r=["tmpc"], w=["ropeb"])
            sc.barrier()
        C.rope_i = 0
        C.sb_i = 0
        C.pt_i = 0

        for s in range(nseq):
            sc.barrier()
            with ExitStack() as st2:
                C.io = st2.enter_context(nc.sbuf_tensor("io_l%d" % s, [128, 2, D], F32))
                load_x(C, s)
                sc.barrier()
            for li in layers:
                with ExitStack() as st2:
                    C.st2 = st2
                    C.uid = "s%dl%d" % (s, li)
                    if li % 3 == 1:
                        for c in range(NCH):
                            sc.dma("sp", C.xspill[:, c * S:(c + 1) * S], C.xT[:, c, :], r=[("xT", c, tt) for tt in range(NT)])
                    make_hT(C, li)
                    if li % 3 == 2:
                        C.cast_eng = "dve"
                        layer_c(C, li // 3)
                        C.cast_eng = "pool"
                    elif li % 3 == 0:
                        layer_a(C, li // 3)
                    else:
                        layer_b(C, li // 3)
                    sc.barrier()
            with ExitStack() as st2:
                C.io = st2.enter_context(nc.sbuf_tensor("io_s%d" % s, [128, 2, D], F32))
                C.rstd = st2.enter_context(nc.sbuf_tensor("rstd_f%d" % s, [128, 512], F32))
                C.sq = st2.enter_context(nc.sbuf_tensor("sq_f%d" % s, [128, 2, 512], BF16))
                final_norm_store(C, s)
                sc.barrier()
        sc.wait_all_on("sp")
    return nc


WSLOT = 1024


WSHAPES = {
    "a_w_in": [2, 1024, 10240], "a_w_out": [2, 1024, 1024],
    "b_w_in": [1, 1024, 3608], "b_gate_b": [1, 24], "b_pe_k": [1, 32, 128], "b_w1_k": [1, 32, 128, 128],
    "b_w2_k": [1, 128, 128], "b_pe_v": [1, 32, 128], "b_w1_v": [1, 32, 128, 128], "b_w2_v": [1, 128, 128],
    "b_w_out": [1, 1024, 1024],
    "c_w_in": [1, 1024, 2560], "c_conv_w": [1, 4, 1280], "c_conv_b": [1, 1280], "c_wa": [1, 10, 128, 128],
    "c_ba": [1, 1280], "c_wx": [1, 10, 128, 128], "c_bx": [1, 1280], "c_lambda": [1, 1280], "c_w_out": [1, 1280, 1024],
}


def load_w(C, src, K, W):
    nc, sc = C.nc, C.sc
    assert K * W <= WSLOT
    si = C.wst_i % len(C.wst)
    C.wst_i += 1
    bi = C.wbf_i % len(C.wbf)
    C.wbf_i += 1
    stv = C.wst[si][:, 0:K * W].rearrange("p (k w) -> p k w", w=W)
    bfv = C.wbf[bi][:, 0:K * W].rearrange("p (k w) -> p k w", w=W)
    sc.dma("sp", stv, src.rearrange("(k p) w -> p k w", p=128), w=[("wst", si)])
    if getattr(C, "cast_eng", "pool") == "dve":
        sc.op("dve", lambda: nc.vector.tensor_copy(out=C.wbf[bi][:, 0:K * W], in_=C.wst[si][:, 0:K * W]),
              r=[("wst", si)], w=[("wbf", bi)])
    else:
        sc.op("pool", lambda: nc.gpsimd.tensor_copy(out=C.wbf[bi][:, 0:K * W], in_=C.wst[si][:, 0:K * W]),
              r=[("wst", si)], w=[("wbf", bi)])
    return bfv, ("wbf", bi)


def hk(c, q):
    return ("hT", c, q)


def make_hT(C, li):
    from contextlib import ExitStack
    nc, sc = C.nc, C.sc
    with ExitStack() as st3:
        C.rstd = st3.enter_context(nc.sbuf_tensor("rstd" + C.uid, [128, 512], F32))
        C.sq = st3.enter_context(nc.sbuf_tensor("sq" + C.uid, [128, 2, 512], BF16))
        _make_hT(C, li)
        sc.barrier()


def _make_hT(C, li):
    nc, sc = C.nc, C.sc
    for q in range(NQ):
        rms_rstd(C, q)
        for c in range(NCH):
            sc.op("dve", lambda: nc.vector.scalar_tensor_tensor(
                out=C.hT[:, c, q * 512:(q + 1) * 512], in0=C.xT[:, c, q * 512:(q + 1) * 512],
                scalar=C.gall[:, li, c:c + 1], in1=C.rstd[:], op0=ALU.mult, op1=ALU.mult),
                r=xk(c, q) + ["rstd", "gall"], w=[hk(c, q)])


def proj_T(C, wv, wkey, wcol, M, q, pb, K=NCH, rhs=None, rkeys=None):
    nc, sc = C.nc, C.sc
    for k in range(K):
        sc.op("pe", lambda: nc.tensor.matmul(C.ps[pb][0:M, :], lhsT=wv[:, k, wcol:wcol + M],
                                              rhs=C.hT[:, k, q * 512:(q + 1) * 512],
                                              start=(k == 0), stop=(k == K - 1)),
              r=[wkey, hk(k, q)], w=["ps%d" % pb])


def residual_add(C, oc, q, pb):
    nc, sc = C.nc, C.sc
    sc.op("dve", lambda: nc.vector.tensor_tensor(out=C.xT[:, oc, q * 512:(q + 1) * 512], in0=C.ps[pb][:],
                                                  in1=C.xT[:, oc, q * 512:(q + 1) * 512], op=ALU.add),
          r=["ps%d" % pb] + xk(oc, q), w=xk(oc, q))


def layer_c(C, j):
    nc, sc, st = C.nc, C.sc, C.st2
    u = C.uid
    sb = lambda n, s, d=F32: st.enter_context(nc.sbuf_tensor(n + u, list(s), d))
    NB = 10
    cpt = sb("cpt", [80, 128])
    cpar = sb("cpar", [128, 80])
    spt = sb("spt", [128, 4, NB])
    onec = sb("onec", [128, 1])
    sc.op("dve", lambda: nc.vector.memset(onec[:], 1.0), w=["onec"])
    XB = [sb("cXB%d" % i, [128, 3 + S]) for i in range(2)]
    B = [None] + [sb("cB%d" % i, [128, 3 + S]) for i in range(1, 4)]
    xcb = sb("xcb", [128, S], BF16)
    SZ = [sb("sz%d" % i, [128, S], BF16) for i in range(2)]
    yT = sb("yT", [128, 5, S], BF16)
    W = C.W
    sc.dma("sp", cpt[0:40, :], W["c_conv_w"][j].rearrange("j (n p) -> (j n) p", p=128), w=["cpt0"])
    for i, nm in enumerate(["c_conv_b", "c_ba", "c_bx", "c_lambda"]):
        sc.dma("sp", cpt[40 + 10 * i:50 + 10 * i, :], W[nm][j].rearrange("(n p) -> n p", p=128), w=["cpt%d" % (i + 1)])
    sc.op("pe", lambda: nc.tensor.transpose(out=C.ps[0][:, 0:80], in_=cpt[:, :], identity=C.ident[0:80, 0:80]),
          r=["cpt%d" % i for i in range(5)] + ["ident"], w=["ps0"])
    sc.op("dve", lambda: nc.vector.tensor_copy(out=cpar[:], in_=C.ps[0][:, 0:80]), r=["ps0"], w=["cpar"])
    lam = cpar[:, 70:80]
    sc.op("dve", lambda: nc.vector.tensor_scalar(out=spt[:, 0, :], in0=lam, scalar1=-1.0, scalar2=None, op0=ALU.mult), r=["cpar"], w=["spt0"])
    sc.op("dve", lambda: nc.vector.tensor_tensor(out=spt[:, 0, :], in0=spt[:, 0, :], in1=lam, op=ALU.max), r=["cpar", "spt0"], w=["spt0"])
    sc.op("act", lambda: nc.scalar.activation(out=spt[:, 0, :], in_=spt[:, 0, :], func=AF.Exp, scale=-1.0), r=["spt0"], w=["spt0"])
    sc.op("dve", lambda: nc.vector.tensor_scalar(out=spt[:, 0, :], in0=spt[:, 0, :], scalar1=1.0, scalar2=None, op0=ALU.add), r=["spt0"], w=["spt0"])
    sc.op("act", lambda: nc.scalar.activation(out=spt[:, 0, :], in_=spt[:, 0, :], func=AF.Ln), r=["spt0"], w=["spt0"])
    sc.op("dve", lambda: nc.vector.tensor_scalar(out=spt[:, 1, :], in0=lam, scalar1=-1.0, scalar2=0.0, op0=ALU.mult, op1=ALU.max), r=["cpar"], w=["spt1"])
    sc.op("dve", lambda: nc.vector.tensor_tensor(out=spt[:, 1, :], in0=spt[:, 1, :], in1=spt[:, 0, :], op=ALU.add), r=["spt0", "spt1"], w=["spt1"])
    sc.op("dve", lambda: nc.vector.tensor_scalar(out=spt[:, 2, :], in0=spt[:, 1, :], scalar1=-8.0, scalar2=None, op0=ALU.mult), r=["spt1"], w=["spt2"])
    sc.op("dve", lambda: nc.vector.tensor_scalar(out=spt[:, 3, :], in0=spt[:, 1, :], scalar1=-16.0, scalar2=None, op0=ALU.mult), r=["spt1"], w=["spt3"])
    for i in range(2):
        sc.op("pool", lambda: nc.gpsimd.memset(XB[i][:, 0:3], 0.0), w=[("XBpad", i)])

    win = W["c_w_in"][j]
    wo = W["c_w_out"][j]

    def P(n):
        xb, sz = XB[n % 2], SZ[n % 2]
        wxb, kxb = load_w(C, win[:, n * 128:(n + 1) * 128], NCH, 128)
        wz, kz = load_w(C, win[:, 1280 + n * 128:1280 + (n + 1) * 128], NCH, 128)
        for q in range(NQ):
            pb = q % 2
            proj_T(C, wxb, kxb, 0, 128, q, pb)
            sc.op("act", lambda: nc.scalar.copy(out=xb[:, 3 + q * 512:3 + (q + 1) * 512], in_=C.ps[pb][:]),
                  r=["ps%d" % pb, ("XBpad", n % 2)], w=[("xb", n % 2, q)])
            yield
            pb2 = 2 + q % 2
            proj_T(C, wz, kz, 0, 128, q, pb2)
            sc.op("act", lambda: nc.scalar.activation(out=sz[:, q * 512:(q + 1) * 512], in_=C.ps[pb2][:], func=AF.Silu),
                  r=["ps%d" % pb2], w=[("sz", n % 2, q)])
            yield

    def E(n):
        n5 = n % 5
        xb, sz = XB[n % 2], SZ[n % 2]
        xc, rr, ii, aa = B[1], B[2], B[3], xb
        wa, ka = load_w(C, W["c_wa"][j, n], 1, 128)
        wx, kx = load_w(C, W["c_wx"][j, n], 1, 128)
        cw = lambda jj: cpar[:, jj * 10 + n:jj * 10 + n + 1]

        def gates(q):
            sl = slice(3 + q * 512, 3 + (q + 1) * 512)
            pb = 4 + q % 2
            sc.op("pe", lambda: nc.tensor.matmul(C.ps[pb][:], lhsT=wa[:, 0, :], rhs=xcb[:, q * 512:(q + 1) * 512], start=True, stop=True),
                  r=[ka, ("xcb", q)], w=["ps%d" % pb])
            sc.op("act", lambda: nc.scalar.activation(out=rr[:, sl], in_=C.ps[pb][:], func=AF.Sigmoid,
                                                      bias=cpar[:, 50 + n:51 + n]), r=["ps%d" % pb, "cpar"], w=[("rr", q)])
            pb = 6 + q % 2
            sc.op("pe", lambda: nc.tensor.matmul(C.ps[pb][:], lhsT=wx[:, 0, :], rhs=xcb[:, q * 512:(q + 1) * 512], start=True, stop=True),
                  r=[kx, ("xcb", q)], w=["ps%d" % pb])
            sc.op("act", lambda: nc.scalar.activation(out=ii[:, sl], in_=C.ps[pb][:], func=AF.Sigmoid,
                                                      bias=cpar[:, 60 + n:61 + n]), r=["ps%d" % pb, "cpar"], w=[("ii", q)])

        for q in range(NQ):
            sl = slice(3 + q * 512, 3 + (q + 1) * 512)
            xbk = [("xb", n % 2, q)] + ([("xb", n % 2, q - 1)] if q > 0 else [("XBpad", n % 2)])
            sc.op("dve", lambda: nc.vector.tensor_scalar(out=xc[:, sl], in0=xb[:, sl], scalar1=cw(3), scalar2=cpar[:, 40 + n:41 + n],
                                                          op0=ALU.mult, op1=ALU.add), r=xbk + ["cpar"], w=[("xc", q)])
            for jj in range(3):
                sc.op("dve", lambda: nc.vector.scalar_tensor_tensor(out=xc[:, sl], in0=xb[:, jj + q * 512:jj + (q + 1) * 512], scalar=cw(jj),
                                                                     in1=xc[:, sl], op0=ALU.mult, op1=ALU.add),
                      r=xbk + [("xc", q), "cpar"], w=[("xc", q)])
            sc.op("pool", lambda: nc.gpsimd.tensor_copy(out=xcb[:, q * 512:(q + 1) * 512], in_=xc[:, sl]), r=[("xc", q)], w=[("xcb", q)])
            if q > 0:
                gates(q - 1)
            yield
        gates(NQ - 1)
        yield
        for q in range(0):
            pb = 4 + q % 2
            sc.op("pe", lambda: nc.tensor.matmul(C.ps[pb][:], lhsT=wa[:, 0, :], rhs=xcb[:, q * 512:(q + 1) * 512], start=True, stop=True),
                  r=[ka, ("xcb", q)], w=["ps%d" % pb])
            sc.op("act", lambda: nc.scalar.activation(out=rr[:, sl], in_=C.ps[pb][:], func=AF.Sigmoid,
                                                      bias=cpar[:, 50 + n:51 + n]), r=["ps%d" % pb, "cpar"], w=[("rr", q)])
            pb = 6 + q % 2
            sc.op("pe", lambda: nc.tensor.matmul(C.ps[pb][:], lhsT=wx[:, 0, :], rhs=xcb[:, q * 512:(q + 1) * 512], start=True, stop=True),
                  r=[kx, ("xcb", q)], w=["ps%d" % pb])
            sc.op("act", lambda: nc.scalar.activation(out=ii[:, sl], in_=C.ps[pb][:], func=AF.Sigmoid,
                                                      bias=cpar[:, 60 + n:61 + n]), r=["ps%d" % pb, "cpar"], w=[("ii", q)])
            yield
        for q in range(NQ):
            sl = slice(3 + q * 512, 3 + (q + 1) * 512)
            sc.op("act", lambda: nc.scalar.activation(out=aa[:, sl], in_=rr[:, sl], func=AF.Exp, scale=spt[:, 2, n:n + 1]),
                  r=[("rr", q), "spt2"], w=[("xb", n % 2, q)])
            sc.op("act", lambda: nc.scalar.activation(out=rr[:, sl], in_=rr[:, sl], func=AF.Exp, scale=spt[:, 3, n:n + 1]),
                  r=[("rr", q), "spt3"], w=[("rr", q)])
            sc.op("pool", lambda: nc.gpsimd.tensor_tensor(out=ii[:, sl], in0=ii[:, sl], in1=xc[:, sl], op=ALU.mult),
                  r=[("ii", q), ("xc", q)], w=[("ii", q)])
        yield
        for q in range(NQ):
            sl = slice(3 + q * 512, 3 + (q + 1) * 512)
            sc.op("act", lambda: nc.scalar.activation(out=rr[:, sl], in_=rr[:, sl], func=AF.Sqrt, scale=-1.0, bias=onec[:, 0:1]),
                  r=[("rr", q), "onec"], w=[("rr", q)])
            sc.op("pool", lambda: nc.gpsimd.tensor_tensor(out=ii[:, sl], in0=ii[:, sl], in1=rr[:, sl], op=ALU.mult),
                  r=[("ii", q), ("rr", q)], w=[("ii", q)])
        yield
        for q in range(NQ):
            sl = slice(3 + q * 512, 3 + (q + 1) * 512)
            init = 0.0 if q == 0 else xc[:, 3 + q * 512 - 1:3 + q * 512]
            sc.op("dve", lambda: nc.vector.tensor_tensor_scan(out=xc[:, sl], data0=aa[:, sl], data1=ii[:, sl], initial=init,
                                                               op0=ALU.mult, op1=ALU.add),
                  r=[("xb", n % 2, q), ("ii", q), ("xc", q)] + ([("xc", q - 1)] if q else []), w=[("xc", q)])
            sc.op("pool", lambda: nc.gpsimd.tensor_tensor(out=yT[:, n5, q * 512:(q + 1) * 512], in0=xc[:, sl], in1=sz[:, q * 512:(q + 1) * 512], op=ALU.mult),
                  r=[("xc", q), ("sz", n % 2, q)], w=[("yT", n5, q)])
            yield

    def O(half):
        for oc in range(8):
            wv, wk = load_w(C, wo[half * 640:(half + 1) * 640, oc * 128:(oc + 1) * 128], 5, 128)
            for q in range(NQ):
                pb = (oc * NQ + q) % 2
                for k in range(5):
                    sc.op("pe", lambda: nc.tensor.matmul(C.ps[pb][:], lhsT=wv[:, k, :],
                                                          rhs=yT[:, k, q * 512:(q + 1) * 512], start=(k == 0), stop=(k == 4)),
                          r=[wk, ("yT", k, q)], w=["ps%d" % pb])
                residual_add(C, oc, q, pb)

    run_interleaved([P(0)])
    for n in range(10):
        te = E(n)
        tp = P(n + 1) if n + 1 < 10 else None
        while te is not None or tp is not None:
            for _ in range(3):
                if te is not None:
                    try:
                        next(te)
                    except StopIteration:
                        te = None
            if tp is not None:
                try:
                    next(tp)
                except StopIteration:
                    tp = None
        if n % 5 == 4:
            O(n // 5)


def xk(c, q):
    return [("xT", c, tt) for tt in range(4 * q, 4 * q + 4)]


def load_x(C, s):
    nc, sc = C.nc, C.sc
    for tt in range(NT):
        b = tt % 2
        sc.dma("sp", C.io[:, b, :], C.x[s, tt * 128:(tt + 1) * 128, :], w=[("io", b, cc) for cc in range(NCH)])
        for c in range(NCH):
            pb = (tt * NCH + c) % 4
            sc.op("pe", lambda: nc.tensor.transpose(out=C.ps[pb][:, 0:128], in_=C.io[:, b, c * 128:(c + 1) * 128],
                                                     identity=C.ident[:]),
                  r=[("io", b, c), "ident"], w=["ps%d" % pb])
            eng = "act" if c % 2 else "dve"
            if eng == "act":
                sc.op("act", lambda: nc.scalar.copy(out=C.xT[:, c, tt * 128:(tt + 1) * 128], in_=C.ps[pb][:, 0:128]),
                      r=["ps%d" % pb], w=[("xT", c, tt)])
            else:
                sc.op("dve", lambda: nc.vector.tensor_copy(out=C.xT[:, c, tt * 128:(tt + 1) * 128], in_=C.ps[pb][:, 0:128]),
                      r=["ps%d" % pb], w=[("xT", c, tt)])


def rms_rstd(C, q, dst_keys_done=None):
    nc, sc = C.nc, C.sc
    pb = 4 + (q % 2)
    for c in range(NCH):
        b = c % 2
        sc.op("act", lambda: nc.scalar.activation(out=C.sq[:, b, :], in_=C.xT[:, c, q * 512:(q + 1) * 512], func=AF.Square),
              r=xk(c, q), w=[("sq", b)])
        sc.op("pe", lambda: nc.tensor.matmul(C.ps[pb][:], lhsT=C.onesb[:], rhs=C.sq[:, b, :], start=(c == 0), stop=(c == NCH - 1)),
              r=[("sq", b), "onesb"], w=["ps%d" % pb])
    sc.op("dve", lambda: nc.vector.tensor_scalar(out=C.rstd[:], in0=C.ps[pb][:], scalar1=EPS, scalar2=None, op0=ALU.add),
          r=["ps%d" % pb], w=["rstd"])
    sc.op("act", lambda: nc.scalar.activation(out=C.rstd[:], in_=C.rstd[:], func=AF.Sqrt), r=["rstd"], w=["rstd"])
    sc.op("dve", lambda: nc.vector.reciprocal(out=C.rstd[:], in_=C.rstd[:]), r=["rstd"], w=["rstd"])


def final_norm_store(C, s):
    nc, sc = C.nc, C.sc
    for q in range(NQ):
        rms_rstd(C, q)
        for c in range(NCH):
            sc.op("dve", lambda: nc.vector.scalar_tensor_tensor(
                out=C.xT[:, c, q * 512:(q + 1) * 512], in0=C.xT[:, c, q * 512:(q + 1) * 512],
                scalar=C.gall[:, 4, c:c + 1], in1=C.rstd[:], op0=ALU.mult, op1=ALU.mult),
                r=xk(c, q) + ["rstd", "gall"], w=xk(c, q))
        for t4 in range(4):
            tt = q * 4 + t4
            b = tt % 2
            for c in range(NCH):
                pb = (tt * NCH + c) % 4
                sc.op("pe", lambda: nc.tensor.transpose(out=C.ps[pb][:, 0:128], in_=C.xT[:, c, tt * 128:(tt + 1) * 128],
                                                         identity=C.ident[:]),
                      r=[("xT", c, tt), "ident"], w=["ps%d" % pb])
                if c % 2:
                    sc.op("act", lambda: nc.scalar.copy(out=C.io[:, b, c * 128:(c + 1) * 128], in_=C.ps[pb][:, 0:128]),
                          r=["ps%d" % pb], w=[("io", b, c)])
                else:
                    sc.op("dve", lambda: nc.vector.tensor_copy(out=C.io[:, b, c * 128:(c + 1) * 128], in_=C.ps[pb][:, 0:128]),
                          r=["ps%d" % pb], w=[("io", b, c)])
            sc.dma("sp", C.out[s, tt * 128:(tt + 1) * 128, :], C.io[:, b, :], r=[("io", b, cc) for cc in range(NCH)])


NEG = -30000.0
SCALE = 128.0 ** -0.5
A_GROUPS = ((128, 1), (512, 4), (2048, 16))


def dil_views(dst, src, dil, q):
    if dil == 1:
        return dst[:, q * 512:(q + 1) * 512], src
    n = 512 // dil
    dv = dst.rearrange("p (r m) -> p m r", r=dil)[:, n * q:n * (q + 1), :]
    sv = src.rearrange("p (m r) -> p m r", r=dil)
    return dv, sv


def rope_store(C, pb, views, q, wkey, qraw, rt):
    rope_part_a(C, pb, views, q, wkey, qraw, rt)()


def rope_part_a(C, pb, views, q, wkey, qraw, rt, rbanks=(2, 3)):
    nc, sc = C.nc, C.sc
    slot = C.rope_i % 2
    C.rope_i += 1
    rb = rbanks[slot % len(rbanks)]
    psk = "ps%d" % pb
    dv, sv = views(slice(0, 128), C.ps[pb][:, :])
    lokey = (wkey[0] + "lo",) + tuple(wkey[1:])
    sc.op("act", lambda: nc.scalar.copy(out=qraw[:, slot, :], in_=C.ps[pb][0:32, :]), r=[psk], w=[("qraw", slot)])
    sc.op("act", lambda: nc.scalar.copy(out=dv, in_=sv), r=[psk], w=[wkey, lokey])

    def part_b():
        sc.op("pe", lambda: nc.tensor.matmul(C.ps[rb][0:32, :], lhsT=C.protb[:, :], rhs=qraw[:, slot, :], start=True, stop=True),
              r=[("qraw", slot), "protb"], w=["ps%d" % rb])
        sc.op("dve", lambda: nc.vector.tensor_tensor(out=rt[:, slot, 0, :], in0=C.ps[pb][0:32, :], in1=C.ropeb[:, 0, q * 512:(q + 1) * 512], op=ALU.mult),
              r=[psk, "ropeb"], w=[("rt", slot, 0)])
        sc.op("dve", lambda: nc.vector.tensor_tensor(out=rt[:, slot, 1, :], in0=C.ps[rb][0:32, :], in1=C.ropeb[:, 1, q * 512:(q + 1) * 512], op=ALU.mult),
              r=["ps%d" % rb, "ropeb"], w=[("rt", slot, 1)])
        dv0, sv0 = views(slice(0, 32), rt[:, slot, 0, :])
        _, sv1 = views(slice(0, 32), rt[:, slot, 1, :])
        sc.op("dve", lambda: nc.vector.tensor_tensor(out=dv0, in0=sv0, in1=sv1, op=ALU.add),
              r=[("rt", slot, 0), ("rt", slot, 1)], w=[lokey])
    return part_b


def run_interleaved(tasks):
    active = list(tasks)
    while active:
        for t in list(active):
            try:
                next(t)
            except StopIteration:
                active.remove(t)


def layer_a(C, j):
    nc, sc, st = C.nc, C.sc, C.st2
    u_ = C.uid
    sb = lambda n, s, d=F32: st.enter_context(nc.sbuf_tensor(n + u_, list(s), d))
    qT = [sb("qT%d" % i, [128, S], BF16) for i in range(2)]
    kT = [sb("kT%d" % i, [128, S], BF16) for i in range(2)]
    V = [sb("V%d" % i, [128, 16, 128], BF16) for i in range(2)]
    szT = [sb("szT%d" % i, [128, S], BF16) for i in range(2)]
    num = sb("num", [128, S])
    den = sb("den", [128, S])
    yT = sb("yT", [128, 4, S], BF16)
    qraw = sb("qraw", [32, 2, 512], BF16)
    rt = sb("rt", [32, 2, 2, 512], BF16)
    PT = sb("PT", [128, 4, 512], BF16)
    win = C.W["a_w_in"][j]
    wout = C.W["a_w_out"][j]
    DIAG, PREV = 0, 1

    PJ = (0, 1, 3)
    pj = [0]
    preW = {}

    def wcols(s_):
        hd, g = divmod(s_, 3)
        cols = [9216 + hd * 128] if g == 0 else []
        return cols + [((g * 3 + which) * 8 + hd) * 128 for which in range(3)]

    def getw(s_, i):
        if (s_, i) not in preW:
            col = wcols(s_)[i]
            preW[(s_, i)] = load_w(C, win[:, col:col + 128], NCH, 128)
        return preW.pop((s_, i)) if False else preW[(s_, i)]

    def nextbank():
        pj[0] += 1
        return PJ[pj[0] % 3]

    def P(s):
        hd, g = divmod(s, 3)
        b = s % 2
        wlen, dil = A_GROUPS[g]
        L = S // dil
        nbs = L // 128
        wi = 0
        if g == 0:
            wz, kz = getw(s, wi)
            wi += 1
            for q in range(NQ):
                pb = nextbank()
                proj_T(C, wz, kz, 0, 128, q, pb)
                sc.op("act", lambda: nc.scalar.activation(out=szT[hd % 2][:, q * 512:(q + 1) * 512], in_=C.ps[pb][:], func=AF.Silu),
                      r=["ps%d" % pb], w=[("szT", hd % 2, q)])
                yield
        pending = None
        for which, dst, nm in ((0, qT[b], "qT"), (1, kT[b], "kT")):
            wv, wk = getw(s, wi)
            wi += 1
            for q in range(NQ):
                pb = nextbank()
                proj_T(C, wv, wk, 0, 128, q, pb)
                nxt = rope_part_a(C, pb, (lambda rows, src, dst=dst, q=q: dil_views(dst[rows, :], src, 1, q)), q, (nm, b, q), qraw, rt,
                                  rbanks=(2,))
                if pending is not None:
                    pending()
                pending = nxt
                yield
        wv, wk = getw(s, wi)
        for b4 in range(4):
            pb = nextbank()
            for bi in range(4):
                bb = b4 * 4 + bi
                r_, jb = bb // nbs, bb % nbs
                t0 = r_ + dil * jb * 128
                t1 = t0 + dil * 127
                hks = [hk(k, qq) for k in range(NCH) for qq in range(t0 // 512, t1 // 512 + 1)]
                for k in range(NCH):
                    sc.op("pe", lambda: nc.tensor.matmul(C.ps[pb][:, bi * 128:(bi + 1) * 128],
                                                          lhsT=C.hT[:, k, t0:t1 + 1:dil], rhs=wv[:, k, :],
                                                          start=(k == 0), stop=(k == NCH - 1)),
                          r=[wk] + hks, w=["ps%d" % pb])
                if bi == 1 and pending is not None:
                    pending()
                    pending = None
                if bi % 2 == 1:
                    yield
            sc.op("act", lambda: nc.scalar.copy(out=V[b][:, b4 * 4:(b4 + 1) * 4, :].rearrange("p a d -> p (a d)"), in_=C.ps[pb][:]),
                  r=["ps%d" % pb], w=[("V", b, b4)])
            yield

    def T(s):
        hd, g = divmod(s, 3)
        b = s % 2
        wlen, dil = A_GROUPS[g]
        L = S // dil
        nbs = L // 128
        qk_keys = [(nm, b, q) for nm in ("qT", "qTlo", "kT", "kTlo") for q in range(NQ)]

        def cols(blk):
            r_, jb = blk // nbs, blk % nbs
            t0 = r_ + dil * 128 * jb
            return slice(t0, t0 + dil * 127 + 1, dil)

        if g == 0:
            units = [[4 * c + i for i in range(4)] for c in range(4)]
        elif g == 1:
            units = [[r_ * nbs + jb for r_ in range(4)] for jb in range(nbs)]
        else:
            units = [[r0 + i for i in range(4)] for r0 in range(0, 16, 4)]

        def s_phase(qbs):
            pairs = []
            for qb in qbs:
                if qb % nbs > 0:
                    pairs.append((qb - 1, qb, PREV))
                pairs.append((qb, qb, DIAG))
            chunks = []
            for c0 in range(0, len(pairs), 4):
                sub = pairs[c0:c0 + 4]
                sbank = 4 + C.sb_i % 2
                C.sb_i += 1
                sk = "ps%d" % sbank
                for p, (kb, qb, typ) in enumerate(sub):
                    sc.op("pe", lambda: nc.tensor.matmul(C.ps[sbank][:, p * 128:(p + 1) * 128], lhsT=kT[b][:, cols(kb)],
                                                          rhs=qT[b][:, cols(qb)], start=True, stop=False),
                          r=qk_keys, w=[sk])
                    sc.op("pe", lambda: nc.tensor.matmul(C.ps[sbank][:, p * 128:(p + 1) * 128], lhsT=C.identb[:],
                                                          rhs=C.maskb[:, typ, :], start=False, stop=True),
                          r=["identb", "maskb"], w=[sk])
                npair = len(sub)
                pti = C.pt_i % 4
                C.pt_i += 1
                sc.op("act", lambda: nc.scalar.activation(out=PT[:, pti, 0:npair * 128], in_=C.ps[sbank][:, 0:npair * 128],
                                                          func=AF.Exp, scale=SCALE), r=[sk], w=[("PT", pti)])
                chunks.append((sub, pti))
            return qbs, chunks

        def pv_phase(state):
            qbs, chunks = state
            for sub, pti in chunks:
                for p, (kb, qb, typ) in enumerate(sub):
                    ci = qbs.index(qb)
                    first = (typ == PREV) or (qb % nbs == 0)
                    sc.op("pe", lambda: nc.tensor.matmul(C.ps[6][:, ci * 128:(ci + 1) * 128], lhsT=V[b][:, kb, :],
                                                          rhs=PT[:, pti, p * 128:(p + 1) * 128], start=first, stop=(typ == DIAG)),
                          r=[("V", b, kb // 4), ("PT", pti)], w=["ps6"])
                    sc.op("pe", lambda: nc.tensor.matmul(C.ps[7][:, ci * 128:(ci + 1) * 128], lhsT=C.ones1b[:],
                                                          rhs=PT[:, pti, p * 128:(p + 1) * 128], start=first, stop=(typ == DIAG)),
                          r=["ones1b", ("PT", pti)], w=["ps7"])
            qb0 = qbs[0]
            if g == 0:
                c = qb0 // 4
                views = lambda t, pv: (t[:, c * 512:(c + 1) * 512], pv)
                chunks_k = [c]
            elif g == 1:
                jb = qb0 % nbs
                views = lambda t, pv: (t[:, jb * 512:(jb + 1) * 512].rearrange("p (m r) -> p m r", r=4),
                                       pv.rearrange("p (r m) -> p m r", r=4))
                chunks_k = [jb]
            else:
                views = lambda t, pv: (t.rearrange("p (m r) -> p m r", r=16)[:, :, qb0:qb0 + 4],
                                       pv.rearrange("p (r m) -> p m r", r=4))
                chunks_k = [0, 1, 2, 3]
            for acc, bank, nm in ((num, 6, "num"), (den, 7, "den")):
                av, pv = views(acc[:, :], C.ps[bank][:, :])
                keys = [(nm, c) for c in chunks_k]
                if g == 0:
                    if nm == "num":
                        sc.op("dve", lambda: nc.vector.tensor_copy(out=av, in_=pv), r=["ps%d" % bank], w=keys)
                    else:
                        sc.op("act", lambda: nc.scalar.copy(out=av, in_=pv), r=["ps%d" % bank], w=keys)
                else:
                    sc.op("dve", lambda: nc.vector.tensor_tensor(out=av, in0=pv, in1=av, op=ALU.add),
                          r=["ps%d" % bank] + keys, w=keys)

        pending = None
        for qbs in units + [None]:
            if qbs is not None:
                stt = s_phase(qbs)
                yield
            else:
                stt = None
            if pending is not None:
                pv_phase(pending)
                yield
            pending = stt

    def F(hd):
        for q in range(NQ):
            sl = slice(q * 512, (q + 1) * 512)
            sc.op("act", lambda: nc.scalar.activation(out=den[:, sl], in_=den[:, sl], func=AF.Ln), r=[("den", q)], w=[("den", q)])
            sc.op("act", lambda: nc.scalar.activation(out=den[:, sl], in_=den[:, sl], func=AF.Exp, scale=-1.0), r=[("den", q)], w=[("den", q)])
            sc.op("pool", lambda: nc.gpsimd.tensor_tensor(out=den[:, sl], in0=den[:, sl], in1=szT[hd % 2][:, sl], op=ALU.mult),
                  r=[("den", q), ("szT", hd % 2, q)], w=[("den", q)])
            sc.op("dve", lambda: nc.vector.tensor_tensor(out=yT[:, hd % 4, sl], in0=num[:, sl], in1=den[:, sl], op=ALU.mult),
                  r=[("num", q), ("den", q)], w=[("yT", hd % 4, q)])

    preO = {}

    def getO(half, og):
        if (half, og) not in preO:
            preO[(half, og)] = load_w(C, wout[half * 512:(half + 1) * 512, og * 256:(og + 1) * 256], 4, 256)
        return preO[(half, og)]

    def O(half):
        for og in range(4):
            wv, wk = getO(half, og)
            for o2 in range(2):
                oc = og * 2 + o2
                for q in range(NQ):
                    pb = (oc * NQ + q) % 2
                    for k in range(4):
                        sc.op("pe", lambda: nc.tensor.matmul(C.ps[pb][:], lhsT=wv[:, k, o2 * 128:(o2 + 1) * 128],
                                                              rhs=yT[:, k, q * 512:(q + 1) * 512], start=(k == 0), stop=(k == 3)),
                              r=[wk, ("yT", k, q)], w=["ps%d" % pb])
                    residual_add(C, oc, q, pb)

    NS = 24
    run_interleaved([P(0)])
    for s in range(NS):
        tasks = [T(s)]
        if s + 1 < NS:
            tasks.append(P(s + 1))
        run_interleaved(tasks)
        hd, g = divmod(s, 3)
        if s + 2 < NS:
            getw(s + 2, 0)
        if g == 2:
            if hd % 4 == 3:
                getO(hd // 4, 0)
            F(hd)
            if hd % 4 == 3:
                O(hd // 4)


def load_w_pre(C, src3, K, W):
    nc, sc = C.nc, C.sc
    assert K * W <= WSLOT
    si = C.wst_i % len(C.wst)
    C.wst_i += 1
    bi = C.wbf_i % len(C.wbf)
    C.wbf_i += 1
    stv = C.wst[si][:, 0:K * W].rearrange("p (k w) -> p k w", w=W)
    bfv = C.wbf[bi][:, 0:K * W].rearrange("p (k w) -> p k w", w=W)
    sc.dma("sp", stv, src3, w=[("wst", si)])
    sc.op("pool", lambda: nc.gpsimd.tensor_copy(out=C.wbf[bi][:, 0:K * W], in_=C.wst[si][:, 0:K * W]),
          r=[("wst", si)], w=[("wbf", bi)])
    return bfv, ("wbf", bi)


def layer_b(C, j):
    nc, sc, st = C.nc, C.sc, C.st2
    u = C.uid
    sb = lambda n, s, d=F32: st.enter_context(nc.sbuf_tensor(n + u, list(s), d))
    W = C.W
    win = W["b_w_in"][j]
    TINY = 1e-30
    DIAG, W4 = 0, 2
    allx = [("xT", c, tt) for c in range(NCH) for tt in range(NT)]
    sc.barrier()
    xb = C.xT.bitcast(BF16)[:].rearrange("p c t -> p (c t)")
    reg = lambda off, n: xb[:, off:off + n]
    qT4 = reg(0, 8192)
    big = reg(8192, 8192).rearrange("p (a t) -> p a t", a=4)
    ksT = reg(16384, 2048)
    kwT = reg(18432, 2048)
    Vs = reg(20480, 2080).rearrange("p (a d) -> p a d", d=130)
    Vw = reg(22560, 2080).rearrange("p (a d) -> p a d", d=130)
    PT = reg(24640, 2048).rearrange("p (a d) -> p a d", d=512)
    cmaskb = reg(26688, 2048).rearrange("p (a d) -> p a d", d=128)
    eexpb = reg(28736, 2048)
    yT8 = sb("yT8", [128, 8, S], BF16)
    g_tok = sb("gtok", [128, NT, 24])
    gb = sb("gb", [128, 24])
    kccT = sb("kccT", [128, 2, 128], BF16)
    vcc = sb("vcc", [128, 2, 162], BF16)
    selk = sb("selk", [128, NT, 32])
    sela = sb("sela", [128, NT, 32])
    o_toks = [sb("otok%d" % t, [128, 4, 128]) for t in range(4)]
    sms = [sb("sm%d" % t, [128, 160]) for t in range(4)]
    selbT4s = [sb("selbT4%d" % t, [32, 512], BF16) for t in range(4)]
    maskb4 = sb("maskb4", [128, 3, 512], BF16)
    qraw = sb("qraw", [32, 2, 512], BF16)
    rt = sb("rt", [32, 2, 2, 512])
    stg = sb("stg", [128, 2048])
    peT = sb("peT", [128, 2, 32], BF16)
    hcb = sb("hcb", [128, 128], BF16)
    biasc = sb("biasc", [128, 2])

    sc.dma("sp", stg[:, :].rearrange("p (a b) -> p a b", a=16), C.c_cmask.rearrange("a p b -> p a b"), w=["stg"])
    sc.op("dve", lambda: nc.vector.tensor_copy(out=cmaskb.rearrange("p a d -> p (a d)"), in_=stg[:, :]), r=["stg"], w=["cmaskb"])
    sc.dma("sp", stg[0:32, :], C.c_eexp, r=["cmaskb"], w=["stg"])
    sc.op("dve", lambda: nc.vector.tensor_copy(out=eexpb[0:32, :], in_=stg[0:32, :]), r=["stg"], w=["eexpb"])
    sc.dma("sp", stg[:, 0:32], C.c_cover, r=["eexpb"], w=["stg"])
    for g in range(2):
        sc.op("dve", lambda: nc.vector.tensor_copy(out=vcc[:, g, 129:161], in_=stg[:, 0:32]), r=["stg"], w=[("vcc", g, "cov")])
        sc.op("dve", lambda: nc.vector.memset(vcc[:, g, 128:129], 1.0), w=[("vcc", g, "one")])
    sc.dma("sp", selk[:], C.c_selk.rearrange("a p n -> p a n"), w=["selk"])
    sc.dma("sp", sela[:], C.c_sela.rearrange("a p n -> p a n"), w=["sela"])
    sc.dma("sp", gb[:], W["b_gate_b"][j].partition_broadcast(128), w=["gb"])
    for kv, nm in enumerate(["b_pe_k", "b_pe_v"]):
        sc.dma("sp", stg[0:32, 128 * (1 + kv):128 * (2 + kv)], W[nm][j], r=[("vcc", 0, "cov"), ("vcc", 1, "cov")], w=[("stgpe", kv)])
        sc.op("pe", lambda: nc.tensor.transpose(out=C.ps[0][:, 0:32], in_=stg[0:32, 128 * (1 + kv):128 * (2 + kv)], identity=C.ident[0:32, 0:32]),
              r=[("stgpe", kv), "ident"], w=["ps0"])
        sc.op("dve", lambda: nc.vector.tensor_copy(out=peT[:, kv, :], in_=C.ps[0][:, 0:32]), r=["ps0"], w=[("peT", kv)])
    for vt in (Vs, Vw):
        sc.op("pool", lambda: nc.gpsimd.memset(vt[:, :, 128:129], 1.0), w=["Vones"])

    nat = lambda dst: (lambda rows, src, dst=dst: None)

    def proj_rope(col, dst2d, nm):
        wv, wk = load_w(C, win[:, col:col + 128], NCH, 128)
        for q in range(NQ):
            pb = q % 2
            proj_T(C, wv, wk, 0, 128, q, pb)
            rope_store(C, pb, (lambda rows, src, q=q: (dst2d[rows, q * 512:(q + 1) * 512], src)), q, (nm, q), qraw, rt)

    def proj_plain(col, dst2d, nm, func=None):
        wv, wk = load_w(C, win[:, col:col + 128], NCH, 128)
        for q in range(NQ):
            pb = q % 2
            proj_T(C, wv, wk, 0, 128, q, pb)
            if func is None:
                sc.op("act", lambda: nc.scalar.copy(out=dst2d[:, q * 512:(q + 1) * 512], in_=C.ps[pb][:]), r=["ps%d" % pb], w=[(nm, q)])
            else:
                sc.op("act", lambda: nc.scalar.activation(out=dst2d[:, q * 512:(q + 1) * 512], in_=C.ps[pb][:], func=func),
                      r=["ps%d" % pb], w=[(nm, q)])

    kvcol = lambda i, g: 1024 + (i * 2 + g) * 128
    for g in range(2):
        proj_rope(kvcol(0, g), big[:, g * 2 + 0, :], "cmp%d" % (g * 2))
        proj_plain(kvcol(1, g), big[:, g * 2 + 1, :], "cmp%d" % (g * 2 + 1))
    def compress_task():
        cst = stg[:, :].rearrange("p (a b) -> p a b", a=2)
        cbf = reg(24640, 2048).rearrange("p (a b) -> p a b", a=2)
        w2bf = reg(30784, 128)
        li = [0]

        def cload(src3, K):
            sl = li[0] % 2
            li[0] += 1
            sc.dma("sp", cst[:, sl, 0:K * 128].rearrange("p (k w) -> p k w", w=128), src3,
                   r=[("peT", 0), ("peT", 1), ("vcc", 0, "cov"), ("vcc", 1, "cov"), "eexpb", "cmaskb"], w=[("cst", sl)])
            sc.op("pool", lambda: nc.gpsimd.tensor_copy(out=cbf[:, sl, 0:K * 128], in_=cst[:, sl, 0:K * 128]),
                  r=[("cst", sl)], w=[("cbf", sl)])
            return cbf[:, sl, :].rearrange("p (k w) -> p k w", w=128), ("cbf", sl)

        for kv in range(2):
            w1 = W["b_w1_k" if kv == 0 else "b_w1_v"][j]
            w2 = W["b_w2_k" if kv == 0 else "b_w2_v"][j]
            w2v_, w2k_ = cload(w2.rearrange("(k p) w -> p k w", p=128), 1)
            sc.op("pool", lambda: nc.gpsimd.tensor_copy(out=w2bf[:, :], in_=w2v_[:, 0, :]), r=[w2k_], w=["w2bf"])
            srcs = [big[:, g * 2 + kv, :] for g in range(2)]
            skeys = [[("cmp%d" % (g * 2 + kv), q) for q in range(NQ)] +
                     ([("cmp%dlo" % (g * 2 + kv), q) for q in range(NQ)] if kv == 0 else []) for g in range(2)]
            for h in range(4):
                wv, wk = cload(w1[h * 8:(h + 1) * 8].rearrange("p d e -> d p e"), 8)
                for p8 in range(8):
                    p = h * 8 + p8
                    sc.op("pe", lambda: nc.tensor.matmul(C.ps[4][:, 0:1], lhsT=wv[:, p8, :], rhs=peT[:, kv, p:p + 1], start=(p == 0), stop=(p == 31)),
                          r=[wk, ("peT", kv)], w=["ps4"])
                yield
                for g in range(2):
                    bank = 5 if g == 0 else 7
                    for p8 in range(8):
                        p = h * 8 + p8
                        sc.op("pe", lambda: nc.tensor.matmul(C.ps[bank][:, 0:127], lhsT=wv[:, p8, :], rhs=srcs[g][:, p:p + 16 * 126 + 1:16],
                                                              start=(p == 0), stop=(p == 31)), r=[wk] + skeys[g], w=["ps%d" % bank])
                    yield
            sc.op("dve", lambda: nc.vector.tensor_copy(out=biasc[:, kv:kv + 1], in_=C.ps[4][:, 0:1]), r=["ps4"], w=[("biasc", kv)])
            for g in range(2):
                bank = 5 if g == 0 else 7
                sc.op("act", lambda: nc.scalar.activation(out=hcb[:, 0:127], in_=C.ps[bank][:, 0:127], func=AF.Silu, bias=biasc[:, kv:kv + 1]),
                      r=["ps%d" % bank, ("biasc", kv)], w=["hcb"])
                if kv == 0:
                    sc.op("pe", lambda: nc.tensor.matmul(C.ps[6][:, 0:127], lhsT=w2bf[:, :], rhs=hcb[:, 0:127], start=True, stop=True),
                          r=["w2bf", "hcb"], w=["ps6"])
                    sc.op("dve", lambda: nc.vector.tensor_copy(out=kccT[:, g, 0:127], in_=C.ps[6][:, 0:127]), r=["ps6"], w=[("kccT", g)])
                else:
                    sc.op("pe", lambda: nc.tensor.matmul(C.ps[6][0:127, 0:128], lhsT=hcb[:, 0:127], rhs=w2bf[:, :], start=True, stop=True),
                          r=["w2bf", "hcb"], w=["ps6"])
                    sc.op("dve", lambda: nc.vector.tensor_copy(out=vcc[0:127, g, 0:128], in_=C.ps[6][0:127, 0:128]), r=["ps6"], w=[("vcc", g, "v")])
                yield

    def gates_task():
        wg, wgk = load_w(C, win[:, 3584:3608], NCH, 24)
        for tt in range(NT):
            pb = tt % 2
            for k in range(NCH):
                sc.op("pe", lambda: nc.tensor.matmul(C.ps[pb][:, 0:24], lhsT=C.hT[:, k, tt * 128:(tt + 1) * 128], rhs=wg[:, k, :],
                                                      start=(k == 0), stop=(k == NCH - 1)), r=[wgk, hk(k, tt // 4)], w=["ps%d" % pb])
            sc.op("dve", lambda: nc.vector.tensor_tensor(out=g_tok[:, tt, :], in0=C.ps[pb][:, 0:24], in1=gb[:], op=ALU.add),
                  r=["ps%d" % pb, "gb"], w=[("gtok", tt)])
            sc.op("act", lambda: nc.scalar.activation(out=g_tok[:, tt, :], in_=g_tok[:, tt, :], func=AF.Sigmoid), r=[("gtok", tt)], w=[("gtok", tt)])
            if tt % 2 == 1:
                yield

    sbi = [0]
    pti = [0]
    for typ in (DIAG, W4):
        for r in range(4):
            sc.op("dve", lambda: nc.vector.tensor_copy(out=maskb4[:, typ, r * 128:(r + 1) * 128], in_=C.maskb[:, typ, :]),
                  r=["maskb"], w=["maskb4"])

    def s_block(lhsT, lkeys, qblk, qk, np_, masks):
        sbank = sbi[0] % 2
        sbi[0] += 1
        sk = "ps%d" % sbank
        sc.op("pe", lambda: nc.tensor.matmul(C.ps[sbank][0:np_, :], lhsT=lhsT, rhs=qblk, start=True, stop=(len(masks) == 0)),
              r=lkeys + qk, w=[sk])
        for mi, (ml, mr, mk) in enumerate(masks):
            sc.op("pe", lambda: nc.tensor.matmul(C.ps[sbank][0:np_, :], lhsT=ml, rhs=mr, start=False, stop=(mi == len(masks) - 1)),
                  r=mk, w=[sk])
        pi = pti[0] % 4
        pti[0] += 1
        sc.op("act", lambda: nc.scalar.activation(out=PT[0:np_, pi, :], in_=C.ps[sbank][0:np_, :], func=AF.Exp, scale=SCALE),
              r=[sk], w=[("PT", pi)])
        return PT[:, pi, :], ("PT", pi)

    def qblock_task(g, i, tk, qkeys_x, kskeys, kwkeys):
        sm = sms[tk]
        o_tok = o_toks[tk]
        selbT4 = selbT4s[tk]
        K_ = lambda nm: (nm, tk)
        rdc, rdx, gco = sm[:, 0:4], sm[:, 4:8], sm[:, 8:12]
        imp, scb = sm[:, 16:48], sm[:, 48:80]
        mx8a, mx8b = sm[:, 80:88], sm[:, 88:96]
        selb = sm[:, 96:128]
        qblk = qT4[:, i * 512:(i + 1) * 512]
        qk = [("qT4", r, i // 4) for r in range(4)] + [("qT4lo", r, i // 4) for r in range(4)]
        gcol = lambda br: g_tok[:, i, g * 12 + br:g * 12 + br + 10:3]
        sbank = sbi[0] % 2
        sbi[0] += 1
        sk = "ps%d" % sbank
        for r in range(4):
            sc.op("pe", lambda: nc.tensor.matmul(C.ps[sbank][0:127, r * 128:(r + 1) * 128], lhsT=kccT[:, g, 0:127],
                                                  rhs=qblk[:, r * 128:(r + 1) * 128], start=True, stop=False),
                  r=[("kccT", g)] + qk, w=[sk])
            sc.op("pe", lambda: nc.tensor.matmul(C.ps[sbank][0:127, r * 128:(r + 1) * 128], lhsT=C.identb[0:127, 0:127],
                                                  rhs=cmaskb[0:127, i, :], start=False, stop=True),
                  r=["identb", "cmaskb"], w=[sk])
        pi = pti[0] % 4
        pti[0] += 1
        sc.op("act", lambda: nc.scalar.activation(out=PT[0:127, pi, :], in_=C.ps[sbank][0:127, :], func=AF.Exp, scale=SCALE),
              r=[sk], w=[("PT", pi)])
        pt, ptk = PT[:, pi, :], ("PT", pi)
        yield
        for r in range(4):
            bank = 2 + r // 2
            c0 = (r % 2) * 161
            sc.op("pe", lambda: nc.tensor.matmul(C.ps[bank][:, c0:c0 + 161], lhsT=pt[0:127, r * 128:(r + 1) * 128],
                                                  rhs=vcc[0:127, g, 0:161], start=True, stop=True),
                  r=[ptk, ("vcc", g, "v"), ("vcc", g, "cov"), ("vcc", g, "one")], w=["ps%d" % bank])
        for h2 in range(2):
            sc.op("dve", lambda: nc.vector.tensor_scalar(out=rdc[:, 2 * h2:2 * h2 + 2], in0=C.ps[2 + h2][:, 128:128 + 162:161],
                                                          scalar1=TINY, scalar2=None, op0=ALU.max), r=["ps%d" % (2 + h2)], w=[K_("rdc")])
        sc.op("dve", lambda: nc.vector.reciprocal(out=rdc, in_=rdc), r=[K_("rdc")], w=[K_("rdc")])
        for r in range(4):
            bank = 2 + r // 2
            c0 = (r % 2) * 161 + 129
            if r == 0:
                sc.op("dve", lambda: nc.vector.tensor_scalar(out=imp, in0=C.ps[bank][:, c0:c0 + 32], scalar1=rdc[:, 0:1], scalar2=None, op0=ALU.mult),
                      r=["ps%d" % bank, K_("rdc")], w=[K_("imp")])
            else:
                sc.op("dve", lambda: nc.vector.scalar_tensor_tensor(out=imp, in0=C.ps[bank][:, c0:c0 + 32], scalar=rdc[:, r:r + 1], in1=imp,
                                                                     op0=ALU.mult, op1=ALU.add), r=["ps%d" % bank, K_("rdc"), K_("imp")], w=[K_("imp")])
        sc.op("dve", lambda: nc.vector.tensor_tensor(out=scb, in0=imp, in1=selk[:, i, :], op=ALU.mult), r=[K_("imp"), "selk"], w=[K_("scb")])
        sc.op("dve", lambda: nc.vector.tensor_tensor(out=scb, in0=scb, in1=sela[:, i, :], op=ALU.add), r=[K_("scb"), "sela"], w=[K_("scb")])
        sc.op("dve", lambda: nc.vector.max(out=mx8a, in_=scb), r=[K_("scb")], w=[K_("mx8a")])
        sc.op("dve", lambda: nc.vector.match_replace(out=imp, in_to_replace=mx8a, in_values=scb, imm_value=-2.0), r=[K_("scb"), K_("mx8a")], w=[K_("imp")])
        sc.op("dve", lambda: nc.vector.max(out=mx8b, in_=imp), r=[K_("imp")], w=[K_("mx8b")])
        sc.op("dve", lambda: nc.vector.tensor_scalar(out=selb, in0=scb, scalar1=mx8b[:, 7:8], scalar2=NEG, op0=ALU.is_lt, op1=ALU.mult),
              r=[K_("scb"), K_("mx8b")], w=[K_("selb")])
        sc.op("dve", lambda: nc.vector.tensor_tensor(out=gco, in0=rdc, in1=gcol(0), op=ALU.mult), r=[K_("rdc"), ("gtok", i)], w=[K_("gco")])
        for r in range(4):
            bank = 2 + r // 2
            c0 = (r % 2) * 161
            sc.op("dve", lambda: nc.vector.tensor_scalar(out=o_tok[:, r, :], in0=C.ps[bank][:, c0:c0 + 128], scalar1=gco[:, r:r + 1], scalar2=None,
                                                          op0=ALU.mult), r=["ps%d" % bank, K_("gco")], w=[K_(("otok", r))])
        yield "PHASE"
        sc.op("pe", lambda: nc.tensor.transpose(out=C.ps[2][0:32, 0:128], in_=selb, identity=C.ident[:]), r=[K_("selb"), "ident"], w=["ps2"])
        src = C.ps[2][0:32, 0:128]
        src4 = bass.AP(src.tensor, src.offset, [list(src.ap[0]), [0, 4], list(src.ap[1])])
        sc.op("act", lambda: nc.scalar.copy(out=selbT4[:, :].rearrange("p (a b) -> p a b", a=4), in_=src4), r=["ps2"], w=[K_("selbT4")])
        yield

        def branch(kbs, kT_, kkeys, vt, vnm, bank0, masks_for, br):
            def pv(kb, pt, ptk):
                for r in range(4):
                    bank = bank0 + r // 2
                    c0 = (r % 2) * 129
                    sc.op("pe", lambda: nc.tensor.matmul(C.ps[bank][:, c0:c0 + 129], lhsT=pt[:, r * 128:(r + 1) * 128], rhs=vt[:, kb, 0:129],
                                                          start=(kb == kbs[0] and r % 2 == 0), stop=(kb == kbs[-1]), skip_group_check=True),
                          r=[ptk, (vnm, kb // 4), "Vones"], w=["ps%d" % bank])
            prev = None
            for kb in list(kbs) + [None]:
                cur = None
                if kb is not None:
                    pt, ptk = s_block(kT_[:, kb * 128:(kb + 1) * 128], kkeys, qblk, qk, 128, masks_for(kb))
                    cur = (kb, pt, ptk)
                if prev is not None:
                    pv(*prev)
                prev = cur
                yield
            for h2 in range(2):
                sc.op("dve", lambda: nc.vector.tensor_scalar(out=rdx[:, 2 * h2:2 * h2 + 2], in0=C.ps[bank0 + h2][:, 128:128 + 130:129],
                                                              scalar1=TINY, scalar2=None, op0=ALU.max), r=["ps%d" % (bank0 + h2)], w=[K_("rdx")])
            sc.op("dve", lambda: nc.vector.reciprocal(out=rdx, in_=rdx), r=[K_("rdx")], w=[K_("rdx")])
            sc.op("dve", lambda: nc.vector.tensor_tensor(out=gco, in0=rdx, in1=gcol(br), op=ALU.mult), r=[K_("rdx"), ("gtok", i)], w=[K_("gco")])
            for r in range(4):
                bank = bank0 + r // 2
                c0 = (r % 2) * 129
                sc.op("dve", lambda: nc.vector.scalar_tensor_tensor(out=o_tok[:, r, :], in0=C.ps[bank][:, c0:c0 + 128], scalar=gco[:, r:r + 1],
                                                                     in1=o_tok[:, r, :], op0=ALU.mult, op1=ALU.add),
                      r=["ps%d" % bank, K_("gco"), K_(("otok", r))], w=[K_(("otok", r))])
            yield

        dmask = (C.identb[:, :], maskb4[:, DIAG, :], ["identb", "maskb4"])
        w4mask = (C.identb[:, :], maskb4[:, W4, :], ["identb", "maskb4"])
        yield from branch(list(range(max(0, i - 4), i + 1)), kwT, kwkeys, Vw, "Vw", 6,
                          lambda kb: ([dmask] if kb == i else []) + ([w4mask] if kb == i - 4 else []), 2)
        yield "PHASE"
        yield from branch(list(range(0, i + 1)), ksT, kskeys, Vs, "Vs", 4,
                          lambda kb: [(eexpb[0:32, kb * 128:(kb + 1) * 128], selbT4[:, :], ["eexpb", K_("selbT4")])] + ([dmask] if kb == i else []), 1)
        yield "PHASE"
        for r in range(4):
            sc.op("pe", lambda: nc.tensor.transpose(out=C.ps[3][:, r * 128:(r + 1) * 128], in_=o_tok[:, r, :], identity=C.ident[:]),
                  r=[K_(("otok", r)), "ident"], w=["ps3"])
        sc.op("dve", lambda: nc.vector.tensor_tensor(out=yT8[:, g * 4:g * 4 + 4, i * 128:(i + 1) * 128],
                                                      in0=C.ps[3][:].rearrange("p (a d) -> p a d", a=4),
                                                      in1=big[:, 0:4, i * 128:(i + 1) * 128], op=ALU.mult),
              r=["ps3"] + [("sz%d" % r, i // 4) for r in range(4)], w=[("yT8", g, i)])
        yield

    def phase2_task(g):
        qT4v = qT4.rearrange("p (b r i) -> p b r i", r=4, i=128)
        pending = [None]

        def rope_chunk(wv, wk, q, views, key):
            pb = q % 2
            proj_T(C, wv, wk, 0, 128, q, pb)
            nxt = rope_part_a(C, pb, views, q, key, qraw, rt)
            if pending[0] is not None:
                pending[0]()
            pending[0] = nxt

        for r in range(4):
            wv, wk = load_w(C, win[:, (g * 4 + r) * 128:(g * 4 + r + 1) * 128], NCH, 128)
            for q in range(NQ):
                rope_chunk(wv, wk, q, (lambda rows, src, q=q, r=r: (qT4v[rows, 4 * q:4 * q + 4, r, :], src.rearrange("p (b i) -> p b i", b=4))),
                           ("qT4", r, q))
                yield
        for col, dst2d, nm in ((kvcol(2, g), ksT, "ksT"), (kvcol(4, g), kwT, "kwT")):
            wv, wk = load_w(C, win[:, col:col + 128], NCH, 128)
            for q in range(NQ):
                rope_chunk(wv, wk, q, (lambda rows, src, q=q, dst2d=dst2d: (dst2d[rows, q * 512:(q + 1) * 512], src)), (nm, q))
                yield
        pending[0]()
        pending[0] = None
        for slot_i, vt, nm in ((3, Vs, "Vs"), (5, Vw, "Vw")):
            wv, wk = load_w(C, win[:, kvcol(slot_i, g):kvcol(slot_i, g) + 128], NCH, 128)
            for b4 in range(4):
                pb = b4 % 2
                for bi in range(4):
                    b = b4 * 4 + bi
                    for k in range(NCH):
                        sc.op("pe", lambda: nc.tensor.matmul(C.ps[pb][:, bi * 128:(bi + 1) * 128], lhsT=C.hT[:, k, b * 128:(b + 1) * 128],
                                                              rhs=wv[:, k, :], start=(k == 0), stop=(k == NCH - 1)),
                              r=[wk, hk(k, b // 4)], w=["ps%d" % pb])
                sc.op("act", lambda: nc.scalar.copy(out=vt[:, b4 * 4:(b4 + 1) * 4, 0:128], in_=C.ps[pb][:].rearrange("p (a d) -> p a d", a=4)),
                      r=["ps%d" % pb, "Vones"], w=[(nm, b4)])
                yield

    def chain(*gens):
        for g_ in gens:
            yield from g_

    for g in range(2):
        if g == 0:
            run_interleaved([compress_task(), chain(gates_task(), phase2_task(0))])
            sc.barrier()
        else:
            run_interleaved([phase2_task(1)])
        for r in range(4):
            proj_plain(2560 + (g * 4 + r) * 128, big[:, r, :], "sz%d" % r, func=AF.Silu)
        kskeys = [("ksT", q) for q in range(NQ)] + [("ksTlo", q) for q in range(NQ)]
        kwkeys = [("kwT", q) for q in range(NQ)] + [("kwTlo", q) for q in range(NQ)]

        gens = [qblock_task(g, i, i % 4, None, kskeys, kwkeys) for i in range(NT)]
        for step in range(NT + 3):
            act = [gens[step - k] for k in range(4) if 0 <= step - k < NT]
            while act:
                for t in list(act):
                    try:
                        if next(t) == "PHASE":
                            act.remove(t)
                    except StopIteration:
                        act.remove(t)
        sc.barrier()

    for c in range(NCH):
        sc.dma("sp", C.xT[:, c, :], C.xspill[:, c * S:(c + 1) * S], w=[("xT", c, tt) for tt in range(NT)])
    wout = W["b_w_out"][j]
    for oc in range(8):
        wv, wk = load_w(C, wout[:, oc * 128:(oc + 1) * 128], 8, 128)
        for o2 in range(1):
            for q in range(NQ):
                pb = (oc * NQ + q) % 2
                for k in range(8):
                    sc.op("pe", lambda: nc.tensor.matmul(C.ps[pb][:], lhsT=wv[:, k, o2 * 128:(o2 + 1) * 128],
                                                          rhs=yT8[:, k, q * 512:(q + 1) * 512], start=(k == 0), stop=(k == 7)),
                          r=[wk], w=["ps%d" % pb])
                residual_add(C, oc, q, pb)


_NC_CACHE = {}


def make_consts():
    half = 16
    inv = 500000.0 ** (-2.0 * np.arange(half, dtype=np.float64) / 32.0)
    ang = np.arange(S, dtype=np.float64)[None, :] * inv[:, None]
    ang = (np.arange(S, dtype=np.float32)[None, :] * inv.astype(np.float32)[:, None]).astype(np.float64)
    cos = np.concatenate([np.cos(ang), np.cos(ang)], 0)
    sin = np.concatenate([-np.sin(ang), np.sin(ang)], 0)
    rope = np.stack([cos, sin]).astype(np.float32)
    k = np.arange(128)[:, None]
    q = np.arange(128)[None, :]
    mask = np.zeros((4, 128, 128), np.float32)
    mask[0] = np.where(q >= k, 0.0, NEG)
    mask[1] = np.where(q <= k, 0.0, NEG)
    mask[2] = np.where(k > q, 0.0, NEG)
    mask[3] = NEG
    prot = np.zeros((32, 32), np.float32)
    for m in range(32):
        prot[(m + 16) % 32, m] = 1.0
    jj = np.arange(128)[:, None]
    ql = np.arange(128)[None, :]
    cmask = np.zeros((16, 128, 128), np.float32)
    for i in range(16):
        cmask[i] = np.where((16 * jj + 31 <= 128 * i + ql) & (jj < 127), 0.0, NEG)
    eexp = (np.arange(S)[None, :] // 64 == np.arange(32)[:, None]).astype(np.float32)
    n = np.arange(32)[None, :]
    cover = ((jj * 16 < (n + 1) * 64) & (jj * 16 + 32 > n * 64) & (jj < 127)).astype(np.float32)
    selk = np.zeros((16, 128, 32), np.float32)
    sela = np.zeros((16, 128, 32), np.float32)
    for i in range(16):
        cur = (128 * i + np.arange(128)[:, None]) // 64
        forced = (n == 0) | (n == cur) | (n == cur - 1)
        valid = n <= cur
        selk[i] = np.where(valid & ~forced, 1.0, 0.0)
        sela[i] = np.where(valid, np.where(forced, 1000.0, 0.0), -1.0)
    return {"c_rope": rope, "c_mask": mask, "c_prot": prot, "c_cmask": cmask, "c_eexp": eexp,
            "c_cover": cover, "c_selk": selk, "c_sela": sela}


def kernel(**inputs):
    layers = inputs.pop("_layers", (0, 1, 2, 3))
    x = np.ascontiguousarray(inputs["x"], dtype=np.float32)
    key = tuple(layers)
    if key not in _NC_CACHE:
        _NC_CACHE[key] = build(layers)
    nc = _NC_CACHE[key]
    ident = np.eye(128, dtype=np.float32)
    consts = make_consts()
    in_maps = []
    for i in range(N_CORES):
        m = {"x": x[i * SEQ_PER_CORE:(i + 1) * SEQ_PER_CORE],
             "norm_g": np.ascontiguousarray(inputs["norm_g"], dtype=np.float32),
             "final_g": np.ascontiguousarray(inputs["final_g"], dtype=np.float32),
             "ident": ident}
        m.update(consts)
        for name in WSHAPES:
            m[name] = np.ascontiguousarray(inputs[name], dtype=np.float32)
        in_maps.append(m)
    res = run_bass_kernel_spmd(nc, in_maps, core_ids=list(range(N_CORES)))
    return np.concatenate([r["out"] for r in res.results], axis=0)
```

```python
import numpy as np
import concourse.bass as bass
import concourse.mybir as mybir
from concourse.bass_utils import run_bass_kernel_spmd

F32 = mybir.dt.float32
BF16 = mybir.dt.bfloat16
U32 = mybir.dt.uint32
AF = mybir.ActivationFunctionType
ALU = mybir.AluOpType
AX = mybir.AxisListType

D = 1024
S = 2048
NCH = 8
NT = 16
NQ = 4
EPS = 1e-6
N_CORES = 8
SEQ_PER_CORE = 2


class Sched:
    NDSEM = 24

    def __init__(self, nc, stack):
        self.nc = nc
        self.engs = {"pe": nc.tensor, "act": nc.scalar, "dve": nc.vector,
                     "pool": nc.gpsimd, "sp": nc.sync}
        self.sem = {}
        for e in self.engs:
            self.sem[e] = stack.enter_context(nc.semaphore("sem_" + e))
        self.dsem = [stack.enter_context(nc.semaphore("dsem%d" % i)) for i in range(self.NDSEM)]
        self.dcount = [0] * self.NDSEM
        self.dnext = 0
        self.count = {e: 0 for e in self.engs}
        self.waited = {}
        self.last_w = {}
        self.readers = {}
        self.ninst = 0

    def _wait(self, eng, tok):
        kind, a, v = tok
        if kind == "e" and a == eng and eng == "pe":
            return
        key = (eng, kind, a)
        if self.waited.get(key, 0) >= v:
            return
        self.waited[key] = v
        sem = self.sem[a] if kind == "e" else self.dsem[a]
        self.engs[eng].wait_ge(sem, v)

    def _deps(self, eng, r, w):
        toks = set()
        for k in r:
            t = self.last_w.get(k)
            if t is not None:
                toks.add(t)
        for k in w:
            t = self.last_w.get(k)
            if t is not None:
                toks.add(t)
            for t2 in self.readers.get(k, {}).values():
                toks.add(t2)
        for t in sorted(toks):
            self._wait(eng, t)

    def _record(self, tok, r, w):
        for k in w:
            self.last_w[k] = tok
            self.readers[k] = {}
        for k in r:
            d = self.readers.setdefault(k, {})
            kk = (tok[0], tok[1])
            if kk not in d or d[kk][2] < tok[2]:
                d[kk] = tok

    def op(self, eng, fn, r=(), w=()):
        pr = [k for k in r if isinstance(k, str) and k.startswith("ps")]
        if pr:
            r = [k for k in r if k not in pr]
            w = list(w) + pr
        self._deps(eng, r, w)
        inst = fn()
        self.count[eng] += 1
        inst.then_inc(self.sem[eng], 1)
        tok = ("e", eng, self.count[eng])
        self._record(tok, r, w)
        self.ninst += 1
        return tok

    def dma(self, q, out, in_, r=(), w=()):
        self._deps(q, r, w)
        k = self.dnext
        self.dnext = (self.dnext + 1) % self.NDSEM
        if self.dcount[k] > 0:
            self._wait(q, ("d", k, 16 * self.dcount[k]))
        self.dcount[k] += 1
        self.engs[q].dma_start(out=out, in_=in_).then_inc(self.dsem[k], 16)
        tok = ("d", k, 16 * self.dcount[k])
        self._record(tok, r, w)
        self.ninst += 1
        return tok

    def barrier(self):
        toks = [("e", e, c) for e, c in self.count.items() if c > 0]
        toks += [("d", k, 16 * c) for k, c in enumerate(self.dcount) if c > 0]
        for e in self.engs:
            for t in toks:
                if t[0] == "e" and t[1] == e:
                    continue
                self._wait(e, t)
        self.last_w = {}
        self.readers = {}

    def wait_all_on(self, eng):
        toks = [("e", e, c) for e, c in self.count.items() if c > 0 and e != eng]
        toks += [("d", k, 16 * c) for k, c in enumerate(self.dcount) if c > 0]
        for t in toks:
            self._wait(eng, t)


class Ctx:
    pass


def build(layers=(0, 1, 2, 3), nseq=SEQ_PER_CORE):
    from contextlib import ExitStack
    nc = bass.Bass("TRN2", target_bir_lowering=False)
    C = Ctx()
    C.nc = nc
    dt = lambda n, s, d=F32, kind="ExternalInput": nc.dram_tensor(n, list(s), d, kind=kind).ap()
    C.x = dt("x", [nseq, S, D])
    C.out = dt("out", [nseq, S, D], kind="ExternalOutput")
    C.norm_g = dt("norm_g", [4, D])
    C.final_g = dt("final_g", [D])
    C.ident_in = dt("ident", [128, 128])
    C.c_rope = dt("c_rope", [2, 32, S])
    C.c_mask = dt("c_mask", [4, 128, 128])
    C.c_prot = dt("c_prot", [32, 32])
    C.c_cmask = dt("c_cmask", [16, 128, 128])
    C.c_eexp = dt("c_eexp", [32, S])
    C.c_cover = dt("c_cover", [128, 32])
    C.c_selk = dt("c_selk", [16, 128, 32])
    C.c_sela = dt("c_sela", [16, 128, 32])
    C.xspill = nc.dram_tensor("xspill", [128, NCH * S], F32, kind="Internal").ap()
    C.W = {}
    for name, shp in WSHAPES.items():
        C.W[name] = dt(name, shp)

    with ExitStack() as st:
        sc = Sched(nc, st)
        C.sc = sc
        sb = lambda n, s, d=F32: st.enter_context(nc.sbuf_tensor(n, list(s), d))
        C.xT = sb("xT", [128, NCH, S])
        C.hT = sb("hT", [128, NCH, S], BF16)
        C.ident = sb("identf", [128, 128])
        C.identb = sb("identb", [128, 128], BF16)
        C.onesb = sb("onesb", [128, 128], BF16)
        C.gall = sb("gall", [128, 5, NCH])
        C.gtmp = sb("gtmp", [40, 128])
        C.wst = [sb("wst%d" % i, [128, WSLOT]) for i in range(3)]
        C.wbf = [sb("wbf%d" % i, [128, WSLOT], BF16) for i in range(6)]
        C.wst_i = 0
        C.wbf_i = 0
        C.ps = [st.enter_context(nc.psum_tensor("ps%d" % i, [128, 512], F32)) for i in range(8)]

        sc.dma("sp", C.ident[:], C.ident_in, w=["ident"])
        sc.op("dve", lambda: nc.vector.tensor_copy(out=C.identb[:], in_=C.ident[:]), r=["ident"], w=["identb"])
        sc.op("dve", lambda: nc.vector.memset(C.onesb[:], 1.0 / D), w=["onesb"])
        sc.dma("sp", C.gtmp[0:32, :], C.norm_g.rearrange("l (c p) -> (l c) p", p=128), w=["gtmp"])
        sc.dma("sp", C.gtmp[32:40, :], C.final_g.rearrange("(c p) -> c p", p=128), w=["gtmp2"])
        sc.op("pe", lambda: nc.tensor.transpose(out=C.ps[0][:, 0:40], in_=C.gtmp[0:40, :], identity=C.ident[0:40, 0:40]),
              r=["gtmp", "gtmp2", "ident"], w=["ps0"])
        sc.op("dve", lambda: nc.vector.tensor_copy(out=C.gall[:].rearrange("p l c -> p (l c)"), in_=C.ps[0][:, 0:40]),
              r=["ps0"], w=["gall"])

        C.ones1b = sb("ones1b", [128, 128], BF16)
        C.maskb = sb("maskb", [128, 4, 128], BF16)
        C.protb = sb("protb", [32, 32], BF16)
        C.ropeb = sb("ropeb", [32, 2, S], BF16)
        sc.op("dve", lambda: nc.vector.memset(C.ones1b[:], 1.0), w=["ones1b"])
        with ExitStack() as st2:
            tmpc = st2.enter_context(nc.sbuf_tensor("tmpc", [128, 2 * S], F32))
            sc.dma("sp", tmpc[:, 0:512].rearrange("p (a b) -> p a b", a=4), C.c_mask.rearrange("a p b -> p a b"), w=["tmpc"])
            sc.op("dve", lambda: nc.vector.tensor_copy(out=C.maskb[:].rearrange("p a b -> p (a b)"), in_=tmpc[:, 0:512]), r=["tmpc"], w=["maskb"])
            sc.dma("sp", tmpc[0:32, 0:32], C.c_prot, r=["maskb"], w=["tmpc"])
            sc.op("dve", lambda: nc.vector.tensor_copy(out=C.protb[:], in_=tmpc[0:32, 0:32]), r=["tmpc"], w=["protb"])
            sc.dma("sp", tmpc[0:32, :].rearrange("p (a b) -> p a b", a=2), C.c_rope.rearrange("a p b -> p a b"), r=["protb"], w=["tmpc"])
            sc.op("dve", lambda: nc.vector.tensor_copy(out=C.ropeb[:].rearrange("p a b -> p (a b)"), in_=tmpc[0:32, :]), r=["tmpc"], w=["ropeb"])
            sc.barrier()
        C.rope_i = 0
        C.sb_i = 0
        C.pt_i = 0

        for s in range(nseq):
            sc.barrier()
            with ExitStack() as st2:
                C.io = st2.enter_context(nc.sbuf_tensor("io_l%d" % s, [128, 2, D], F32))
                load_x(C, s)
                sc.barrier()
            for li in layers:
                with ExitStack() as st2:
                    C.st2 = st2
                    C.uid = "s%dl%d" % (s, li)
                    if li % 3 == 1:
                        for c in range(NCH):
                            sc.dma("sp", C.xspill[:, c * S:(c + 1) * S], C.xT[:, c, :], r=[("xT", c, tt) for tt in range(NT)])
                    make_hT(C, li)
                    if li % 3 == 2:
                        C.cast_eng = "dve"
                        layer_c(C, li // 3)
                        C.cast_eng = "pool"
                    elif li % 3 == 0:
                        layer_a(C, li // 3)
                    else:
                        layer_b(C, li // 3)
                    sc.barrier()
            with ExitStack() as st2:
                C.io = st2.enter_context(nc.sbuf_tensor("io_s%d" % s, [128, 2, D], F32))
                C.rstd = st2.enter_context(nc.sbuf_tensor("rstd_f%d" % s, [128, 512], F32))
                C.sq = st2.enter_context(nc.sbuf_tensor("sq_f%d" % s, [128, 2, 512], BF16))
                final_norm_store(C, s)
                sc.barrier()
        sc.wait_all_on("sp")
    return nc


WSLOT = 1024


WSHAPES = {
    "a_w_in": [2, 1024, 10240], "a_w_out": [2, 1024, 1024],
    "b_w_in": [1, 1024, 3608], "b_gate_b": [1, 24], "b_pe_k": [1, 32, 128], "b_w1_k": [1, 32, 128, 128],
    "b_w2_k": [1, 128, 128], "b_pe_v": [1, 32, 128], "b_w1_v": [1, 32, 128, 128], "b_w2_v": [1, 128, 128],
    "b_w_out": [1, 1024, 1024],
    "c_w_in": [1, 1024, 2560], "c_conv_w": [1, 4, 1280], "c_conv_b": [1, 1280], "c_wa": [1, 10, 128, 128],
    "c_ba": [1, 1280], "c_wx": [1, 10, 128, 128], "c_bx": [1, 1280], "c_lambda": [1, 1280], "c_w_out": [1, 1280, 1024],
}


def load_w(C, src, K, W):
    nc, sc = C.nc, C.sc
    assert K * W <= WSLOT
    si = C.wst_i % len(C.wst)
    C.wst_i += 1
    bi = C.wbf_i % len(C.wbf)
    C.wbf_i += 1
    stv = C.wst[si][:, 0:K * W].rearrange("p (k w) -> p k w", w=W)
    bfv = C.wbf[bi][:, 0:K * W].rearrange("p (k w) -> p k w", w=W)
    sc.dma("sp", stv, src.rearrange("(k p) w -> p k w", p=128), w=[("wst", si)])
    if getattr(C, "cast_eng", "pool") == "dve":
        sc.op("dve", lambda: nc.vector.tensor_copy(out=C.wbf[bi][:, 0:K * W], in_=C.wst[si][:, 0:K * W]),
              r=[("wst", si)], w=[("wbf", bi)])
    else:
        sc.op("pool", lambda: nc.gpsimd.tensor_copy(out=C.wbf[bi][:, 0:K * W], in_=C.wst[si][:, 0:K * W]),
              r=[("wst", si)], w=[("wbf", bi)])
    return bfv, ("wbf", bi)


def hk(c, q):
    return ("hT", c, q)


def make_hT(C, li):
    from contextlib import ExitStack
    nc, sc = C.nc, C.sc
    with ExitStack() as st3:
        C.rstd = st3.enter_context(nc.sbuf_tensor("rstd" + C.uid, [128, 512], F32))
        C.sq = st3.enter_context(nc.sbuf_tensor("sq" + C.uid, [128, 2, 512], BF16))
        _make_hT(C, li)
        sc.barrier()


def _make_hT(C, li):
    nc, sc = C.nc, C.sc
    for q in range(NQ):
        rms_rstd(C, q)
        for c in range(NCH):
            sc.op("dve", lambda: nc.vector.scalar_tensor_tensor(
                out=C.hT[:, c, q * 512:(q + 1) * 512], in0=C.xT[:, c, q * 512:(q + 1) * 512],
                scalar=C.gall[:, li, c:c + 1], in1=C.rstd[:], op0=ALU.mult, op1=ALU.mult),
                r=xk(c, q) + ["rstd", "gall"], w=[hk(c, q)])


def proj_T(C, wv, wkey, wcol, M, q, pb, K=NCH, rhs=None, rkeys=None):
    nc, sc = C.nc, C.sc
    for k in range(K):
        sc.op("pe", lambda: nc.tensor.matmul(C.ps[pb][0:M, :], lhsT=wv[:, k, wcol:wcol + M],
                                              rhs=C.hT[:, k, q * 512:(q + 1) * 512],
                                              start=(k == 0), stop=(k == K - 1)),
              r=[wkey, hk(k, q)], w=["ps%d" % pb])


def residual_add(C, oc, q, pb):
    nc, sc = C.nc, C.sc
    sc.op("dve", lambda: nc.vector.tensor_tensor(out=C.xT[:, oc, q * 512:(q + 1) * 512], in0=C.ps[pb][:],
                                                  in1=C.xT[:, oc, q * 512:(q + 1) * 512], op=ALU.add),
          r=["ps%d" % pb] + xk(oc, q), w=xk(oc, q))


def layer_c(C, j):
    nc, sc, st = C.nc, C.sc, C.st2
    u = C.uid
    sb = lambda n, s, d=F32: st.enter_context(nc.sbuf_tensor(n + u, list(s), d))
    NB = 10
    cpt = sb("cpt", [80, 128])
    cpar = sb("cpar", [128, 80])
    spt = sb("spt", [128, 4, NB])
    onec = sb("onec", [128, 1])
    sc.op("dve", lambda: nc.vector.memset(onec[:], 1.0), w=["onec"])
    XB = [sb("cXB%d" % i, [128, 3 + S]) for i in range(2)]
    B = [None] + [sb("cB%d" % i, [128, 3 + S]) for i in range(1, 4)]
    xcb = sb("xcb", [128, S], BF16)
    SZ = [sb("sz%d" % i, [128, S], BF16) for i in range(2)]
    yT = sb("yT", [128, 5, S], BF16)
    W = C.W
    sc.dma("sp", cpt[0:40, :], W["c_conv_w"][j].rearrange("j (n p) -> (j n) p", p=128), w=["cpt0"])
    for i, nm in enumerate(["c_conv_b", "c_ba", "c_bx", "c_lambda"]):
        sc.dma("sp", cpt[40 + 10 * i:50 + 10 * i, :], W[nm][j].rearrange("(n p) -> n p", p=128), w=["cpt%d" % (i + 1)])
    sc.op("pe", lambda: nc.tensor.transpose(out=C.ps[0][:, 0:80], in_=cpt[:, :], identity=C.ident[0:80, 0:80]),
          r=["cpt%d" % i for i in range(5)] + ["ident"], w=["ps0"])
    sc.op("dve", lambda: nc.vector.tensor_copy(out=cpar[:], in_=C.ps[0][:, 0:80]), r=["ps0"], w=["cpar"])
    lam = cpar[:, 70:80]
    sc.op("dve", lambda: nc.vector.tensor_scalar(out=spt[:, 0, :], in0=lam, scalar1=-1.0, scalar2=None, op0=ALU.mult), r=["cpar"], w=["spt0"])
    sc.op("dve", lambda: nc.vector.tensor_tensor(out=spt[:, 0, :], in0=spt[:, 0, :], in1=lam, op=ALU.max), r=["cpar", "spt0"], w=["spt0"])
    sc.op("act", lambda: nc.scalar.activation(out=spt[:, 0, :], in_=spt[:, 0, :], func=AF.Exp, scale=-1.0), r=["spt0"], w=["spt0"])
    sc.op("dve", lambda: nc.vector.tensor_scalar(out=spt[:, 0, :], in0=spt[:, 0, :], scalar1=1.0, scalar2=None, op0=ALU.add), r=["spt0"], w=["spt0"])
    sc.op("act", lambda: nc.scalar.activation(out=spt[:, 0, :], in_=spt[:, 0, :], func=AF.Ln), r=["spt0"], w=["spt0"])
    sc.op("dve", lambda: nc.vector.tensor_scalar(out=spt[:, 1, :], in0=lam, scalar1=-1.0, scalar2=0.0, op0=ALU.mult, op1=ALU.max), r=["cpar"], w=["spt1"])
    sc.op("dve", lambda: nc.vector.tensor_tensor(out=spt[:, 1, :], in0=spt[:, 1, :], in1=spt[:, 0, :], op=ALU.add), r=["spt0", "spt1"], w=["spt1"])
    sc.op("dve", lambda: nc.vector.tensor_scalar(out=spt[:, 2, :], in0=spt[:, 1, :], scalar1=-8.0, scalar2=None, op0=ALU.mult), r=["spt1"], w=["spt2"])
    sc.op("dve", lambda: nc.vector.tensor_scalar(out=spt[:, 3, :], in0=spt[:, 1, :], scalar1=-16.0, scalar2=None, op0=ALU.mult), r=["spt1"], w=["spt3"])
    for i in range(2):
        sc.op("pool", lambda: nc.gpsimd.memset(XB[i][:, 0:3], 0.0), w=[("XBpad", i)])

    win = W["c_w_in"][j]
    wo = W["c_w_out"][j]

    def P(n):
        xb, sz = XB[n % 2], SZ[n % 2]
        wxb, kxb = load_w(C, win[:, n * 128:(n + 1) * 128], NCH, 128)
        wz, kz = load_w(C, win[:, 1280 + n * 128:1280 + (n + 1) * 128], NCH, 128)
        for q in range(NQ):
            pb = q % 2
            proj_T(C, wxb, kxb, 0, 128, q, pb)
            sc.op("act", lambda: nc.scalar.copy(out=xb[:, 3 + q * 512:3 + (q + 1) * 512], in_=C.ps[pb][:]),
                  r=["ps%d" % pb, ("XBpad", n % 2)], w=[("xb", n % 2, q)])
            yield
            pb2 = 2 + q % 2
            proj_T(C, wz, kz, 0, 128, q, pb2)
            sc.op("act", lambda: nc.scalar.activation(out=sz[:, q * 512:(q + 1) * 512], in_=C.ps[pb2][:], func=AF.Silu),
                  r=["ps%d" % pb2], w=[("sz", n % 2, q)])
            yield

    def E(n):
        n5 = n % 5
        xb, sz = XB[n % 2], SZ[n % 2]
        xc, rr, ii, aa = B[1], B[2], B[3], xb
        wa, ka = load_w(C, W["c_wa"][j, n], 1, 128)
        wx, kx = load_w(C, W["c_wx"][j, n], 1, 128)
        cw = lambda jj: cpar[:, jj * 10 + n:jj * 10 + n + 1]

        def gates(q):
            sl = slice(3 + q * 512, 3 + (q + 1) * 512)
            pb = 4 + q % 2
            sc.op("pe", lambda: nc.tensor.matmul(C.ps[pb][:], lhsT=wa[:, 0, :], rhs=xcb[:, q * 512:(q + 1) * 512], start=True, stop=True),
                  r=[ka, ("xcb", q)], w=["ps%d" % pb])
            sc.op("act", lambda: nc.scalar.activation(out=rr[:, sl], in_=C.ps[pb][:], func=AF.Sigmoid,
                                                      bias=cpar[:, 50 + n:51 + n]), r=["ps%d" % pb, "cpar"], w=[("rr", q)])
            pb = 6 + q % 2
            sc.op("pe", lambda: nc.tensor.matmul(C.ps[pb][:], lhsT=wx[:, 0, :], rhs=xcb[:, q * 512:(q + 1) * 512], start=True, stop=True),
                  r=[kx, ("xcb", q)], w=["ps%d" % pb])
            sc.op("act", lambda: nc.scalar.activation(out=ii[:, sl], in_=C.ps[pb][:], func=AF.Sigmoid,
                                                      bias=cpar[:, 60 + n:61 + n]), r=["ps%d" % pb, "cpar"], w=[("ii", q)])

        for q in range(NQ):
            sl = slice(3 + q * 512, 3 + (q + 1) * 512)
            xbk = [("xb", n % 2, q)] + ([("xb", n % 2, q - 1)] if q > 0 else [("XBpad", n % 2)])
            sc.op("dve", lambda: nc.vector.tensor_scalar(out=xc[:, sl], in0=xb[:, sl], scalar1=cw(3), scalar2=cpar[:, 40 + n:41 + n],
                                                          op0=ALU.mult, op1=ALU.add), r=xbk + ["cpar"], w=[("xc", q)])
            for jj in range(3):
                sc.op("dve", lambda: nc.vector.scalar_tensor_tensor(out=xc[:, sl], in0=xb[:, jj + q * 512:jj + (q + 1) * 512], scalar=cw(jj),
                                                                     in1=xc[:, sl], op0=ALU.mult, op1=ALU.add),
                      r=xbk + [("xc", q), "cpar"], w=[("xc", q)])
            sc.op("pool", lambda: nc.gpsimd.tensor_copy(out=xcb[:, q * 512:(q + 1) * 512], in_=xc[:, sl]), r=[("xc", q)], w=[("xcb", q)])
            if q > 0:
                gates(q - 1)
            yield
        gates(NQ - 1)
        yield
        for q in range(0):
            pb = 4 + q % 2
            sc.op("pe", lambda: nc.tensor.matmul(C.ps[pb][:], lhsT=wa[:, 0, :], rhs=xcb[:, q * 512:(q + 1) * 512], start=True, stop=True),
                  r=[ka, ("xcb", q)], w=["ps%d" % pb])
            sc.op("act", lambda: nc.scalar.activation(out=rr[:, sl], in_=C.ps[pb][:], func=AF.Sigmoid,
                                                      bias=cpar[:, 50 + n:51 + n]), r=["ps%d" % pb, "cpar"], w=[("rr", q)])
            pb = 6 + q % 2
            sc.op("pe", lambda: nc.tensor.matmul(C.ps[pb][:], lhsT=wx[:, 0, :], rhs=xcb[:, q * 512:(q + 1) * 512], start=True, stop=True),
                  r=[kx, ("xcb", q)], w=["ps%d" % pb])
            sc.op("act", lambda: nc.scalar.activation(out=ii[:, sl], in_=C.ps[pb][:], func=AF.Sigmoid,
                                                      bias=cpar[:, 60 + n:61 + n]), r=["ps%d" % pb, "cpar"], w=[("ii", q)])
            yield
        for q in range(NQ):
            sl = slice(3 + q * 512, 3 + (q + 1) * 512)
            sc.op("act", lambda: nc.scalar.activation(out=aa[:, sl], in_=rr[:, sl], func=AF.Exp, scale=spt[:, 2, n:n + 1]),
                  r=[("rr", q), "spt2"], w=[("xb", n % 2, q)])
            sc.op("act", lambda: nc.scalar.activation(out=rr[:, sl], in_=rr[:, sl], func=AF.Exp, scale=spt[:, 3, n:n + 1]),
                  r=[("rr", q), "spt3"], w=[("rr", q)])
            sc.op("pool", lambda: nc.gpsimd.tensor_tensor(out=ii[:, sl], in0=ii[:, sl], in1=xc[:, sl], op=ALU.mult),
                  r=[("ii", q), ("xc", q)], w=[("ii", q)])
        yield
        for q in range(NQ):
            sl = slice(3 + q * 512, 3 + (q + 1) * 512)
            sc.op("act", lambda: nc.scalar.activation(out=rr[:, sl], in_=rr[:, sl], func=AF.Sqrt, scale=-1.0, bias=onec[:, 0:1]),
                  r=[("rr", q), "onec"], w=[("rr", q)])
            sc.op("pool", lambda: nc.gpsimd.tensor_tensor(out=ii[:, sl], in0=ii[:, sl], in1=rr[:, sl], op=ALU.mult),
                  r=[("ii", q), ("rr", q)], w=[("ii", q)])
        yield
        for q in range(NQ):
            sl = slice(3 + q * 512, 3 + (q + 1) * 512)
            init = 0.0 if q == 0 else xc[:, 3 + q * 512 - 1:3 + q * 512]
            sc.op("dve", lambda: nc.vector.tensor_tensor_scan(out=xc[:, sl], data0=aa[:, sl], data1=ii[:, sl], initial=init,
                                                               op0=ALU.mult, op1=ALU.add),
                  r=[("xb", n % 2, q), ("ii", q), ("xc", q)] + ([("xc", q - 1)] if q else []), w=[("xc", q)])
            sc.op("pool", lambda: nc.gpsimd.tensor_tensor(out=yT[:, n5, q * 512:(q + 1) * 512], in0=xc[:, sl], in1=sz[:, q * 512:(q + 1) * 512], op=ALU.mult),
                  r=[("xc", q), ("sz", n % 2, q)], w=[("yT", n5, q)])
            yield

    def O(half):
        for oc in range(8):
            wv, wk = load_w(C, wo[half * 640:(half + 1) * 640, oc * 128:(oc + 1) * 128], 5, 128)
            for q in range(NQ):
                pb = (oc * NQ + q) % 2
                for k in range(5):
                    sc.op("pe", lambda: nc.tensor.matmul(C.ps[pb][:], lhsT=wv[:, k, :],
                                                          rhs=yT[:, k, q * 512:(q + 1) * 512], start=(k == 0), stop=(k == 4)),
                          r=[wk, ("yT", k, q)], w=["ps%d" % pb])
                residual_add(C, oc, q, pb)

    run_interleaved([P(0)])
    for n in range(10):
        te = E(n)
        tp = P(n + 1) if n + 1 < 10 else None
        while te is not None or tp is not None:
            for _ in range(6):
                if te is not None:
                    try:
                        next(te)
                    except StopIteration:
                        te = None
            if tp is not None:
                try:
                    next(tp)
                except StopIteration:
                    tp = None
        if n % 5 == 4:
            O(n // 5)


def xk(c, q):
    return [("xT", c, tt) for tt in range(4 * q, 4 * q + 4)]


def load_x(C, s):
    nc, sc = C.nc, C.sc
    for tt in range(NT):
        b = tt % 2
        sc.dma("sp", C.io[:, b, :], C.x[s, tt * 128:(tt + 1) * 128, :], w=[("io", b, cc) for cc in range(NCH)])
        for c in range(NCH):
            pb = (tt * NCH + c) % 4
            sc.op("pe", lambda: nc.tensor.transpose(out=C.ps[pb][:, 0:128], in_=C.io[:, b, c * 128:(c + 1) * 128],
                                                     identity=C.ident[:]),
                  r=[("io", b, c), "ident"], w=["ps%d" % pb])
            eng = "act" if c % 2 else "dve"
            if eng == "act":
                sc.op("act", lambda: nc.scalar.copy(out=C.xT[:, c, tt * 128:(tt + 1) * 128], in_=C.ps[pb][:, 0:128]),
                      r=["ps%d" % pb], w=[("xT", c, tt)])
            else:
                sc.op("dve", lambda: nc.vector.tensor_copy(out=C.xT[:, c, tt * 128:(tt + 1) * 128], in_=C.ps[pb][:, 0:128]),
                      r=["ps%d" % pb], w=[("xT", c, tt)])


def rms_rstd(C, q, dst_keys_done=None):
    nc, sc = C.nc, C.sc
    pb = 4 + (q % 2)
    for c in range(NCH):
        b = c % 2
        sc.op("act", lambda: nc.scalar.activation(out=C.sq[:, b, :], in_=C.xT[:, c, q * 512:(q + 1) * 512], func=AF.Square),
              r=xk(c, q), w=[("sq", b)])
        sc.op("pe", lambda: nc.tensor.matmul(C.ps[pb][:], lhsT=C.onesb[:], rhs=C.sq[:, b, :], start=(c == 0), stop=(c == NCH - 1)),
              r=[("sq", b), "onesb"], w=["ps%d" % pb])
    sc.op("dve", lambda: nc.vector.tensor_scalar(out=C.rstd[:], in0=C.ps[pb][:], scalar1=EPS, scalar2=None, op0=ALU.add),
          r=["ps%d" % pb], w=["rstd"])
    sc.op("act", lambda: nc.scalar.activation(out=C.rstd[:], in_=C.rstd[:], func=AF.Sqrt), r=["rstd"], w=["rstd"])
    sc.op("dve", lambda: nc.vector.reciprocal(out=C.rstd[:], in_=C.rstd[:]), r=["rstd"], w=["rstd"])


def final_norm_store(C, s):
    nc, sc = C.nc, C.sc
    for q in range(NQ):
        rms_rstd(C, q)
        for c in range(NCH):
            sc.op("dve", lambda: nc.vector.scalar_tensor_tensor(
                out=C.xT[:, c, q * 512:(q + 1) * 512], in0=C.xT[:, c, q * 512:(q + 1) * 512],
                scalar=C.gall[:, 4, c:c + 1], in1=C.rstd[:], op0=ALU.mult, op1=ALU.mult),
                r=xk(c, q) + ["rstd", "gall"], w=xk(c, q))
        for t4 in range(4):
            tt = q * 4 + t4
            b = tt % 2
            for c in range(NCH):
                pb = (tt * NCH + c) % 4
                sc.op("pe", lambda: nc.tensor.transpose(out=C.ps[pb][:, 0:128], in_=C.xT[:, c, tt * 128:(tt + 1) * 128],
                                                         identity=C.ident[:]),
                      r=[("xT", c, tt), "ident"], w=["ps%d" % pb])
                if c % 2:
                    sc.op("act", lambda: nc.scalar.copy(out=C.io[:, b, c * 128:(c + 1) * 128], in_=C.ps[pb][:, 0:128]),
                          r=["ps%d" % pb], w=[("io", b, c)])
                else:
                    sc.op("dve", lambda: nc.vector.tensor_copy(out=C.io[:, b, c * 128:(c + 1) * 128], in_=C.ps[pb][:, 0:128]),
                          r=["ps%d" % pb], w=[("io", b, c)])
            sc.dma("sp", C.out[s, tt * 128:(tt + 1) * 128, :], C.io[:, b, :], r=[("io", b, cc) for cc in range(NCH)])


NEG = -30000.0
SCALE = 128.0 ** -0.5
A_GROUPS = ((128, 1), (512, 4), (2048, 16))


def dil_views(dst, src, dil, q):
    if dil == 1:
        return dst[:, q * 512:(q + 1) * 512], src
    n = 512 // dil
    dv = dst.rearrange("p (r m) -> p m r", r=dil)[:, n * q:n * (q + 1), :]
    sv = src.rearrange("p (m r) -> p m r", r=dil)
    return dv, sv


def rope_store(C, pb, views, q, wkey, qraw, rt):
    rope_part_a(C, pb, views, q, wkey, qraw, rt)()


def rope_part_a(C, pb, views, q, wkey, qraw, rt, rbanks=(2, 3)):
    nc, sc = C.nc, C.sc
    slot = C.rope_i % 2
    C.rope_i += 1
    rb = rbanks[slot % len(rbanks)]
    psk = "ps%d" % pb
    dv, sv = views(slice(0, 128), C.ps[pb][:, :])
    lokey = (wkey[0] + "lo",) + tuple(wkey[1:])
    sc.op("act", lambda: nc.scalar.copy(out=qraw[:, slot, :], in_=C.ps[pb][0:32, :]), r=[psk], w=[("qraw", slot)])
    sc.op("act", lambda: nc.scalar.copy(out=dv, in_=sv), r=[psk], w=[wkey, lokey])

    def part_b():
        sc.op("pe", lambda: nc.tensor.matmul(C.ps[rb][0:32, :], lhsT=C.protb[:, :], rhs=qraw[:, slot, :], start=True, stop=True),
              r=[("qraw", slot), "protb"], w=["ps%d" % rb])
        sc.op("dve", lambda: nc.vector.tensor_tensor(out=rt[:, slot, 0, :], in0=C.ps[pb][0:32, :], in1=C.ropeb[:, 0, q * 512:(q + 1) * 512], op=ALU.mult),
              r=[psk, "ropeb"], w=[("rt", slot, 0)])
        sc.op("dve", lambda: nc.vector.tensor_tensor(out=rt[:, slot, 1, :], in0=C.ps[rb][0:32, :], in1=C.ropeb[:, 1, q * 512:(q + 1) * 512], op=ALU.mult),
              r=["ps%d" % rb, "ropeb"], w=[("rt", slot, 1)])
        dv0, sv0 = views(slice(0, 32), rt[:, slot, 0, :])
        _, sv1 = views(slice(0, 32), rt[:, slot, 1, :])
        sc.op("dve", lambda: nc.vector.tensor_tensor(out=dv0, in0=sv0, in1=sv1, op=ALU.add),
              r=[("rt", slot, 0), ("rt", slot, 1)], w=[lokey])
    return part_b


def run_interleaved(tasks):
    active = list(tasks)
    while active:
        for t in list(active):
            try:
                next(t)
            except StopIteration:
                active.remove(t)


def layer_a(C, j):
    nc, sc, st = C.nc, C.sc, C.st2
    u_ = C.uid
    sb = lambda n, s, d=F32: st.enter_context(nc.sbuf_tensor(n + u_, list(s), d))
    qT = [sb("qT%d" % i, [128, S], BF16) for i in range(2)]
    kT = [sb("kT%d" % i, [128, S], BF16) for i in range(2)]
    V = [sb("V%d" % i, [128, 16, 128], BF16) for i in range(2)]
    szT = [sb("szT%d" % i, [128, S], BF16) for i in range(2)]
    num = sb("num", [128, S])
    den = sb("den", [128, S])
    yT = sb("yT", [128, 4, S], BF16)
    qraw = sb("qraw", [32, 2, 512], BF16)
    rt = sb("rt", [32, 2, 2, 512], BF16)
    PT = sb("PT", [128, 4, 512], BF16)
    win = C.W["a_w_in"][j]
    wout = C.W["a_w_out"][j]
    DIAG, PREV = 0, 1

    PJ = (0, 1, 3)
    pj = [0]
    preW = {}

    def wcols(s_):
        hd, g = divmod(s_, 3)
        cols = [9216 + hd * 128] if g == 0 else []
        return cols + [((g * 3 + which) * 8 + hd) * 128 for which in range(3)]

    def getw(s_, i):
        if (s_, i) not in preW:
            col = wcols(s_)[i]
            preW[(s_, i)] = load_w(C, win[:, col:col + 128], NCH, 128)
        return preW.pop((s_, i)) if False else preW[(s_, i)]

    def nextbank():
        pj[0] += 1
        return PJ[pj[0] % 3]

    def P(s):
        hd, g = divmod(s, 3)
        b = s % 2
        wlen, dil = A_GROUPS[g]
        L = S // dil
        nbs = L // 128
        wi = 0
        if g == 0:
            wz, kz = getw(s, wi)
            wi += 1
            for q in range(NQ):
                pb = nextbank()
                proj_T(C, wz, kz, 0, 128, q, pb)
                sc.op("act", lambda: nc.scalar.activation(out=szT[hd % 2][:, q * 512:(q + 1) * 512], in_=C.ps[pb][:], func=AF.Silu),
                      r=["ps%d" % pb], w=[("szT", hd % 2, q)])
                yield
        pending = None
        for which, dst, nm in ((0, qT[b], "qT"), (1, kT[b], "kT")):
            wv, wk = getw(s, wi)
            wi += 1
            for q in range(NQ):
                pb = nextbank()
                proj_T(C, wv, wk, 0, 128, q, pb)
                nxt = rope_part_a(C, pb, (lambda rows, src, dst=dst, q=q: dil_views(dst[rows, :], src, 1, q)), q, (nm, b, q), qraw, rt,
                                  rbanks=(2,))
                if pending is not None:
                    pending()
                pending = nxt
                yield
        wv, wk = getw(s, wi)
        for b4 in range(4):
            pb = nextbank()
            for bi in range(4):
                bb = b4 * 4 + bi
                r_, jb = bb // nbs, bb % nbs
                t0 = r_ + dil * jb * 128
                t1 = t0 + dil * 127
                hks = [hk(k, qq) for k in range(NCH) for qq in range(t0 // 512, t1 // 512 + 1)]
                for k in range(NCH):
                    sc.op("pe", lambda: nc.tensor.matmul(C.ps[pb][:, bi * 128:(bi + 1) * 128],
                                                          lhsT=C.hT[:, k, t0:t1 + 1:dil], rhs=wv[:, k, :],
                                                          start=(k == 0), stop=(k == NCH - 1)),
                          r=[wk] + hks, w=["ps%d" % pb])
                if bi == 1 and pending is not None:
                    pending()
                    pending = None
                if bi % 2 == 1:
                    yield
            sc.op("act", lambda: nc.scalar.copy(out=V[b][:, b4 * 4:(b4 + 1) * 4, :].rearrange("p a d -> p (a d)"), in_=C.ps[pb][:]),
                  r=["ps%d" % pb], w=[("V", b, b4)])
            yield

    def T(s):
        hd, g = divmod(s, 3)
        b = s % 2
        wlen, dil = A_GROUPS[g]
        L = S // dil
        nbs = L // 128
        qk_keys = [(nm, b, q) for nm in ("qT", "qTlo", "kT", "kTlo") for q in range(NQ)]

        def cols(blk):
            r_, jb = blk // nbs, blk % nbs
            t0 = r_ + dil * 128 * jb
            return slice(t0, t0 + dil * 127 + 1, dil)

        if g == 0:
            units = [[4 * c + i for i in range(4)] for c in range(4)]
        elif g == 1:
            units = [[r_ * nbs + jb for r_ in range(4)] for jb in range(nbs)]
        else:
            units = [[r0 + i for i in range(4)] for r0 in range(0, 16, 4)]

        def s_phase(qbs):
            pairs = []
            for qb in qbs:
                if qb % nbs > 0:
                    pairs.append((qb - 1, qb, PREV))
                pairs.append((qb, qb, DIAG))
            chunks = []
            for c0 in range(0, len(pairs), 4):
                sub = pairs[c0:c0 + 4]
                sbank = 4 + C.sb_i % 2
                C.sb_i += 1
                sk = "ps%d" % sbank
                for p, (kb, qb, typ) in enumerate(sub):
                    sc.op("pe", lambda: nc.tensor.matmul(C.ps[sbank][:, p * 128:(p + 1) * 128], lhsT=kT[b][:, cols(kb)],
                                                          rhs=qT[b][:, cols(qb)], start=True, stop=False),
                          r=qk_keys, w=[sk])
                    sc.op("pe", lambda: nc.tensor.matmul(C.ps[sbank][:, p * 128:(p + 1) * 128], lhsT=C.identb[:],
                                                          rhs=C.maskb[:, typ, :], start=False, stop=True),
                          r=["identb", "maskb"], w=[sk])
                npair = len(sub)
                pti = C.pt_i % 4
                C.pt_i += 1
                sc.op("act", lambda: nc.scalar.activation(out=PT[:, pti, 0:npair * 128], in_=C.ps[sbank][:, 0:npair * 128],
                                                          func=AF.Exp, scale=SCALE), r=[sk], w=[("PT", pti)])
                chunks.append((sub, pti))
            return qbs, chunks

        def pv_phase(state):
            qbs, chunks = state
            for sub, pti in chunks:
                for p, (kb, qb, typ) in enumerate(sub):
                    ci = qbs.index(qb)
                    first = (typ == PREV) or (qb % nbs == 0)
                    sc.op("pe", lambda: nc.tensor.matmul(C.ps[6][:, ci * 128:(ci + 1) * 128], lhsT=V[b][:, kb, :],
                                                          rhs=PT[:, pti, p * 128:(p + 1) * 128], start=first, stop=(typ == DIAG)),
                          r=[("V", b, kb // 4), ("PT", pti)], w=["ps6"])
                    sc.op("pe", lambda: nc.tensor.matmul(C.ps[7][:, ci * 128:(ci + 1) * 128], lhsT=C.ones1b[:],
                                                          rhs=PT[:, pti, p * 128:(p + 1) * 128], start=first, stop=(typ == DIAG)),
                          r=["ones1b", ("PT", pti)], w=["ps7"])
            qb0 = qbs[0]
            if g == 0:
                c = qb0 // 4
                views = lambda t, pv: (t[:, c * 512:(c + 1) * 512], pv)
                chunks_k = [c]
            elif g == 1:
                jb = qb0 % nbs
                views = lambda t, pv: (t[:, jb * 512:(jb + 1) * 512].rearrange("p (m r) -> p m r", r=4),
                                       pv.rearrange("p (r m) -> p m r", r=4))
                chunks_k = [jb]
            else:
                views = lambda t, pv: (t.rearrange("p (m r) -> p m r", r=16)[:, :, qb0:qb0 + 4],
                                       pv.rearrange("p (r m) -> p m r", r=4))
                chunks_k = [0, 1, 2, 3]
            for acc, bank, nm in ((num, 6, "num"), (den, 7, "den")):
                av, pv = views(acc[:, :], C.ps[bank][:, :])
                keys = [(nm, c) for c in chunks_k]
                if g == 0:
                    if nm == "num":
                        sc.op("dve", lambda: nc.vector.tensor_copy(out=av, in_=pv), r=["ps%d" % bank], w=keys)
                    else:
                        sc.op("act", lambda: nc.scalar.copy(out=av, in_=pv), r=["ps%d" % bank], w=keys)
                else:
                    sc.op("dve", lambda: nc.vector.tensor_tensor(out=av, in0=pv, in1=av, op=ALU.add),
                          r=["ps%d" % bank] + keys, w=keys)

        pending = None
        for qbs in units + [None]:
            if qbs is not None:
                stt = s_phase(qbs)
                yield
            else:
                stt = None
            if pending is not None:
                pv_phase(pending)
                yield
            pending = stt

    def F(hd):
        for q in range(NQ):
            sl = slice(q * 512, (q + 1) * 512)
            sc.op("act", lambda: nc.scalar.activation(out=den[:, sl], in_=den[:, sl], func=AF.Ln), r=[("den", q)], w=[("den", q)])
            sc.op("act", lambda: nc.scalar.activation(out=den[:, sl], in_=den[:, sl], func=AF.Exp, scale=-1.0), r=[("den", q)], w=[("den", q)])
            sc.op("pool", lambda: nc.gpsimd.tensor_tensor(out=den[:, sl], in0=den[:, sl], in1=szT[hd % 2][:, sl], op=ALU.mult),
                  r=[("den", q), ("szT", hd % 2, q)], w=[("den", q)])
            sc.op("dve", lambda: nc.vector.tensor_tensor(out=yT[:, hd % 4, sl], in0=num[:, sl], in1=den[:, sl], op=ALU.mult),
                  r=[("num", q), ("den", q)], w=[("yT", hd % 4, q)])

    preO = {}

    def getO(half, og):
        if (half, og) not in preO:
            preO[(half, og)] = load_w(C, wout[half * 512:(half + 1) * 512, og * 256:(og + 1) * 256], 4, 256)
        return preO[(half, og)]

    def O(half):
        for og in range(4):
            wv, wk = getO(half, og)
            for o2 in range(2):
                oc = og * 2 + o2
                for q in range(NQ):
                    pb = (oc * NQ + q) % 2
                    for k in range(4):
                        sc.op("pe", lambda: nc.tensor.matmul(C.ps[pb][:], lhsT=wv[:, k, o2 * 128:(o2 + 1) * 128],
                                                              rhs=yT[:, k, q * 512:(q + 1) * 512], start=(k == 0), stop=(k == 3)),
                              r=[wk, ("yT", k, q)], w=["ps%d" % pb])
                    residual_add(C, oc, q, pb)

    NS = 24
    run_interleaved([P(0)])
    for s in range(NS):
        tasks = [T(s)]
        if s + 1 < NS:
            tasks.append(P(s + 1))
        run_interleaved(tasks)
        hd, g = divmod(s, 3)
        if s + 2 < NS:
            getw(s + 2, 0)
        if g == 2:
            if hd % 4 == 3:
                getO(hd // 4, 0)
            F(hd)
            if hd % 4 == 3:
                O(hd // 4)


def load_w_pre(C, src3, K, W):
    nc, sc = C.nc, C.sc
    assert K * W <= WSLOT
    si = C.wst_i % len(C.wst)
    C.wst_i += 1
    bi = C.wbf_i % len(C.wbf)
    C.wbf_i += 1
    stv = C.wst[si][:, 0:K * W].rearrange("p (k w) -> p k w", w=W)
    bfv = C.wbf[bi][:, 0:K * W].rearrange("p (k w) -> p k w", w=W)
    sc.dma("sp", stv, src3, w=[("wst", si)])
    sc.op("pool", lambda: nc.gpsimd.tensor_copy(out=C.wbf[bi][:, 0:K * W], in_=C.wst[si][:, 0:K * W]),
          r=[("wst", si)], w=[("wbf", bi)])
    return bfv, ("wbf", bi)


def layer_b(C, j):
    nc, sc, st = C.nc, C.sc, C.st2
    u = C.uid
    sb = lambda n, s, d=F32: st.enter_context(nc.sbuf_tensor(n + u, list(s), d))
    W = C.W
    win = W["b_w_in"][j]
    TINY = 1e-30
    DIAG, W4 = 0, 2
    allx = [("xT", c, tt) for c in range(NCH) for tt in range(NT)]
    sc.barrier()
    xb = C.xT.bitcast(BF16)[:].rearrange("p c t -> p (c t)")
    reg = lambda off, n: xb[:, off:off + n]
    qT4 = reg(0, 8192)
    big = reg(8192, 8192).rearrange("p (a t) -> p a t", a=4)
    ksT = reg(16384, 2048)
    kwT = reg(18432, 2048)
    Vs = reg(20480, 2080).rearrange("p (a d) -> p a d", d=130)
    Vw = reg(22560, 2080).rearrange("p (a d) -> p a d", d=130)
    PT = reg(24640, 2048).rearrange("p (a d) -> p a d", d=512)
    cmaskb = reg(26688, 2048).rearrange("p (a d) -> p a d", d=128)
    eexpb = reg(28736, 2048)
    yT8 = sb("yT8", [128, 8, S], BF16)
    g_tok = sb("gtok", [128, NT, 24])
    gb = sb("gb", [128, 24])
    kccT = sb("kccT", [128, 2, 128], BF16)
    vcc = sb("vcc", [128, 2, 162], BF16)
    selk = sb("selk", [128, NT, 32])
    sela = sb("sela", [128, NT, 32])
    o_toks = [sb("otok%d" % t, [128, 4, 128]) for t in range(4)]
    sms = [sb("sm%d" % t, [128, 160]) for t in range(4)]
    selbT4s = [sb("selbT4%d" % t, [32, 512], BF16) for t in range(4)]
    maskb4 = sb("maskb4", [128, 3, 512], BF16)
    qraw = sb("qraw", [32, 2, 512], BF16)
    rt = sb("rt", [32, 2, 2, 512])
    stg = sb("stg", [128, 2048])
    peT = sb("peT", [128, 2, 32], BF16)
    hcb = sb("hcb", [128, 128], BF16)
    biasc = sb("biasc", [128, 2])

    sc.dma("sp", stg[:, :].rearrange("p (a b) -> p a b", a=16), C.c_cmask.rearrange("a p b -> p a b"), w=["stg"])
    sc.op("dve", lambda: nc.vector.tensor_copy(out=cmaskb.rearrange("p a d -> p (a d)"), in_=stg[:, :]), r=["stg"], w=["cmaskb"])
    sc.dma("sp", stg[0:32, :], C.c_eexp, r=["cmaskb"], w=["stg"])
    sc.op("dve", lambda: nc.vector.tensor_copy(out=eexpb[0:32, :], in_=stg[0:32, :]), r=["stg"], w=["eexpb"])
    sc.dma("sp", stg[:, 0:32], C.c_cover, r=["eexpb"], w=["stg"])
    for g in range(2):
        sc.op("dve", lambda: nc.vector.tensor_copy(out=vcc[:, g, 129:161], in_=stg[:, 0:32]), r=["stg"], w=[("vcc", g, "cov")])
        sc.op("dve", lambda: nc.vector.memset(vcc[:, g, 128:129], 1.0), w=[("vcc", g, "one")])
    sc.dma("sp", selk[:], C.c_selk.rearrange("a p n -> p a n"), w=["selk"])
    sc.dma("sp", sela[:], C.c_sela.rearrange("a p n -> p a n"), w=["sela"])
    sc.dma("sp", gb[:], W["b_gate_b"][j].partition_broadcast(128), w=["gb"])
    for kv, nm in enumerate(["b_pe_k", "b_pe_v"]):
        sc.dma("sp", stg[0:32, 128 * (1 + kv):128 * (2 + kv)], W[nm][j], r=[("vcc", 0, "cov"), ("vcc", 1, "cov")], w=[("stgpe", kv)])
        sc.op("pe", lambda: nc.tensor.transpose(out=C.ps[0][:, 0:32], in_=stg[0:32, 128 * (1 + kv):128 * (2 + kv)], identity=C.ident[0:32, 0:32]),
              r=[("stgpe", kv), "ident"], w=["ps0"])
        sc.op("dve", lambda: nc.vector.tensor_copy(out=peT[:, kv, :], in_=C.ps[0][:, 0:32]), r=["ps0"], w=[("peT", kv)])
    for vt in (Vs, Vw):
        sc.op("pool", lambda: nc.gpsimd.memset(vt[:, :, 128:129], 1.0), w=["Vones"])

    nat = lambda dst: (lambda rows, src, dst=dst: None)

    def proj_rope(col, dst2d, nm):
        wv, wk = load_w(C, win[:, col:col + 128], NCH, 128)
        for q in range(NQ):
            pb = q % 2
            proj_T(C, wv, wk, 0, 128, q, pb)
            rope_store(C, pb, (lambda rows, src, q=q: (dst2d[rows, q * 512:(q + 1) * 512], src)), q, (nm, q), qraw, rt)

    def proj_plain(col, dst2d, nm, func=None):
        wv, wk = load_w(C, win[:, col:col + 128], NCH, 128)
        for q in range(NQ):
            pb = q % 2
            proj_T(C, wv, wk, 0, 128, q, pb)
            if func is None:
                sc.op("act", lambda: nc.scalar.copy(out=dst2d[:, q * 512:(q + 1) * 512], in_=C.ps[pb][:]), r=["ps%d" % pb], w=[(nm, q)])
            else:
                sc.op("act", lambda: nc.scalar.activation(out=dst2d[:, q * 512:(q + 1) * 512], in_=C.ps[pb][:], func=func),
                      r=["ps%d" % pb], w=[(nm, q)])

    kvcol = lambda i, g: 1024 + (i * 2 + g) * 128
    for g in range(2):
        proj_rope(kvcol(0, g), big[:, g * 2 + 0, :], "cmp%d" % (g * 2))
        proj_plain(kvcol(1, g), big[:, g * 2 + 1, :], "cmp%d" % (g * 2 + 1))
    def compress_task():
        cst = stg[:, :].rearrange("p (a b) -> p a b", a=2)
        cbf = reg(24640, 2048).rearrange("p (a b) -> p a b", a=2)
        w2bf = reg(30784, 128)
        li = [0]

        def cload(src3, K):
            sl = li[0] % 2
            li[0] += 1
            sc.dma("sp", cst[:, sl, 0:K * 128].rearrange("p (k w) -> p k w", w=128), src3,
                   r=[("peT", 0), ("peT", 1), ("vcc", 0, "cov"), ("vcc", 1, "cov"), "eexpb", "cmaskb"], w=[("cst", sl)])
            sc.op("pool", lambda: nc.gpsimd.tensor_copy(out=cbf[:, sl, 0:K * 128], in_=cst[:, sl, 0:K * 128]),
                  r=[("cst", sl)], w=[("cbf", sl)])
            return cbf[:, sl, :].rearrange("p (k w) -> p k w", w=128), ("cbf", sl)

        for kv in range(2):
            w1 = W["b_w1_k" if kv == 0 else "b_w1_v"][j]
            w2 = W["b_w2_k" if kv == 0 else "b_w2_v"][j]
            w2v_, w2k_ = cload(w2.rearrange("(k p) w -> p k w", p=128), 1)
            sc.op("pool", lambda: nc.gpsimd.tensor_copy(out=w2bf[:, :], in_=w2v_[:, 0, :]), r=[w2k_], w=["w2bf"])
            srcs = [big[:, g * 2 + kv, :] for g in range(2)]
            skeys = [[("cmp%d" % (g * 2 + kv), q) for q in range(NQ)] +
                     ([("cmp%dlo" % (g * 2 + kv), q) for q in range(NQ)] if kv == 0 else []) for g in range(2)]
            for h in range(4):
                wv, wk = cload(w1[h * 8:(h + 1) * 8].rearrange("p d e -> d p e"), 8)
                for p8 in range(8):
                    p = h * 8 + p8
                    sc.op("pe", lambda: nc.tensor.matmul(C.ps[4][:, 0:1], lhsT=wv[:, p8, :], rhs=peT[:, kv, p:p + 1], start=(p == 0), stop=(p == 31)),
                          r=[wk, ("peT", kv)], w=["ps4"])
                yield
                for g in range(2):
                    bank = 5 if g == 0 else 7
                    for p8 in range(8):
                        p = h * 8 + p8
                        sc.op("pe", lambda: nc.tensor.matmul(C.ps[bank][:, 0:127], lhsT=wv[:, p8, :], rhs=srcs[g][:, p:p + 16 * 126 + 1:16],
                                                              start=(p == 0), stop=(p == 31)), r=[wk] + skeys[g], w=["ps%d" % bank])
                    yield
            sc.op("dve", lambda: nc.vector.tensor_copy(out=biasc[:, kv:kv + 1], in_=C.ps[4][:, 0:1]), r=["ps4"], w=[("biasc", kv)])
            for g in range(2):
                bank = 5 if g == 0 else 7
                sc.op("act", lambda: nc.scalar.activation(out=hcb[:, 0:127], in_=C.ps[bank][:, 0:127], func=AF.Silu, bias=biasc[:, kv:kv + 1]),
                      r=["ps%d" % bank, ("biasc", kv)], w=["hcb"])
                if kv == 0:
                    sc.op("pe", lambda: nc.tensor.matmul(C.ps[6][:, 0:127], lhsT=w2bf[:, :], rhs=hcb[:, 0:127], start=True, stop=True),
                          r=["w2bf", "hcb"], w=["ps6"])
                    sc.op("dve", lambda: nc.vector.tensor_copy(out=kccT[:, g, 0:127], in_=C.ps[6][:, 0:127]), r=["ps6"], w=[("kccT", g)])
                else:
                    sc.op("pe", lambda: nc.tensor.matmul(C.ps[6][0:127, 0:128], lhsT=hcb[:, 0:127], rhs=w2bf[:, :], start=True, stop=True),
                          r=["w2bf", "hcb"], w=["ps6"])
                    sc.op("dve", lambda: nc.vector.tensor_copy(out=vcc[0:127, g, 0:128], in_=C.ps[6][0:127, 0:128]), r=["ps6"], w=[("vcc", g, "v")])
                yield

    def gates_task():
        wg, wgk = load_w(C, win[:, 3584:3608], NCH, 24)
        for tt in range(NT):
            pb = tt % 2
            for k in range(NCH):
                sc.op("pe", lambda: nc.tensor.matmul(C.ps[pb][:, 0:24], lhsT=C.hT[:, k, tt * 128:(tt + 1) * 128], rhs=wg[:, k, :],
                                                      start=(k == 0), stop=(k == NCH - 1)), r=[wgk, hk(k, tt // 4)], w=["ps%d" % pb])
            sc.op("dve", lambda: nc.vector.tensor_tensor(out=g_tok[:, tt, :], in0=C.ps[pb][:, 0:24], in1=gb[:], op=ALU.add),
                  r=["ps%d" % pb, "gb"], w=[("gtok", tt)])
            sc.op("act", lambda: nc.scalar.activation(out=g_tok[:, tt, :], in_=g_tok[:, tt, :], func=AF.Sigmoid), r=[("gtok", tt)], w=[("gtok", tt)])
            if tt % 2 == 1:
                yield

    sbi = [0]
    pti = [0]
    for typ in (DIAG, W4):
        for r in range(4):
            sc.op("dve", lambda: nc.vector.tensor_copy(out=maskb4[:, typ, r * 128:(r + 1) * 128], in_=C.maskb[:, typ, :]),
                  r=["maskb"], w=["maskb4"])

    def s_block(lhsT, lkeys, qblk, qk, np_, masks):
        sbank = sbi[0] % 2
        sbi[0] += 1
        sk = "ps%d" % sbank
        sc.op("pe", lambda: nc.tensor.matmul(C.ps[sbank][0:np_, :], lhsT=lhsT, rhs=qblk, start=True, stop=(len(masks) == 0)),
              r=lkeys + qk, w=[sk])
        for mi, (ml, mr, mk) in enumerate(masks):
            sc.op("pe", lambda: nc.tensor.matmul(C.ps[sbank][0:np_, :], lhsT=ml, rhs=mr, start=False, stop=(mi == len(masks) - 1)),
                  r=mk, w=[sk])
        pi = pti[0] % 4
        pti[0] += 1
        sc.op("act", lambda: nc.scalar.activation(out=PT[0:np_, pi, :], in_=C.ps[sbank][0:np_, :], func=AF.Exp, scale=SCALE),
              r=[sk], w=[("PT", pi)])
        return PT[:, pi, :], ("PT", pi)

    def qblock_task(g, i, tk, qkeys_x, kskeys, kwkeys):
        sm = sms[tk]
        o_tok = o_toks[tk]
        selbT4 = selbT4s[tk]
        K_ = lambda nm: (nm, tk)
        rdc, rdx, gco = sm[:, 0:4], sm[:, 4:8], sm[:, 8:12]
        imp, scb = sm[:, 16:48], sm[:, 48:80]
        mx8a, mx8b = sm[:, 80:88], sm[:, 88:96]
        selb = sm[:, 96:128]
        qblk = qT4[:, i * 512:(i + 1) * 512]
        qk = [("qT4", r, i // 4) for r in range(4)] + [("qT4lo", r, i // 4) for r in range(4)]
        gcol = lambda br: g_tok[:, i, g * 12 + br:g * 12 + br + 10:3]
        sbank = sbi[0] % 2
        sbi[0] += 1
        sk = "ps%d" % sbank
        for r in range(4):
            sc.op("pe", lambda: nc.tensor.matmul(C.ps[sbank][0:127, r * 128:(r + 1) * 128], lhsT=kccT[:, g, 0:127],
                                                  rhs=qblk[:, r * 128:(r + 1) * 128], start=True, stop=False),
                  r=[("kccT", g)] + qk, w=[sk])
            sc.op("pe", lambda: nc.tensor.matmul(C.ps[sbank][0:127, r * 128:(r + 1) * 128], lhsT=C.identb[0:127, 0:127],
                                                  rhs=cmaskb[0:127, i, :], start=False, stop=True),
                  r=["identb", "cmaskb"], w=[sk])
        pi = pti[0] % 4
        pti[0] += 1
        sc.op("act", lambda: nc.scalar.activation(out=PT[0:127, pi, :], in_=C.ps[sbank][0:127, :], func=AF.Exp, scale=SCALE),
              r=[sk], w=[("PT", pi)])
        pt, ptk = PT[:, pi, :], ("PT", pi)
        yield
        for r in range(4):
            bank = 2 + r // 2
            c0 = (r % 2) * 161
            sc.op("pe", lambda: nc.tensor.matmul(C.ps[bank][:, c0:c0 + 161], lhsT=pt[0:127, r * 128:(r + 1) * 128],
                                                  rhs=vcc[0:127, g, 0:161], start=True, stop=True),
                  r=[ptk, ("vcc", g, "v"), ("vcc", g, "cov"), ("vcc", g, "one")], w=["ps%d" % bank])
        for h2 in range(2):
            sc.op("dve", lambda: nc.vector.tensor_scalar(out=rdc[:, 2 * h2:2 * h2 + 2], in0=C.ps[2 + h2][:, 128:128 + 162:161],
                                                          scalar1=TINY, scalar2=None, op0=ALU.max), r=["ps%d" % (2 + h2)], w=[K_("rdc")])
        sc.op("dve", lambda: nc.vector.reciprocal(out=rdc, in_=rdc), r=[K_("rdc")], w=[K_("rdc")])
        for r in range(4):
            bank = 2 + r // 2
            c0 = (r % 2) * 161 + 129
            if r == 0:
                sc.op("dve", lambda: nc.vector.tensor_scalar(out=imp, in0=C.ps[bank][:, c0:c0 + 32], scalar1=rdc[:, 0:1], scalar2=None, op0=ALU.mult),
                      r=["ps%d" % bank, K_("rdc")], w=[K_("imp")])
            else:
                sc.op("dve", lambda: nc.vector.scalar_tensor_tensor(out=imp, in0=C.ps[bank][:, c0:c0 + 32], scalar=rdc[:, r:r + 1], in1=imp,
                                                                     op0=ALU.mult, op1=ALU.add), r=["ps%d" % bank, K_("rdc"), K_("imp")], w=[K_("imp")])
        sc.op("dve", lambda: nc.vector.tensor_tensor(out=scb, in0=imp, in1=selk[:, i, :], op=ALU.mult), r=[K_("imp"), "selk"], w=[K_("scb")])
        sc.op("dve", lambda: nc.vector.tensor_tensor(out=scb, in0=scb, in1=sela[:, i, :], op=ALU.add), r=[K_("scb"), "sela"], w=[K_("scb")])
        sc.op("dve", lambda: nc.vector.max(out=mx8a, in_=scb), r=[K_("scb")], w=[K_("mx8a")])
        sc.op("dve", lambda: nc.vector.match_replace(out=imp, in_to_replace=mx8a, in_values=scb, imm_value=-2.0), r=[K_("scb"), K_("mx8a")], w=[K_("imp")])
        sc.op("dve", lambda: nc.vector.max(out=mx8b, in_=imp), r=[K_("imp")], w=[K_("mx8b")])
        sc.op("dve", lambda: nc.vector.tensor_scalar(out=selb, in0=scb, scalar1=mx8b[:, 7:8], scalar2=NEG, op0=ALU.is_lt, op1=ALU.mult),
              r=[K_("scb"), K_("mx8b")], w=[K_("selb")])
        sc.op("dve", lambda: nc.vector.tensor_tensor(out=gco, in0=rdc, in1=gcol(0), op=ALU.mult), r=[K_("rdc"), ("gtok", i)], w=[K_("gco")])
        for r in range(4):
            bank = 2 + r // 2
            c0 = (r % 2) * 161
            sc.op("dve", lambda: nc.vector.tensor_scalar(out=o_tok[:, r, :], in0=C.ps[bank][:, c0:c0 + 128], scalar1=gco[:, r:r + 1], scalar2=None,
                                                          op0=ALU.mult), r=["ps%d" % bank, K_("gco")], w=[K_(("otok", r))])
        yield "PHASE"
        sc.op("pe", lambda: nc.tensor.transpose(out=C.ps[2][0:32, 0:128], in_=selb, identity=C.ident[:]), r=[K_("selb"), "ident"], w=["ps2"])
        src = C.ps[2][0:32, 0:128]
        src4 = bass.AP(src.tensor, src.offset, [list(src.ap[0]), [0, 4], list(src.ap[1])])
        sc.op("act", lambda: nc.scalar.copy(out=selbT4[:, :].rearrange("p (a b) -> p a b", a=4), in_=src4), r=["ps2"], w=[K_("selbT4")])
        yield

        def branch(kbs, kT_, kkeys, vt, vnm, bank0, masks_for, br):
            def pv(kb, pt, ptk):
                for r in range(4):
                    bank = bank0 + r // 2
                    c0 = (r % 2) * 129
                    sc.op("pe", lambda: nc.tensor.matmul(C.ps[bank][:, c0:c0 + 129], lhsT=pt[:, r * 128:(r + 1) * 128], rhs=vt[:, kb, 0:129],
                                                          start=(kb == kbs[0] and r % 2 == 0), stop=(kb == kbs[-1]), skip_group_check=True),
                          r=[ptk, (vnm, kb // 4), "Vones"], w=["ps%d" % bank])
            prev = None
            for kb in list(kbs) + [None]:
                cur = None
                if kb is not None:
                    pt, ptk = s_block(kT_[:, kb * 128:(kb + 1) * 128], kkeys, qblk, qk, 128, masks_for(kb))
                    cur = (kb, pt, ptk)
                if prev is not None:
                    pv(*prev)
                prev = cur
                yield
            for h2 in range(2):
                sc.op("dve", lambda: nc.vector.tensor_scalar(out=rdx[:, 2 * h2:2 * h2 + 2], in0=C.ps[bank0 + h2][:, 128:128 + 130:129],
                                                              scalar1=TINY, scalar2=None, op0=ALU.max), r=["ps%d" % (bank0 + h2)], w=[K_("rdx")])
            sc.op("dve", lambda: nc.vector.reciprocal(out=rdx, in_=rdx), r=[K_("rdx")], w=[K_("rdx")])
            sc.op("dve", lambda: nc.vector.tensor_tensor(out=gco, in0=rdx, in1=gcol(br), op=ALU.mult), r=[K_("rdx"), ("gtok", i)], w=[K_("gco")])
            for r in range(4):
                bank = bank0 + r // 2
                c0 = (r % 2) * 129
                sc.op("dve", lambda: nc.vector.scalar_tensor_tensor(out=o_tok[:, r, :], in0=C.ps[bank][:, c0:c0 + 128], scalar=gco[:, r:r + 1],
                                                                     in1=o_tok[:, r, :], op0=ALU.mult, op1=ALU.add),
                      r=["ps%d" % bank, K_("gco"), K_(("otok", r))], w=[K_(("otok", r))])
            yield

        dmask = (C.identb[:, :], maskb4[:, DIAG, :], ["identb", "maskb4"])
        w4mask = (C.identb[:, :], maskb4[:, W4, :], ["identb", "maskb4"])
        yield from branch(list(range(max(0, i - 4), i + 1)), kwT, kwkeys, Vw, "Vw", 6,
                          lambda kb: ([dmask] if kb == i else []) + ([w4mask] if kb == i - 4 else []), 2)
        yield "PHASE"
        yield from branch(list(range(0, i + 1)), ksT, kskeys, Vs, "Vs", 4,
                          lambda kb: [(eexpb[0:32, kb * 128:(kb + 1) * 128], selbT4[:, :], ["eexpb", K_("selbT4")])] + ([dmask] if kb == i else []), 1)
        yield "PHASE"
        for r in range(4):
            sc.op("pe", lambda: nc.tensor.transpose(out=C.ps[3][:, r * 128:(r + 1) * 128], in_=o_tok[:, r, :], identity=C.ident[:]),
                  r=[K_(("otok", r)), "ident"], w=["ps3"])
        sc.op("dve", lambda: nc.vector.tensor_tensor(out=yT8[:, g * 4:g * 4 + 4, i * 128:(i + 1) * 128],
                                                      in0=C.ps[3][:].rearrange("p (a d) -> p a d", a=4),
                                                      in1=big[:, 0:4, i * 128:(i + 1) * 128], op=ALU.mult),
              r=["ps3"] + [("sz%d" % r, i // 4) for r in range(4)], w=[("yT8", g, i)])
        yield

    def phase2_task(g):
        qT4v = qT4.rearrange("p (b r i) -> p b r i", r=4, i=128)
        pending = [None]

        def rope_chunk(wv, wk, q, views, key):
            pb = q % 2
            proj_T(C, wv, wk, 0, 128, q, pb)
            nxt = rope_part_a(C, pb, views, q, key, qraw, rt)
            if pending[0] is not None:
                pending[0]()
            pending[0] = nxt

        for r in range(4):
            wv, wk = load_w(C, win[:, (g * 4 + r) * 128:(g * 4 + r + 1) * 128], NCH, 128)
            for q in range(NQ):
                rope_chunk(wv, wk, q, (lambda rows, src, q=q, r=r: (qT4v[rows, 4 * q:4 * q + 4, r, :], src.rearrange("p (b i) -> p b i", b=4))),
                           ("qT4", r, q))
                yield
        for col, dst2d, nm in ((kvcol(2, g), ksT, "ksT"), (kvcol(4, g), kwT, "kwT")):
            wv, wk = load_w(C, win[:, col:col + 128], NCH, 128)
            for q in range(NQ):
                rope_chunk(wv, wk, q, (lambda rows, src, q=q, dst2d=dst2d: (dst2d[rows, q * 512:(q + 1) * 512], src)), (nm, q))
                yield
        pending[0]()
        pending[0] = None
        for slot_i, vt, nm in ((3, Vs, "Vs"), (5, Vw, "Vw")):
            wv, wk = load_w(C, win[:, kvcol(slot_i, g):kvcol(slot_i, g) + 128], NCH, 128)
            for b4 in range(4):
                pb = b4 % 2
                for bi in range(4):
                    b = b4 * 4 + bi
                    for k in range(NCH):
                        sc.op("pe", lambda: nc.tensor.matmul(C.ps[pb][:, bi * 128:(bi + 1) * 128], lhsT=C.hT[:, k, b * 128:(b + 1) * 128],
                                                              rhs=wv[:, k, :], start=(k == 0), stop=(k == NCH - 1)),
                              r=[wk, hk(k, b // 4)], w=["ps%d" % pb])
                sc.op("act", lambda: nc.scalar.copy(out=vt[:, b4 * 4:(b4 + 1) * 4, 0:128], in_=C.ps[pb][:].rearrange("p (a d) -> p a d", a=4)),
                      r=["ps%d" % pb, "Vones"], w=[(nm, b4)])
                yield

    def chain(*gens):
        for g_ in gens:
            yield from g_

    for g in range(2):
        if g == 0:
            run_interleaved([compress_task(), chain(gates_task(), phase2_task(0))])
            sc.barrier()
        else:
            run_interleaved([phase2_task(1)])
        for r in range(4):
            proj_plain(2560 + (g * 4 + r) * 128, big[:, r, :], "sz%d" % r, func=AF.Silu)
        kskeys = [("ksT", q) for q in range(NQ)] + [("ksTlo", q) for q in range(NQ)]
        kwkeys = [("kwT", q) for q in range(NQ)] + [("kwTlo", q) for q in range(NQ)]

        gens = [qblock_task(g, i, i % 4, None, kskeys, kwkeys) for i in range(NT)]
        for step in range(NT + 3):
            act = [gens[step - k] for k in range(4) if 0 <= step - k < NT]
            while act:
                for t in list(act):
                    try:
                        if next(t) == "PHASE":
                            act.remove(t)
                    except StopIteration:
                        act.remove(t)
        sc.barrier()

    for c in range(NCH):
        sc.dma("sp", C.xT[:, c, :], C.xspill[:, c * S:(c + 1) * S], w=[("xT", c, tt) for tt in range(NT)])
    wout = W["b_w_out"][j]
    for oc in range(8):
        wv, wk = load_w(C, wout[:, oc * 128:(oc + 1) * 128], 8, 128)
        for o2 in range(1):
            for q in range(NQ):
                pb = (oc * NQ + q) % 2
                for k in range(8):
                    sc.op("pe", lambda: nc.tensor.matmul(C.ps[pb][:], lhsT=wv[:, k, o2 * 128:(o2 + 1) * 128],
                                                          rhs=yT8[:, k, q * 512:(q + 1) * 512], start=(k == 0), stop=(k == 7)),
                          r=[wk], w=["ps%d" % pb])
                residual_add(C, oc, q, pb)


_NC_CACHE = {}


def make_consts():
    half = 16
    inv = 500000.0 ** (-2.0 * np.arange(half, dtype=np.float64) / 32.0)
    ang = np.arange(S, dtype=np.float64)[None, :] * inv[:, None]
    ang = (np.arange(S, dtype=np.float32)[None, :] * inv.astype(np.float32)[:, None]).astype(np.float64)
    cos = np.concatenate([np.cos(ang), np.cos(ang)], 0)
    sin = np.concatenate([-np.sin(ang), np.sin(ang)], 0)
    rope = np.stack([cos, sin]).astype(np.float32)
    k = np.arange(128)[:, None]
    q = np.arange(128)[None, :]
    mask = np.zeros((4, 128, 128), np.float32)
    mask[0] = np.where(q >= k, 0.0, NEG)
    mask[1] = np.where(q <= k, 0.0, NEG)
    mask[2] = np.where(k > q, 0.0, NEG)
    mask[3] = NEG
    prot = np.zeros((32, 32), np.float32)
    for m in range(32):
        prot[(m + 16) % 32, m] = 1.0
    jj = np.arange(128)[:, None]
    ql = np.arange(128)[None, :]
    cmask = np.zeros((16, 128, 128), np.float32)
    for i in range(16):
        cmask[i] = np.where((16 * jj + 31 <= 128 * i + ql) & (jj < 127), 0.0, NEG)
    eexp = (np.arange(S)[None, :] // 64 == np.arange(32)[:, None]).astype(np.float32)
    n = np.arange(32)[None, :]
    cover = ((jj * 16 < (n + 1) * 64) & (jj * 16 + 32 > n * 64) & (jj < 127)).astype(np.float32)
    selk = np.zeros((16, 128, 32), np.float32)
    sela = np.zeros((16, 128, 32), np.float32)
    for i in range(16):
        cur = (128 * i + np.arange(128)[:, None]) // 64
        forced = (n == 0) | (n == cur) | (n == cur - 1)
        valid = n <= cur
        selk[i] = np.where(valid & ~forced, 1.0, 0.0)
        sela[i] = np.where(valid, np.where(forced, 1000.0, 0.0), -1.0)
    return {"c_rope": rope, "c_mask": mask, "c_prot": prot, "c_cmask": cmask, "c_eexp": eexp,
            "c_cover": cover, "c_selk": selk, "c_sela": sela}


def kernel(**inputs):
    layers = inputs.pop("_layers", (0, 1, 2, 3))
    x = np.ascontiguousarray(inputs["x"], dtype=np.float32)
    key = tuple(layers)
    if key not in _NC_CACHE:
        _NC_CACHE[key] = build(layers)
    nc = _NC_CACHE[key]
    ident = np.eye(128, dtype=np.float32)
    consts = make_consts()
    in_maps = []
    for i in range(N_CORES):
        m = {"x": x[i * SEQ_PER_CORE:(i + 1) * SEQ_PER_CORE],
             "norm_g": np.ascontiguousarray(inputs["norm_g"], dtype=np.float32),
             "final_g": np.ascontiguousarray(inputs["final_g"], dtype=np.float32),
             "ident": ident}
        m.update(consts)
        for name in WSHAPES:
            m[name] = np.ascontiguousarray(inputs[name], dtype=np.float32)
        in_maps.append(m)
    res = run_bass_kernel_spmd(nc, in_maps, core_ids=list(range(N_CORES)))
    return np.concatenate([r["out"] for r in res.results], axis=0)
```
